# Optimizing a Trainium2 kernel written in Bass

```python
import jax, jax.numpy as jnp
from jax import lax
import numpy as np

D_MODEL = 4096
BATCH = 4
SEQ = 4096
DEPTH = 1
DEC_BATCH = 16
DEC_SEQ = 64
PAST_LEN = 2048

CHUNK = 64
MIX_WIDTH = D_MODEL
WIDTH_A = MIX_WIDTH // 2
N_HEADS_A = 8
DV_A = WIDTH_A // N_HEADS_A
DK_A = DV_A // 2
WIDTH_B = MIX_WIDTH - WIDTH_A
N_BLOCKS_B = 16
BLOCK_B = WIDTH_B // N_BLOCKS_B
CONV_W = 4
LRU_C = 8.0
D_FF = -(-8 * D_MODEL // (3 * 256)) * 256
EPS = 1e-6

Q_COLS = N_HEADS_A * DK_A
IN_SPLITS = (Q_COLS, 2 * Q_COLS, 2 * Q_COLS + WIDTH_A, 2 * Q_COLS + 2 * WIDTH_A,
             2 * Q_COLS + 2 * WIDTH_A + N_HEADS_A, 2 * Q_COLS + 2 * WIDTH_A + 2 * N_HEADS_A,
             2 * Q_COLS + 2 * WIDTH_A + 2 * N_HEADS_A + WIDTH_B)
IN_COLS = 2 * Q_COLS + 2 * WIDTH_A + 2 * N_HEADS_A + 2 * WIDTH_B

kernel_name = 'hymba_mlstm_rglru_adaln_stream_step'


def _rmsnorm(x, g):
    xf = x.astype(jnp.float32)
    y = xf * lax.rsqrt(jnp.mean(xf * xf, axis=-1, keepdims=True) + EPS)
    return (y * g.astype(jnp.float32)).astype(x.dtype)


def _block_diag(x, w, b):
    B, T, _ = x.shape
    xr = x.reshape(B, T, N_BLOCKS_B, BLOCK_B)
    return jnp.einsum('btni,nij->btnj', xr, w).reshape(B, T, WIDTH_B) + b


def _causal_conv(x, buf, w, b):
    T = x.shape[1]
    xp = jnp.concatenate([buf.astype(x.dtype), x], axis=1)
    y = xp[:, 0:T] * w[0]
    for j in range(1, CONV_W):
        y = y + xp[:, j:j + T] * w[j]
    return y + b, xp[:, T:]


def _lin_combine(left, right):
    a1, b1 = left
    a2, b2 = right
    return a1 * a2, a2 * b1 + b2


def _rg_lru(x, r, i, lam, h0):
    log_a = LRU_C * r * jax.nn.log_sigmoid(lam)
    a = jnp.exp(log_a)
    u = jnp.sqrt(-jnp.expm1(2.0 * log_a)) * (i * x)
    u = u.at[:, 0].add(a[:, 0] * h0)
    _, h = lax.associative_scan(_lin_combine, (a, u), axis=1)
    return h


def _mlstm_chunk(carry, xs):
    C, n, m = carry
    q, k, v, ig, lf = xs
    L = q.shape[2]
    bcum = jnp.cumsum(lf, axis=-1)
    causal = jnp.tril(jnp.ones((L, L), dtype=bool))
    dmat = jnp.where(causal, bcum[..., :, None] - bcum[..., None, :] + ig[..., None, :], -jnp.inf)
    m_inter = bcum + m[..., None]
    m_t = jnp.maximum(m_inter, jnp.max(dmat, axis=-1))
    w = jnp.exp(dmat - m_t[..., None])
    s_inter = jnp.exp(m_inter - m_t)
    s = jnp.einsum('bhtd,bhsd->bhts', q, k) * w
    num = s_inter[..., None] * jnp.einsum('bhtd,bhde->bhte', q, C) + jnp.einsum('bhts,bhse->bhte', s, v)
    den = s_inter * jnp.einsum('bhtd,bhd->bht', q, n) + jnp.sum(s, axis=-1)
    h = num / jnp.maximum(jnp.abs(den), jnp.exp(-m_t))[..., None]
    m_new = m_t[..., -1]
    decay = jnp.exp(bcum[..., -1] + m - m_new)
    w_last = jnp.exp(bcum[..., -1:] - bcum + ig - m_new[..., None])
    C_new = decay[..., None, None] * C + jnp.einsum('bhs,bhsd,bhse->bhde', w_last, k, v)
    n_new = decay[..., None] * n + jnp.einsum('bhs,bhsd->bhd', w_last, k)
    return (C_new, n_new, m_new), h


def _mlstm(q, k, v, ig, lf, C, n, m):
    B, H, T, _ = q.shape
    L = min(T, CHUNK)
    NC = T // L

    def to_chunks(a):
        return jnp.moveaxis(a.reshape(a.shape[:2] + (NC, L) + a.shape[3:]), 2, 0)

    (C, n, m), h = lax.scan(_mlstm_chunk, (C, n, m), (to_chunks(q), to_chunks(k), to_chunks(v), to_chunks(ig), to_chunks(lf)))
    h = jnp.moveaxis(h, 0, 2).reshape(B, H, T, DV_A)
    return h, (C, n, m)


def _mixer(h, state, p):
    C0, n0, m0, hl0, buf0 = state
    B, T, _ = h.shape
    f32 = jnp.float32
    proj = h @ p['w_in']
    q, k, v, o, ig, fg, xb, gb = jnp.split(proj, IN_SPLITS, axis=-1)

    def heads(a, d):
        return a.reshape(B, T, N_HEADS_A, d).transpose(0, 2, 1, 3).astype(f32)
    q = heads(q, DK_A)
    k = heads(k, DK_A) * (DK_A ** -0.5)
    v = heads(v, DV_A)
    bg = p['b_gates_a'].astype(f32)
    ig = (ig.astype(f32) + bg[0]).transpose(0, 2, 1)
    lf = jax.nn.log_sigmoid(fg.astype(f32) + bg[1]).transpose(0, 2, 1)
    ha, (C1, n1, m1) = _mlstm(q, k, v, ig, lf, C0.astype(f32), n0.astype(f32), m0.astype(f32))
    ha = _rmsnorm(ha, p['head_norm_g'][:, None, :]).transpose(0, 2, 1, 3).reshape(B, T, WIDTH_A)
    ya = ha.astype(h.dtype) * jax.nn.sigmoid(o)

    xc, buf1 = _causal_conv(xb, buf0, p['conv_w'], p['conv_b'])
    r = jax.nn.sigmoid(_block_diag(xc, p['w_r'], p['b_r']))
    i = jax.nn.sigmoid(_block_diag(xc, p['w_i'], p['b_i']))
    hl = _rg_lru(xc.astype(f32), r.astype(f32), i.astype(f32), p['lru_lambda'].astype(f32), hl0.astype(f32))
    yb = hl.astype(h.dtype) * jax.nn.gelu(gb)

    out = jnp.concatenate([ya, yb], axis=-1) @ p['w_out']
    return out, (C1, n1, m1, hl[:, -1], buf1)


def _layer(x, c, state, p):
    ada = jax.nn.silu(c) @ p['w_ada'] + p['b_ada']
    sh1, sc1, g1, sh2, sc2, g2 = jnp.split(ada, 6, axis=-1)
    h = _rmsnorm(x, p['norm1_g']) * (1.0 + sc1[:, None]) + sh1[:, None]
    mix, new_state = _mixer(h, state, p)
    x = x + g1[:, None] * mix
    h = _rmsnorm(x, p['norm2_g']) * (1.0 + sc2[:, None]) + sh2[:, None]
    gate, up = jnp.split(h @ p['w_gu'], 2, axis=-1)
    x = x + g2[:, None] * ((jax.nn.silu(gate) * up) @ p['w_down'])
    return x, new_state


def setup_inputs(seed: int = 0) -> dict:
    key = jax.random.key(seed)
    ks = jax.random.split(key, 32)
    f32 = jnp.float32

    def nrm(k, shape, s):
        return jax.random.normal(k, shape, f32) * s

    a0 = jax.random.uniform(ks[23], (DEPTH, WIDTH_B), f32, 0.9, 0.999)
    forget_b = jnp.linspace(3.0, 6.0, N_HEADS_A, dtype=f32)[None] + nrm(ks[15], (DEPTH, N_HEADS_A), 0.1)
    return {
        'x_prompt': nrm(ks[0], (BATCH, SEQ, D_MODEL), 1.0),
        'x_sample': nrm(ks[1], (DEC_BATCH, DEC_SEQ, D_MODEL), 1.0),
        'c_prompt': nrm(ks[2], (BATCH, D_MODEL), 1.0),
        'c_sample': nrm(ks[3], (DEC_BATCH, D_MODEL), 1.0),
        'state_mlstm_C': nrm(ks[4], (DEPTH, DEC_BATCH, N_HEADS_A, DK_A, DV_A), 0.5),
        'state_mlstm_n': nrm(ks[5], (DEPTH, DEC_BATCH, N_HEADS_A, DK_A), 0.5),
        'state_mlstm_m': jax.random.uniform(ks[6], (DEPTH, DEC_BATCH, N_HEADS_A), f32, -1.0, 1.0),
        'state_lru_h': nrm(ks[7], (DEPTH, DEC_BATCH, WIDTH_B), 0.5),
        'state_conv': nrm(ks[8], (DEPTH, DEC_BATCH, CONV_W - 1, WIDTH_B), 1.0),
        'w_ada': nrm(ks[9], (DEPTH, D_MODEL, 6 * D_MODEL), 0.5 * D_MODEL ** -0.5),
        'b_ada': nrm(ks[10], (DEPTH, 6 * D_MODEL), 0.02),
        'norm1_g': 1.0 + nrm(ks[11], (DEPTH, D_MODEL), 0.02),
        'norm2_g': 1.0 + nrm(ks[12], (DEPTH, D_MODEL), 0.02),
        'w_in': nrm(ks[13], (DEPTH, D_MODEL, IN_COLS), D_MODEL ** -0.5),
        'b_gates_a': jnp.stack([nrm(ks[14], (DEPTH, N_HEADS_A), 0.1), forget_b], axis=1),
        'head_norm_g': 1.0 + nrm(ks[16], (DEPTH, N_HEADS_A, DV_A), 0.02),
        'conv_w': nrm(ks[17], (DEPTH, CONV_W, WIDTH_B), CONV_W ** -0.5),
        'conv_b': nrm(ks[18], (DEPTH, WIDTH_B), 0.02),
        'w_r': nrm(ks[19], (DEPTH, N_BLOCKS_B, BLOCK_B, BLOCK_B), BLOCK_B ** -0.5),
        'b_r': nrm(ks[20], (DEPTH, WIDTH_B), 0.02),
        'w_i': nrm(ks[21], (DEPTH, N_BLOCKS_B, BLOCK_B, BLOCK_B), BLOCK_B ** -0.5),
        'b_i': nrm(ks[22], (DEPTH, WIDTH_B), 0.02),
        'lru_lambda': jnp.log(a0) - jnp.log1p(-a0),
        'w_out': nrm(ks[24], (DEPTH, MIX_WIDTH, D_MODEL), MIX_WIDTH ** -0.5),
        'w_gu': nrm(ks[25], (DEPTH, D_MODEL, 2 * D_FF), D_MODEL ** -0.5),
        'w_down': nrm(ks[26], (DEPTH, D_FF, D_MODEL), D_FF ** -0.5),
        'normf_g': 1.0 + nrm(ks[27], (D_MODEL,), 0.02),
    }


def reference(x_prompt, x_sample, c_prompt, c_sample, state_mlstm_C, state_mlstm_n, state_mlstm_m,
              state_lru_h, state_conv, w_ada, b_ada, norm1_g, norm2_g, w_in, b_gates_a, head_norm_g,
              conv_w, conv_b, w_r, b_r, w_i, b_i, lru_lambda, w_out, w_gu, w_down, normf_g):
    f32 = jnp.float32
    xp, xs = x_prompt, x_sample
    Bp = xp.shape[0]
    p_states = []
    s_states = []
    for l in range(DEPTH):
        p = {'w_ada': w_ada[l], 'b_ada': b_ada[l], 'norm1_g': norm1_g[l], 'norm2_g': norm2_g[l],
             'w_in': w_in[l], 'b_gates_a': b_gates_a[l], 'head_norm_g': head_norm_g[l],
             'conv_w': conv_w[l], 'conv_b': conv_b[l], 'w_r': w_r[l], 'b_r': b_r[l],
             'w_i': w_i[l], 'b_i': b_i[l], 'lru_lambda': lru_lambda[l], 'w_out': w_out[l],
             'w_gu': w_gu[l], 'w_down': w_down[l]}
        zero_state = (jnp.zeros((Bp, N_HEADS_A, DK_A, DV_A), f32),
                      jnp.zeros((Bp, N_HEADS_A, DK_A), f32),
                      jnp.zeros((Bp, N_HEADS_A), f32),
                      jnp.zeros((Bp, WIDTH_B), f32),
                      jnp.zeros((Bp, CONV_W - 1, WIDTH_B), xp.dtype))
        xp, st_p = _layer(xp, c_prompt, zero_state, p)
        cache_state = (state_mlstm_C[l], state_mlstm_n[l], state_mlstm_m[l], state_lru_h[l], state_conv[l])
        xs, st_s = _layer(xs, c_sample, cache_state, p)
        p_states.append(st_p)
        s_states.append(st_s)
    y_prompt = _rmsnorm(xp, normf_g)
    y_sample = _rmsnorm(xs, normf_g)
    p_C = jnp.stack([s[0] for s in p_states])
    p_n = jnp.stack([s[1] for s in p_states])
    p_m = jnp.stack([s[2] for s in p_states])
    p_h = jnp.stack([s[3] for s in p_states])
    p_conv = jnp.stack([s[4] for s in p_states])
    s_C = jnp.stack([s[0] for s in s_states])
    s_n = jnp.stack([s[1] for s in s_states])
    s_m = jnp.stack([s[2] for s in s_states])
    s_h = jnp.stack([s[3] for s in s_states])
    s_conv = jnp.stack([s[4] for s in s_states])
    return (y_prompt, y_sample, p_C, p_n, p_m, p_h, p_conv, s_C, s_n, s_m, s_h, s_conv)
```

```python
from contextlib import ExitStack
import numpy as np
import concourse.bass as bass
import concourse.mybir as mybir
from concourse.bass_utils import run_bass_kernel_spmd

F32 = mybir.dt.float32
BF16 = mybir.dt.bfloat16
U8 = mybir.dt.uint8
AF = mybir.ActivationFunctionType
ALU = mybir.AluOpType
AX = mybir.AxisListType

D = 4096
KC = 32
IN_COLS = 10256
DFF = 11008
EPS = 1e-6
NEGBIG = -1.0e9
Q0, K0, V0, O0, IG0, FG0, XB0, GB0 = 0, 1024, 2048, 4096, 6144, 6152, 6160, 8208


class Prog:
    E = ['pe', 'act', 'dve', 'pool', 'sp']

    def __init__(self, nc, n_dma_sems=24):
        self.nc = nc
        self.ops = {e: [] for e in self.E}
        self.cnt = {e: 0 for e in self.E}
        self.known = {e: {} for e in self.E}
        self.buf = {}
        self.nd = n_dma_sems
        self.dcnt = [0] * n_dma_sems
        self.drr = 0
        self.nsw = 6
        self.swrr = 0
        self.final = []
        self.marks = []

    def _deps(self, reads, writes):
        deps = {}

        def add(s, v):
            if deps.get(s, 0) < v:
                deps[s] = v
        for k in reads:
            b = self.buf.get(k)
            if b and b['w']:
                add(*b['w'])
        for k in writes:
            b = self.buf.get(k)
            if b:
                if b['w']:
                    add(*b['w'])
                for s, v in b['r'].items():
                    add(s, v)
        return deps

    def _wait(self, e, deps):
        kn = self.known[e]
        for s, v in deps.items():
            if s == 'pe' and e == 'pe':
                continue
            if kn.get(s, 0) < v:
                self.ops[e].append(('w', s, v))
                kn[s] = v

    def _mark(self, reads, writes, tag):
        s, v = tag
        for k in reads:
            b = self.buf.setdefault(k, {'w': None, 'r': {}})
            if b['r'].get(s, 0) < v:
                b['r'][s] = v
        for k in writes:
            self.buf[k] = {'w': tag, 'r': {}}

    @staticmethod
    def _excl(reads, writes):
        ps_r = [k for k in reads if isinstance(k, tuple) and k[0] == 'ps']
        if ps_r:
            reads = [k for k in reads if not (isinstance(k, tuple) and k[0] == 'ps')]
            writes = list(writes) + [k for k in ps_r if k not in writes]
        return reads, writes

    def op(self, e, fn, reads=(), writes=()):
        reads, writes = self._excl(reads, writes)
        self._wait(e, self._deps(reads, writes))
        self.cnt[e] += 1
        self.ops[e].append(('o', fn, e, 1))
        self._mark(reads, writes, (e, self.cnt[e]))

    def dma(self, q, fn, reads=(), writes=(), final=False):
        if q == 'pool':
            i = self.nd - self.nsw + self.swrr
            self.swrr = (self.swrr + 1) % self.nsw
        else:
            i = self.drr
            self.drr = (self.drr + 1) % (self.nd - self.nsw)
        s = 'd%d' % i
        deps = self._deps(reads, writes)
        if self.dcnt[i] > 0:
            deps[s] = max(deps.get(s, 0), self.dcnt[i])
        self._wait(q, deps)
        self.dcnt[i] += 16
        self.ops[q].append(('o', fn, s, 16))
        self._mark(reads, writes, (s, self.dcnt[i]))
        if final:
            self.final.append((s, self.dcnt[i]))

    def mark(self, label):
        for e in self.E:
            self.ops[e].append(('m', label))

    def alias(self, new_keys, old_keys):
        r = {}
        for ok in old_keys:
            b = self.buf.get(ok)
            if not b:
                continue
            if b['w']:
                r[b['w'][0]] = max(r.get(b['w'][0], 0), b['w'][1])
            for s_, v_ in b['r'].items():
                r[s_] = max(r.get(s_, 0), v_)
        for nk in new_keys:
            nb = self.buf.setdefault(nk, {'w': None, 'r': {}})
            for s_, v_ in r.items():
                nb['r'][s_] = max(nb['r'].get(s_, 0), v_)

    def build(self):
        nc = self.nc
        for s, v in self.final:
            self._wait('sp', {s: v})
        for i in range(self.nd):
            if self.dcnt[i] > 0:
                self._wait('sp', {'d%d' % i: self.dcnt[i]})
        with ExitStack() as st:
            sems = {}
            for name in self.E + ['d%d' % i for i in range(self.nd)]:
                sems[name] = st.enter_context(nc.semaphore('s_' + name))
            st.enter_context(nc.allow_low_precision(reason="bf16 matmul operands, fp32 accumulation"))
            block = st.enter_context(nc.Block())

            def run(eng, lst):
                for o in lst:
                    if o[0] == 'm':
                        self.marks.append((str(eng), o[1], nc.get_next_instruction_name()))
                        continue
                    if o[0] == 'w':
                        eng.wait_ge(sems[o[1]], o[2])
                    else:
                        inst = o[1](eng)
                        inst.then_inc(sems[o[2]], o[3])

            @block.tensor
            def _(t):
                run(t, self.ops['pe'])

            @block.scalar
            def _(t):
                run(t, self.ops['act'])

            @block.vector
            def _(t):
                run(t, self.ops['dve'])

            @block.gpsimd
            def _(t):
                run(t, self.ops['pool'])

            @block.sync
            def _(t):
                run(t, self.ops['sp'])


LAST_MARKS = []


class StopBuild(Exception):
    pass


class Arena:
    def __init__(self, nc, nbytes):
        self.t = nc.alloc_sbuf_tensor('arena', [128, nbytes], U8)
        self.ap = self.t.ap()
        self.nbytes = nbytes

    def view(self, off, shape, dt, parts=128, p0=0):
        sz = 2 if dt == BF16 else 4
        n = 1
        for s in shape:
            n *= s
        assert off % 4 == 0 and off + n * sz <= self.nbytes, (off, shape)
        v = self.ap[p0:p0 + parts, off:off + n * sz].bitcast(dt)
        if len(shape) == 2:
            v = v.rearrange('p (a b) -> p a b', a=shape[0])
        elif len(shape) == 3:
            v = v.rearrange('p (a b c) -> p a b c', a=shape[0], b=shape[1])
        return v


def bcast_ap(handle, offset, nparts, n):
    return bass.AP(handle, offset, [[0, nparts], [1, n]])


def build_program(cfg):
    NPRE = cfg.get('npre', 4)
    NMAIN = cfg.get('nmain', 4)
    DO_SAMPLE = cfg.get('sample', True)
    nc = bass.Bass("TRN2", target_bir_lowering=False)
    P = Prog(nc)
    STOP = cfg.get('stop')

    def din(name, shape):
        return nc.dram_tensor(name, list(shape), F32, kind="ExternalInput")

    def dout(name, shape):
        return nc.dram_tensor(name, list(shape), F32, kind="ExternalOutput")

    xpre_h = din('xpre', [2048, D]); xmain_h = din('xmain', [2048, D]); xs_h = din('xs', [128, D])
    c3_h = din('c3', [3, D]); flag_h = din('flag', [128, 1])
    isC_h = din('isC', [2, 8, 128, 256]); isn_h = din('isn', [2, 8, 128]); ism_h = din('ism', [2, 8])
    ish_h = din('ish', [2, 2048]); iscv_h = din('iscv', [2, 3, 2048])
    wada_h = din('w_ada', [D, 6 * D]); bada_h = din('b_ada', [6 * D])
    n1g_h = din('norm1_g', [D]); n2g_h = din('norm2_g', [D])
    win_h = din('w_in', [D, IN_COLS]); bg_h = din('b_gates_a', [2, 8]); hng_h = din('head_norm_g', [8, 256])
    cw_h = din('conv_w', [4, 2048]); cb_h = din('conv_b', [2048])
    wr_h = din('w_r', [16, 128, 128]); br_h = din('b_r', [2048]); wi_h = din('w_i', [16, 128, 128]); bi_h = din('b_i', [2048])
    lam_h = din('lru_lambda', [2048])
    wout_h = din('w_out', [D, D]); wgu_h = din('w_gu', [D, 2 * DFF]); wdn_h = din('w_down', [DFF, D])
    nfg_h = din('normf_g', [D]); consts_h = din('consts', [128, 1024])

    yp_h = dout('yp', [2048, D]); ys_h = dout('ys', [128, D])
    opC_h = dout('opC', [8, 128, 256]); opn_h = dout('opn', [8, 128]); opm_h = dout('opm', [1, 8])
    oph_h = dout('oph', [2048]); opcv_h = dout('opcv', [3, 2048])
    osC_h = dout('osC', [2, 8, 128, 256]); osn_h = dout('osn', [2, 8, 128]); osm_h = dout('osm', [2, 8])
    osh_h = dout('osh', [2, 2048]); oscv_h = dout('oscv', [2, 3, 2048])

    wbin_h = nc.dram_tensor('wbin', [D, IN_COLS], BF16, kind="Internal")
    wbout_h = nc.dram_tensor('wbout', [D, D], BF16, kind="Internal")
    wbgu_h = nc.dram_tensor('wbgu', [D, 2 * DFF], BF16, kind="Internal")
    wbdn_h = nc.dram_tensor('wbdn', [DFF, D], BF16, kind="Internal")
    ada_h = nc.dram_tensor('ada_s', [3, 6 * D], F32, kind="Internal")
    wbr_h = nc.dram_tensor('wbr', [16, 128, 128], BF16, kind="Internal")
    wgs_h = nc.dram_tensor('wgs', [128, 2 * KC * 40], BF16, kind="Internal")
    wbi_h = nc.dram_tensor('wbi', [16, 128, 128], BF16, kind="Internal")

    A = Arena(nc, 212480)
    RING, R_H, R_Y, R_X, PERS = 0, 49152, 81920, 114688, 180224
    NSLOT = 3
    ps = [nc.alloc_psum_tensor('ps%d' % i, [128, 512], F32).ap() for i in range(8)]

    def psk(i):
        return ('ps', i)

    po = [PERS]

    def pal(shape, dt, parts=128):
        sz = 2 if dt == BF16 else 4
        n = 1
        for s in shape:
            n *= s
        off = po[0]
        po[0] += (n * sz + 31) // 32 * 32
        return A.view(off, shape, dt, parts=parts)

    Cst = pal([8, 257], F32); Cb = pal([8, 257], BF16)
    identF = pal([128], F32); identB = pal([128], BF16)
    NEGM = pal([64], F32, parts=64)
    scanmask = pal([512], F32, parts=40)
    ones40 = pal([128], F32, parts=40)
    ccols = pal([32], F32)
    MODC = pal([3, 4, 32], F32)
    hst = pal([16], F32); convbuf = pal([16, 3], F32)
    lruc = pal([16, 10], F32)
    mst = pal([1], F32, parts=40)
    flagc = pal([1], F32)
    bgc = pal([2], F32, parts=40)
    smallc = pal([64], F32)
    TMP = po[0]
    TMPSZ = A.nbytes - TMP
    assert TMPSZ >= 12288 + 1024, TMPSZ
    SCV = A.view(TMP + 12288, [2, 16, 3], F32)
    SCVO = A.view(TMP + 12288 + 384, [2, 16, 3], F32)
    SH = A.view(TMP + 12288 + 768, [2, 16], F32)
    SHO = A.view(TMP + 12288 + 896, [2, 16], F32)

    consts = consts_h.ap()

    def mm(out_ap, pairs, reads, writes):
        def fn(pe, pairs=pairs, out_ap=out_ap):
            n = len(pairs)
            last = None
            for i, (l, r) in enumerate(pairs):
                last = pe.matmul(out_ap, l, r, start=(i == 0), stop=(i == n - 1))
            return last
        P.op('pe', fn, reads, writes)

    def tr(out_ap, in_ap, ident, reads, writes):
        P.op('pe', lambda pe: pe.transpose(out_ap, in_ap, ident), reads, writes)

    def act(out_ap, in_ap, func, reads, writes, bias=None, scale=None, accum=None):
        kw = {}
        if bias is not None:
            kw['bias'] = bias
        if scale is not None:
            kw['scale'] = scale
        if accum is not None:
            kw['accum_out'] = accum
        P.op('act', lambda e: e.activation(out_ap, in_ap, func, **kw), reads, writes)

    def ts(eng, out_ap, in_ap, s1, s2, op0, op1, reads, writes):
        if s2 is None and not isinstance(s1, (int, float)):
            P.op(eng, lambda e: e.tensor_scalar(out_ap, in_ap, s1, 0.0, op0, ALU.add), reads, writes)
        elif s2 is None:
            P.op(eng, lambda e: e.tensor_scalar(out_ap, in_ap, s1, None, op0), reads, writes)
        else:
            P.op(eng, lambda e: e.tensor_scalar(out_ap, in_ap, s1, s2, op0, op1), reads, writes)

    def tt(eng, out_ap, a, b, op, reads, writes):
        P.op(eng, lambda e: e.tensor_tensor(out_ap, a, b, op), reads, writes)

    def stt(eng, out_ap, in0, sc, in1, op0, op1, reads, writes):
        P.op(eng, lambda e: e.scalar_tensor_tensor(out_ap, in0, sc, in1, op0, op1), reads, writes)

    def cp(eng, out_ap, in_ap, reads, writes):
        if eng == 'act':
            P.op(eng, lambda e: e.activation(out_ap, in_ap, AF.Copy), reads, writes)
        else:
            P.op(eng, lambda e: e.tensor_copy(out_ap, in_ap), reads, writes)

    def rsum(out_ap, in_ap, reads, writes):
        P.op('dve', lambda e: e.reduce_sum(out_ap, in_ap, AX.X), reads, writes)

    def recip(out_ap, in_ap, reads, writes):
        P.op('dve', lambda e: e.reciprocal(out_ap, in_ap), reads, writes)

    def mset(eng, ap, val, writes):
        P.op(eng, lambda e: e.memset(ap, val), (), writes)

    def dma(q, out_ap, in_ap, reads, writes, final=False, nonc=False):
        if nonc:
            P.dma(q, lambda e: e.dma_start(out=out_ap, in_=in_ap, allow_slow_non_contiguous=True), reads, writes, final)
        else:
            P.dma(q, lambda e: e.dma_start(out=out_ap, in_=in_ap), reads, writes, final)

    ring_i = [0]

    def ring_slot():
        i = ring_i[0] % NSLOT
        ring_i[0] += 1
        return i, RING + i * 16384, ('ring', i)

    tile_first = [False]

    def body():
        dma('sp', identF, consts[:, 0:128], (), ['identF'])
        dma('sp', NEGM, consts[0:64, 128:192], (), ['NEGM'])
        dma('sp', scanmask, consts[0:40, 192:704], (), ['scanmask'])
        dma('sp', ccols, consts[:, 704:736], (), ['ccols'])
        dma('sp', flagc, flag_h.ap(), (), ['flagc'])
        cp('dve', identB, identF, ['identF'], ['identB'])
        mset('dve', ones40, 1.0, ['ones40'])
        mset('dve', bgc, 0.0, ['bgc'])
        bgap = bg_h.ap()
        for r0 in (0, 32):
            dma('sp', bgc[r0:r0 + 8, 0:1], bgap[0:1, :].rearrange('a h -> h a'), (), ['bgc'], nonc=True)
            dma('sp', bgc[r0:r0 + 8, 1:2], bgap[1:2, :].rearrange('a h -> h a'), (), ['bgc'], nonc=True)
        for col, h_ in ((6, cb_h), (7, br_h), (8, bi_h), (9, lam_h)):
            dma('sp', lruc[:, :, col], h_.ap().rearrange('(b p) -> p b', p=128), (), ['lruc'], nonc=True)
        for j in range(4):
            dma('sp', lruc[:, :, 2 + j], cw_h.ap()[j, :].rearrange('(b p) -> p b', p=128), (), ['lruc'], nonc=True)
        act(lruc[:, :, 0], lruc[:, :, 9], AF.Exp, ['lruc'], ['lruc'], scale=-1.0)
        act(lruc[:, :, 0], lruc[:, :, 0], AF.Ln, ['lruc'], ['lruc'], bias=1.0)
        ts('dve', lruc[:, :, 1], lruc[:, :, 0], -16.0, None, ALU.mult, None, ['lruc'], ['lruc'])
        ts('dve', lruc[:, :, 0], lruc[:, :, 0], -8.0, None, ALU.mult, None, ['lruc'], ['lruc'])

        if STOP == 'p0':
            raise StopBuild()
        P.mark('ada')
        XO = R_X
        c3t = A.view(XO, [D], F32, parts=3)
        c3e = A.view(XO + 16384, [D], F32, parts=3)
        scT = A.view(XO + 32768, [KC, 4], BF16)
        badat = A.view(XO + 36864, [D], F32, parts=3)
        adast = A.view(XO + 36864 + 16384, [D], F32, parts=3)
        dma('sp', c3t, c3_h.ap(), (), ['c3t'])
        act(c3e, c3t, AF.Exp, ['c3t'], ['c3e'], scale=-1.0)
        ts('dve', c3e, c3e, 1.0, None, ALU.add, None, ['c3e'], ['c3e'])
        recip(c3e, c3e, ['c3e'], ['c3e'])
        tt('dve', c3t, c3t, c3e, ALU.mult, ['c3t', 'c3e'], ['c3t'])
        for kc in range(KC):
            pst = ps[kc % 2]
            tr(pst[:, 0:3], c3t[:, kc * 128:(kc + 1) * 128], identF[0:3, 0:3], ['c3t', 'identF'], [psk(kc % 2)])
            cp('dve', scT[:, kc, 0:3], pst[:, 0:3], [psk(kc % 2)], ['scT'])
        wada_v = wada_h.ap().rearrange('(kc p) c -> p kc c', p=128)
        for part in range(6):
            dma('sp', badat, bcast_ap(bada_h, part * D, 3, D), (), ['badat'])
            for sg in range(16):
                si, so_, skey = ring_slot()
                slab = A.view(so_, [KC, 256], BF16)
                c0 = part * D + sg * 256
                dma('pool', slab, wada_v[:, :, c0:c0 + 256], (), [skey])
                pb = 2 + (sg % 2)
                mm(ps[pb][0:3, 0:256], [(scT[:, kc, 0:3], slab[:, kc, :]) for kc in range(KC)], ['scT', skey], [psk(pb)])
                tt('dve', adast[:, sg * 256:(sg + 1) * 256], ps[pb][0:3, 0:256], badat[:, sg * 256:(sg + 1) * 256], ALU.add,
                   [psk(pb), 'badat'], ['adast'])
            dma('sp', ada_h.ap()[:, part * D:(part + 1) * D], adast, ['adast'], ['ada_dram'])

        if STOP == 'ada':
            raise StopBuild()
        rowt = A.view(XO, [128], F32, parts=32)
        gcol = A.view(XO + 512, [2, 32], F32)
        for gi, gh in enumerate((n1g_h, n2g_h)):
            dma('sp', rowt, gh.ap().rearrange('(j p) -> j p', p=128), (), ['rowt'])
            tr(ps[0][:, 0:32], rowt, identF[0:32, 0:32], ['rowt', 'identF'], [psk(0)])
            cp('dve', gcol[:, gi, :], ps[0][:, 0:32], [psk(0)], ['gcol'])
        for slot in range(3):
            for (part, mi, gi) in ((1, 0, 0), (0, 1, None), (4, 2, 1), (3, 3, None)):
                dma('sp', rowt, ada_h.ap()[slot, part * D:(part + 1) * D].rearrange('(j p) -> j p', p=128), ['ada_dram'], ['rowt'])
                tr(ps[0][:, 0:32], rowt, identF[0:32, 0:32], ['rowt', 'identF'], [psk(0)])
                if gi is None:
                    cp('dve', MODC[:, slot, mi, :], ps[0][:, 0:32], [psk(0)], ['MODC'])
                else:
                    stt('dve', MODC[:, slot, mi, :], ps[0][:, 0:32], 1.0, gcol[:, gi, :], ALU.add, ALU.mult,
                        [psk(0), 'gcol'], ['MODC'])

        if STOP == 'modc':
            raise StopBuild()
        P.mark('cast')
        def cast_w(src_h, dst_h, rows, key, npieces):
            step = rows // npieces
            for i in range(npieces):
                dma('pool', dst_h.ap()[i * step:(i + 1) * step, :], src_h.ap()[i * step:(i + 1) * step, :], (), [(key, i)])
            return [(key, i) for i in range(npieces)]
        WG0 = A.view(R_X, [2, KC, 40], BF16)
        mset('dve', WG0, 0.0, ['WG0', 'c3t', 'rowt', 'gcol'])
        winv0 = win_h.ap().rearrange('(kc p) c -> p kc c', p=128)
        for gi, g0 in enumerate((IG0, FG0)):
            for r0 in (0, 32):
                dma('pool', WG0[:, gi, :, r0:r0 + 8], winv0[:, :, g0:g0 + 8], (), ['WG0'], nonc=True)
        dma('sp', wgs_h.ap(), WG0.rearrange('p a b c -> p (a b c)'), ['WG0'], ['wgs'])
        dma('pool', wbr_h.ap(), wr_h.ap(), (), ['wbr'])
        dma('pool', wbi_h.ap(), wi_h.ap(), (), ['wbr'])
        k_win = cast_w(win_h, wbin_h, D, 'wbin', 8)
        k_wout = cast_w(wout_h, wbout_h, D, 'wbout', 4)
        k_wgu = cast_w(wgu_h, wbgu_h, D, 'wbgu', 16)
        k_wdn = cast_w(wdn_h, wbdn_h, DFF, 'wbdn', 8)

        wbin_v = wbin_h.ap().rearrange('(kc p) c -> p kc c', p=128)
        wbout_v = wbout_h.ap().rearrange('(kc p) c -> p kc c', p=128)
        wbgu_v = wbgu_h.ap().rearrange('(kc p) c -> p kc c', p=128)
        wbdn_v = wbdn_h.ap().rearrange('(f p) c -> p f c', p=128)

        if STOP == 'cast':
            raise StopBuild()
        mset('dve', Cst, 0.0, ['Cst'])
        mset('dve', mst, 0.0, ['mst'])
        mset('dve', hst, 0.0, ['hst'])
        mset('dve', convbuf, 0.0, ['convbuf'])

        def run_tile(x_src, T, segs, mode, y_dst=None):
            NBLK = T // 128
            NCH = T // 64
            full = (mode == 'main')
            h1T = A.view(R_H, [KC, T], BF16)
            yT = A.view(R_Y, [KC, T], BF16)
            KEYS_B = [('st', i) for i in range(12)] + ['COLS', 'DECB', 'WG', 'gainb', 'sc4'] + \
                [(n_, i) for n_ in ('qT', 'kT', 'kw', 'vext', 'so', 'grh', 'WT', 'SWT', 'HT', 'ND', 'HN', 'T1', 'dcol', 'rcol') for i in range(2)] + \
                [('yatok', b_) for b_ in range(4)]
            KEYS_L = ['WRI'] + [(n_, i) for n_ in ('XBH', 'XC', 'XCb', 'g0', 'g1', 'AA', 'A2', 'HH', 'GX', 'GW') for i in range(2)]
            KEYS_A = [('xin', 0), ('xin', 1), ('axn', 0), ('axn', 1)]
            KEYS_X1 = [('x1', b_) for b_ in range(4)]
            ykeys = [('yT', j) for j in range(KC)]
            KEYS_YD = ['G2B', ('aT', 0), ('aT', 1), ('cxn', 0), ('cxn', 1), 'csqj']

            P.mark(mode + str(T) + ':A')
            def norm_to_T(src_fn, dstT, mod_scale, mod_shift, xn_off, junk_off, tag, otag='h'):
                for blk in range(NBLK):
                    xin, xkey = src_fn(blk)
                    xn = A.view(xn_off + (blk % 2) * 8192, [D], BF16)
                    xnk = (tag + 'xn', blk % 2)
                    ssq = smallc[:, (blk % 2):(blk % 2) + 1]
                    sk = (tag + 'ssq', blk % 2)
                    if cfg.get('astop') == 0:
                        raise StopBuild()
                    sqj = A.view(junk_off, [D], F32)
                    act(sqj, xin, AF.Square, [xkey], [(tag + 'sqj')])
                    rsum(ssq, sqj, [(tag + 'sqj')], [sk])
                    if cfg.get('astop') == 1:
                        raise StopBuild()
                    ts('dve', ssq, ssq, 1.0 / D, EPS, ALU.mult, ALU.add, [sk], [sk])
                    act(ssq, ssq, AF.Ln, [sk], [sk])
                    act(ssq, ssq, AF.Exp, [sk], [sk], scale=-0.5)
                    if cfg.get('astop') == 2:
                        raise StopBuild()
                    ts('dve', xn, xin, ssq, None, ALU.mult, None, [xkey, sk], [xnk])
                    if cfg.get('astop') == 3:
                        raise StopBuild()
                    for j in range(KC):
                        pb = (j // 4) % 2
                        pbt = ps[pb].bitcast(BF16).rearrange('p (a b) -> p a b', a=8)
                        if j % 4 == 0:
                            for jj in range(4):
                                tr(pbt[:, jj, :], xn[:, (j + jj) * 128:(j + jj + 1) * 128], identB, [xnk, 'identB'],
                                   [psk(pb)] if jj == 0 else [])
                            P.buf[psk(pb)]['w'] = ('pe', P.cnt['pe'])
                            if cfg.get('astop') == 4:
                                raise StopBuild()
                        for (slot, t0, t1) in segs:
                            lo = max(t0, blk * 128); hi = min(t1, (blk + 1) * 128)
                            if lo >= hi:
                                continue
                            eng = 'dve' if pb == 0 else 'pool'
                            if eng == 'pool':
                                act(dstT[:, j, lo:hi], pbt[:, j % 4, lo - blk * 128:hi - blk * 128], AF.Identity,
                                    [psk(pb), 'MODC'], [(otag + 'T', j)], bias=mod_shift(slot, j), scale=mod_scale(slot, j))
                            else:
                                ts('dve', dstT[:, j, lo:hi], pbt[:, j % 4, lo - blk * 128:hi - blk * 128],
                                   mod_scale(slot, j), mod_shift(slot, j), ALU.mult, ALU.add,
                                   [psk(pb), 'MODC'], [(otag + 'T', j)])
                    if cfg.get('astop') == 5:
                        raise StopBuild()

            def srcA(blk):
                xin = A.view(R_X + (blk % 2) * 16384, [D], F32)
                key = ('xin', blk % 2)
                dma('sp', xin, x_src[blk * 128:(blk + 1) * 128, :], (), [key])
                return xin, key
            hkeys = [('hT', j) for j in range(KC)]
            P.alias(KEYS_A, KEYS_X1 + KEYS_L + KEYS_B + ['c3t', 'c3e', 'scT', 'badat', 'adast', 'rowt', 'gcol', 'WG0'])
            P.alias(hkeys, ['NFG', 'fjunk'])
            norm_to_T(srcA, h1T, lambda s, j: MODC[:, s, 0, j:j + 1], lambda s, j: MODC[:, s, 1, j:j + 1],
                      R_X + 32768, R_X + 49152, 'a')
            hkeys = [('hT', j) for j in range(KC)]

            if STOP == 'A':
                raise StopBuild()
            P.mark(mode + str(T) + ':Bstats')
            SB = R_X
            MB = R_X + 28672

            def stat(i):
                return A.view(SB + i * 2048, [T], F32, parts=40)
            S_A, S_Z, S_W, S_BC, S_D0, S_M, S_GL, S_GRB, S_SI, S_EM, S_WL, S_MI = [stat(i) for i in range(12)]
            S_GRH = [stat(12), stat(13)]
            mo = [MB]

            def mal(shape, dt, parts=128):
                sz = 2 if dt == BF16 else 4
                n = 1
                for s in shape:
                    n *= s
                off = mo[0]
                mo[0] += (n * sz + 31) // 32 * 32
                assert mo[0] <= R_X + 65536, mo[0]
                return A.view(off, shape, dt, parts=parts)
            WG = mal([2, KC, 40], BF16)
            COLS = mal([NBLK, 120], F32)
            DECB = mal([8, 8], F32)
            qT = [mal([T], BF16) for _ in range(2)]
            kT = [mal([T], BF16) for _ in range(2)]
            kw = [mal([NBLK, 128], BF16) for _ in range(2)]
            vext = [mal([NBLK, 257], BF16) for _ in range(2)]
            so = [mal([NBLK, 256], BF16) for _ in range(2)]
            yatok = mal([NBLK, 256], BF16)
            gainb = mal([2, 256], F32)
            WT = [mal([64], F32) for _ in range(2)]
            SWT = [mal([64], BF16) for _ in range(2)]
            HT = [mal([257], F32) for _ in range(2)]
            ND = [mal([257], F32) for _ in range(2)]
            HN = [mal([256], F32) for _ in range(2)]
            T1 = [mal([256], F32) for _ in range(2)]
            sc4 = mal([16], F32)

            P.alias(KEYS_B, KEYS_A)
            P.alias(ykeys, KEYS_YD)
            if mode == 'main' and tile_first[0]:
                ts('dve', mst, mst, flagc[0:40, 0:1], None, ALU.mult, None, ['mst', 'flagc'], ['mst'])
            dma('sp', WG.rearrange('p a b c -> p (a b c)'), wgs_h.ap(), ['wgs'] + k_win + k_wout + k_wgu + k_wdn, ['WG'])
            for gi, dst in enumerate((S_A, S_Z)):
                pb = 2 + gi
                mm(ps[pb][0:40, 0:T], [(WG[:, gi, kc, :], h1T[:, kc, :]) for kc in range(KC)], ['WG'] + hkeys, [psk(pb)])
                ts('dve', dst, ps[pb][0:40, 0:T], bgc[:, gi:gi + 1], None, ALU.add, None, [psk(pb), 'bgc'], [('st', gi)])
            act(S_W, S_Z, AF.Abs, [('st', 1)], [('st', 2)])
            act(S_W, S_W, AF.Exp, [('st', 2)], [('st', 2)], scale=-1.0)
            act(S_W, S_W, AF.Ln, [('st', 2)], [('st', 2)], bias=1.0)
            stt('dve', S_Z, S_Z, 0.0, S_W, ALU.min, ALU.subtract, [('st', 1), ('st', 2)], [('st', 1)])
            for (slot, t0, t1) in segs:
                n = t1 - t0
                nch = n // 64
                P.op('dve', lambda e, t0=t0, t1=t1, n=n: e.tensor_tensor_scan(S_BC[:, t0:t1], scanmask[:, 0:n], S_Z[:, t0:t1], 0.0,
                                                                             ALU.mult, ALU.add),
                     [('st', 1), 'scanmask'], [('st', 3)])
            tt('dve', S_A, S_A, S_BC, ALU.subtract, [('st', 0), ('st', 3)], [('st', 0)])
            mset('dve', S_D0, 0.0, [('st', 4)])
            mkeys = []
            for (slot, t0, t1) in segs:
                n = t1 - t0
                nch = n // 64
                if nch > 1:
                    d0v = S_D0[:, t0:t1].rearrange('p (c l) -> p c l', l=64)
                    bcv = S_BC[:, t0:t1].rearrange('p (c l) -> p c l', l=64)
                    cp('dve', d0v[:, 1:nch, 0:1], bcv[:, 0:nch - 1, 63:64], [('st', 3)], [('st', 4)])
                mk = ('mst', slot)
                if slot != 0:
                    for r0 in (0, 32):
                        dma('sp', mst[r0:r0 + 8, :], ism_h.ap()[slot - 1:slot, :].rearrange('a h -> h a'), (), ['mst'], nonc=True)
                P.op('dve', lambda e, t0=t0, t1=t1: e.tensor_tensor_scan(S_M[:, t0:t1], S_D0[:, t0:t1], S_A[:, t0:t1], mst[:, 0:1],
                                                                         ALU.add, ALU.max),
                     [('st', 4), ('st', 0), 'mst'], [('st', 5)])
                miv = S_MI[:, t0:t1].rearrange('p (c l) -> p c l', l=64)
                mv = S_M[:, t0:t1].rearrange('p (c l) -> p c l', l=64)
                bcv = S_BC[:, t0:t1].rearrange('p (c l) -> p c l', l=64)
                mset('dve', S_MI[:, t0:t1], 0.0, [('st', 11)])
                ts('dve', S_MI[:, t0:t0 + 64], S_MI[:, t0:t0 + 64], mst[:, 0:1], None, ALU.add, None, [('st', 11), 'mst'], [('st', 11)])
                for c in range(1, nch):
                    tt('dve', sc4[0:40, 0:1], bcv[:, c - 1, 63:64], mv[:, c - 1, 63:64], ALU.add, [('st', 3), ('st', 5)], ['sc4'])
                    ts('dve', miv[:, c, :], miv[:, c, :], sc4[0:40, 0:1], None, ALU.add, None, [('st', 11), 'sc4'], [('st', 11)])
                for c in range(nch):
                    ts('dve', S_WL[:, t0 + c * 64:t0 + (c + 1) * 64], S_A[:, t0 + c * 64:t0 + (c + 1) * 64], mv[:, c, 63:64], None,
                       ALU.subtract, None, [('st', 0), ('st', 5)], [('st', 10)])
                tt('dve', mst[:, 0:1], bcv[:, nch - 1, 63:64], mv[:, nch - 1, 63:64], ALU.add, [('st', 3), ('st', 5), ('st', 11)], ['mst'])
                if slot != 0:
                    dma('sp', osm_h.ap()[slot - 1:slot, :].rearrange('a h -> h a'), mst[0:8, :], ['mst'], [], final=True, nonc=True)
            act(S_WL, S_WL, AF.Exp, [('st', 10)], [('st', 10)])
            tt('dve', S_SI, S_MI, S_M, ALU.subtract, [('st', 11), ('st', 5)], [('st', 8)])
            act(S_SI, S_SI, AF.Exp, [('st', 8)], [('st', 8)])
            tt('dve', S_EM, S_BC, S_M, ALU.add, [('st', 3), ('st', 5)], [('st', 9)])
            act(S_EM, S_EM, AF.Exp, [('st', 9)], [('st', 9)], scale=-1.0)
            ts('dve', S_GL, S_A, ccols[0:40, 0:1], ccols[0:40, 1:2], ALU.mult, ALU.add, [('st', 0), 'ccols'], [('st', 6)])
            ts('dve', S_GRB, S_M, ccols[0:40, 2:3], ccols[0:40, 0:1], ALU.mult, ALU.add, [('st', 5), 'ccols'], [('st', 7)])
            for blk in range(NBLK):
                pst = ps[4][:, 0:120].rearrange('p (a b) -> p a b', a=3)
                for i, (src, k) in enumerate(((S_SI, 8), (S_EM, 9), (S_WL, 10))):
                    tr(pst[:, i, :], src[:, blk * 128:(blk + 1) * 128], identF[0:40, 0:40], [('st', k), 'identF'],
                       [psk(4)] if i == 0 else [])
                P.buf[psk(4)]['w'] = ('pe', P.cnt['pe'])
                cp('dve', COLS[:, blk, :], ps[4][:, 0:120], [psk(4)], ['COLS'])
            siv = S_SI.rearrange('p (c l) -> p c l', l=64)
            for h in range(8):
                ts('dve', sc4[0:40, 0:NCH], siv[:, :, 63], ccols[0:40, 11 + h:12 + h], None, ALU.mult, None,
                   [('st', 8), 'ccols'], ['sc4'])
                mm(ps[5][:, 0:NCH], [(ones40, sc4[0:40, 0:NCH])], ['ones40', 'sc4'], [psk(5)])
                cp('dve', DECB[:, h, 0:NCH], ps[5][:, 0:NCH], [psk(5)], ['DECB'])

            if STOP == 'stats':
                raise StopBuild()
            P.mark(mode + str(T) + ':Bmlstm')
            def seg_of(tok):
                for (slot, t0, t1) in segs:
                    if t0 <= tok < t1:
                        return slot, t0, t1
                raise AssertionError

            def load_slab(view, col0, ncols, wkeys):
                si, so_, skey = ring_slot()
                slab = A.view(so_, [KC, ncols], BF16)
                dma('sp', slab, view[:, :, col0:col0 + ncols], wkeys, [skey])
                return slab, skey

            def proj_fm(dst, slab, skey, c0, scale, dkey):
                pb = 2 + (ring_i[0] + c0 // 128) % 2
                mm(ps[pb][:, 0:T], [(slab[:, kc, c0:c0 + 128], h1T[:, kc, :]) for kc in range(KC)], [skey] + hkeys, [psk(pb)])
                act(dst, ps[pb][:, 0:T], AF.Copy, [psk(pb)], [dkey], scale=scale)

            def proj_tm(slab, skey, blk, pb):
                mm(ps[pb][:, 0:256], [(h1T[:, kc, blk * 128:(blk + 1) * 128], slab[:, kc, :]) for kc in range(KC)],
                   [skey] + hkeys, [psk(pb)])

            for hp in range(4):
                heads = (2 * hp, 2 * hp + 1)
                if full:
                    slab, skey = load_slab(wbin_v, Q0 + 256 * hp, 256, k_win)
                    for hh in range(2):
                        proj_fm(qT[hh], slab, skey, hh * 128, 1.0, ('qT', hh))
                slab, skey = load_slab(wbin_v, K0 + 256 * hp, 256, k_win)
                for hh in range(2):
                    proj_fm(kT[hh], slab, skey, hh * 128, 128.0 ** -0.5, ('kT', hh))
                    for blk in range(NBLK):
                        pbt = ps[blk % 2].bitcast(BF16)
                        tr(pbt[:, 0:128], kT[hh][:, blk * 128:(blk + 1) * 128], identB, [('kT', hh), 'identB'], [psk(blk % 2)])
                        ts('dve', kw[hh][:, blk, :], pbt[:, 0:128], COLS[:, blk, 80 + heads[hh]:81 + heads[hh]], None, ALU.mult, None,
                           [psk(blk % 2), 'COLS'], [('kw', hh)])
                for hh in range(2):
                    slab, skey = load_slab(wbin_v, V0 + 256 * heads[hh], 256, k_win)
                    mset('dve', vext[hh][:, :, 256:257], 1.0, [('vext', hh)])
                    for blk in range(NBLK):
                        pb = 2 + blk % 2
                        proj_tm(slab, skey, blk, pb)
                        act(vext[hh][:, blk, 0:256], ps[pb][:, 0:256], AF.Copy, [psk(pb)], [('vext', hh)])
                if full:
                    for hh in range(2):
                        slab, skey = load_slab(wbin_v, O0 + 256 * heads[hh], 256, k_win)
                        for blk in range(NBLK):
                            pb = 2 + blk % 2
                            proj_tm(slab, skey, blk, pb)
                            act(T1[blk % 2], ps[pb][:, 0:256], AF.Exp, [psk(pb)], [('T1', blk % 2)], scale=-1.0)
                            ts('dve', T1[blk % 2], T1[blk % 2], 1.0, None, ALU.add, None, [('T1', blk % 2)], [('T1', blk % 2)])
                            recip(so[hh][:, blk, :], T1[blk % 2], [('T1', blk % 2)], [('so', hh)])
                    dma('sp', gainb, bcast_ap(hng_h, heads[0] * 256, 128, 512), (), ['gainb'])
                for hh in range(2):
                    h = heads[hh]
                    ck = ('Cst', h)
                    cbk = ('Cb', h)
                    grh = S_GRH[hh]
                    ts('dve', grh, S_GRB, ccols[0:40, 3 + h:4 + h], None, ALU.mult, None, [('st', 7), 'ccols'], [('grh', hh)])
                    cur_slot = None
                    for c in range(NCH):
                        slot, t0, t1 = seg_of(c * 64)
                        blk, half = c // 2, c % 2
                        p0 = half * 64
                        cs = slice(c * 64, (c + 1) * 64)
                        if slot != cur_slot:
                            cur_slot = slot
                            if slot != 0:
                                q_ = slot - 1
                                dma('sp', Cst[:, h, 0:256], isC_h.ap()[q_, h, :, :], (), [ck])
                                dma('sp', Cst[:, h, 256:257], isn_h.ap()[q_, h:h + 1, :].rearrange('a d -> d a'), (), [ck], nonc=True)
                            elif mode == 'main' and c == 0 and tile_first[0]:
                                ts('dve', Cst[:, h, :], Cst[:, h, :], flagc[:, 0:1], None, ALU.mult, None, [ck, 'flagc'], [ck])
                            cp('act', Cb[:, h, :], Cst[:, h, :], [ck], [cbk])
                        if full:
                            pS = ps[6]
                            mm(pS[p0:p0 + 64, 0:64], [(kT[hh][:, cs], qT[hh][:, cs])], [('kT', hh), ('qT', hh)], [psk(6)])
                            mm(ps[7][p0:p0 + 64, 320:384], [(S_GL[:, cs], grh[:, cs]), (identF[0:64, 0:64], NEGM)],
                               [('st', 6), ('grh', hh), 'identF', 'NEGM'], [psk(7)])
                            wt = WT[c % 2]; swt = SWT[c % 2]
                            act(wt[p0:p0 + 64, :], ps[7][p0:p0 + 64, 320:384], AF.Exp, [psk(7)], [('WT', c % 2)])
                            tt('dve', swt[p0:p0 + 64, :], pS[p0:p0 + 64, 0:64], wt[p0:p0 + 64, :], ALU.mult,
                               [psk(6), ('WT', c % 2)], [('SWT', c % 2)])
                            mm(ps[7][p0:p0 + 64, 0:257], [(swt[p0:p0 + 64, :], vext[hh][p0:p0 + 64, blk, :])],
                               [('SWT', c % 2), ('vext', hh)], [psk(7)])
                            mm(ps[5][p0:p0 + 64, 0:257], [(qT[hh][:, cs], Cb[:, h, :])], [('qT', hh), cbk], [psk(5)])
                            ht = HT[c % 2]; nd = ND[c % 2]; hn = HN[c % 2]; t1_ = T1[c % 2]
                            act(ht[p0:p0 + 64, :], ps[5][p0:p0 + 64, 0:257], AF.Identity, [psk(5), 'COLS'], [('HT', c % 2)],
                                scale=COLS[p0:p0 + 64, blk, h:h + 1])
                            tt('dve', nd[p0:p0 + 64, :], ht[p0:p0 + 64, :], ps[7][p0:p0 + 64, 0:257], ALU.add,
                               [('HT', c % 2), psk(7)], [('ND', c % 2)])
                            dcol = sc4[p0:p0 + 64, 4 + (c % 2):5 + (c % 2)]
                            act(dcol, nd[p0:p0 + 64, 256:257], AF.Abs, [('ND', c % 2)], [('dcol', c % 2)])
                            ts('dve', dcol, dcol, COLS[p0:p0 + 64, blk, 40 + h:41 + h], None, ALU.max, None,
                               [('dcol', c % 2), 'COLS'], [('dcol', c % 2)])
                            recip(dcol, dcol, [('dcol', c % 2)], [('dcol', c % 2)])
                            ts('dve', hn[p0:p0 + 64, :], nd[p0:p0 + 64, 0:256], dcol, None, ALU.mult, None,
                               [('ND', c % 2), ('dcol', c % 2)], [('HN', c % 2)])
                            rcol = sc4[p0:p0 + 64, 6 + (c % 2):7 + (c % 2)]
                            act(t1_[p0:p0 + 64, :], hn[p0:p0 + 64, :], AF.Square, [('HN', c % 2)], [('T1', c % 2)])
                            rsum(rcol, t1_[p0:p0 + 64, :], [('T1', c % 2)], [('rcol', c % 2)])
                            ts('dve', rcol, rcol, 1.0 / 256, EPS, ALU.mult, ALU.add, [('rcol', c % 2)], [('rcol', c % 2)])
                            act(rcol, rcol, AF.Ln, [('rcol', c % 2)], [('rcol', c % 2)])
                            act(rcol, rcol, AF.Exp, [('rcol', c % 2)], [('rcol', c % 2)], scale=-0.5)
                            stt('dve', t1_[p0:p0 + 64, :], hn[p0:p0 + 64, :], rcol, gainb[p0:p0 + 64, hh, :], ALU.mult, ALU.mult,
                                [('HN', c % 2), ('rcol', c % 2), 'gainb'], [('T1', c % 2)])
                            tt('pool', yatok[p0:p0 + 64, blk, :], t1_[p0:p0 + 64, :], so[hh][p0:p0 + 64, blk, :], ALU.mult,
                               [('T1', c % 2), ('so', hh)], [('yatok', blk)])
                        mm(ps[4][:, 0:257], [(kw[hh][p0:p0 + 64, blk, :], vext[hh][p0:p0 + 64, blk, :])],
                           [('kw', hh), ('vext', hh)], [psk(4)])
                        stt('dve', Cst[:, h, :], Cst[:, h, :], DECB[:, h, c:c + 1], ps[4][:, 0:257], ALU.mult, ALU.add,
                            [ck, 'DECB', psk(4)], [ck])
                        last_of_seg = ((c + 1) * 64 == t1)
                        if not last_of_seg:
                            cp('act', Cb[:, h, :], Cst[:, h, :], [ck], [cbk])
                        elif slot != 0:
                            q_ = slot - 1
                            dma('sp', osC_h.ap()[q_, h, :, :], Cst[:, h, 0:256], [ck], [], final=True)
                            dma('sp', osn_h.ap()[q_, h:h + 1, :].rearrange('a d -> d a'), Cst[:, h, 256:257], [ck], [], final=True, nonc=True)
                        if full and half == 1:
                            for e2 in range(2):
                                pbt = ps[e2].bitcast(BF16)
                                tr(pbt[:, 0:128], yatok[:, blk, e2 * 128:(e2 + 1) * 128], identB, [('yatok', blk), 'identB'], [psk(e2)])
                                cp('act', yT[:, 2 * h + e2, blk * 128:(blk + 1) * 128], pbt[:, 0:128], [psk(e2)], [('yT', 2 * h + e2)])

            if STOP == 'mlstm':
                raise StopBuild()
            P.mark(mode + str(T) + ':Blru')
            lo_ = [R_X]

            def lal(shape, dt, parts=128):
                sz = 2 if dt == BF16 else 4
                n = 1
                for s in shape:
                    n *= s
                off = lo_[0]
                lo_[0] += (n * sz + 31) // 32 * 32
                assert lo_[0] <= R_X + 65536
                return A.view(off, shape, dt, parts=parts)
            WRI = lal([2, 16, 128], BF16)
            XBH = [lal([3 + T], F32) for _ in range(2)]
            XC = [lal([T], F32) for _ in range(2)]
            XCb = [lal([T], BF16) for _ in range(2)]
            RG = [lal([T], F32) for _ in range(2)]
            IGt = [lal([T], F32) for _ in range(2)]
            AAt = [lal([T], F32) for _ in range(2)]
            A2t = [lal([T], F32) for _ in range(2)]
            HH = [lal([T], F32) for _ in range(2)]
            GX = [lal([T], F32) for _ in range(2)]
            GW = [lal([T], F32) for _ in range(2)]
            P.alias(KEYS_L, KEYS_B)
            dma('sp', WRI[:, 0, :, :], wbr_h.ap().rearrange('n i j -> i n j'), ['wbr'], ['WRI'])
            dma('sp', WRI[:, 1, :, :], wbi_h.ap().rearrange('n i j -> i n j'), ['wbr'], ['WRI'])
            has_sample = any(slot != 0 for (slot, _a, _b) in segs)
            if has_sample:
                for (slot, t0, t1) in segs:
                    q_ = slot - 1
                    for j in range(3):
                        dma('sp', SCV[:, q_, :, j], iscv_h.ap()[q_, j, :].rearrange('(b p) -> p b', p=128), (), ['SCV'], nonc=True)
                    dma('sp', SH[:, q_, :], ish_h.ap()[q_, :].rearrange('(b p) -> p b', p=128), (), ['SH'], nonc=True)
            lslab = {}

            def lru_proj(nb):
                sp_, b2 = nb // 2, nb % 2
                u = nb % 2
                K = lambda n_: (n_, u)
                if b2 == 0:
                    lslab['x'] = load_slab(wbin_v, XB0 + 256 * sp_, 256, k_win)
                    if full:
                        lslab['g'] = load_slab(wbin_v, GB0 + 256 * sp_, 256, k_win)
                slabx, skx = lslab['x']
                pb = 2 + u
                mm(ps[pb][:, 0:T], [(slabx[:, kc, b2 * 128:(b2 + 1) * 128], h1T[:, kc, :]) for kc in range(KC)],
                   [skx] + hkeys, [psk(pb)])
                act(XBH[u][:, 3:3 + T], ps[pb][:, 0:T], AF.Copy, [psk(pb)], [K('XBH')])
                for (slot, t0, t1) in segs:
                    n = t1 - t0
                    if slot != 0:
                        q_ = slot - 1
                        cp('dve', convbuf[:, nb, :], SCV[:, q_, nb, :], ['SCV'], [('convbuf', nb)])
                    elif mode == 'main' and tile_first[0]:
                        ts('dve', convbuf[:, nb, :], convbuf[:, nb, :], flagc[:, 0:1], None, ALU.mult, None, [('convbuf', nb), 'flagc'], [('convbuf', nb)])
                        ts('dve', hst[:, nb:nb + 1], hst[:, nb:nb + 1], flagc[:, 0:1], None, ALU.mult, None, [('hst', nb), 'flagc'], [('hst', nb)])
                    if t0 == 0:
                        cp('dve', XBH[u][:, 0:3], convbuf[:, nb, :], [('convbuf', nb)], [K('XBH')])
                        xp = XBH[u]
                        xk = K('XBH')
                    else:
                        xp = GW[u]
                        xk = K('GW')
                        cp('dve', xp[:, 0:3], convbuf[:, nb, :], [('convbuf', nb)], [K('GW')])
                        cp('dve', xp[:, 3:3 + n], XBH[u][:, 3 + t0:3 + t1], [K('XBH')], [K('GW')])
                    xcs = XC[u][:, t0:t1]
                    ts('dve', xcs, xp[:, 0:n], lruc[:, nb, 2:3], lruc[:, nb, 6:7], ALU.mult, ALU.add, [xk, 'lruc'], [K('XC')])
                    for j in range(1, 4):
                        stt('dve', xcs, xp[:, j:j + n], lruc[:, nb, 2 + j:3 + j], xcs, ALU.mult, ALU.add,
                            [xk, 'lruc', K('XC')], [K('XC')])
                    if slot != 0:
                        cp('dve', SCVO[:, slot - 1, nb, :], XBH[u][:, t1:t1 + 3], [K('XBH')], ['SCVO'])
                    else:
                        cp('dve', convbuf[:, nb, :], XBH[u][:, t1:t1 + 3], [K('XBH')], [('convbuf', nb)])
                cp('act', XCb[u], XC[u], [K('XC')], [K('XCb')])
                if full:
                    slabg, skg = lslab['g']
                    pq = 6 + u
                    mm(ps[pq][:, 0:T], [(slabg[:, kc, b2 * 128:(b2 + 1) * 128], h1T[:, kc, :]) for kc in range(KC)],
                       [skg] + hkeys, [psk(pq)])
                    act(GX[u], ps[pq][:, 0:T], AF.Copy, [psk(pq)], [K('GX')])
                    act(GW[u], ps[pq][:, 0:T], AF.Square, [psk(pq), K('GW')], [K('GW')])
                    ts('pool', GW[u], GW[u], 0.044715, 1.0, ALU.mult, ALU.add, [K('GW')], [K('GW')])
                    tt('pool', GW[u], GW[u], GX[u], ALU.mult, [K('GW'), K('GX')], [K('GW')])
                    act(GW[u], GW[u], AF.Exp, [K('GW')], [K('GW')], scale=-1.5957691216057308)
                    ts('pool', GW[u], GW[u], 1.0, None, ALU.add, None, [K('GW')], [K('GW')])
                    recip(GW[u], GW[u], [K('GW')], [K('GW')])
                    tt('pool', GW[u], GW[u], GX[u], ALU.mult, [K('GW'), K('GX')], [K('GW')])

            def lru_rest(nb):
                u = nb % 2
                K = lambda n_: (n_, u)
                for gi, (dst, bcol) in enumerate(((RG[u], 7), (IGt[u], 8))):
                    pg = 4 + gi
                    mm(ps[pg][:, 0:T], [(WRI[:, gi, nb, :], XCb[u])], ['WRI', K('XCb')], [psk(pg)])
                    ts('dve', dst, ps[pg][:, 0:T], lruc[:, nb, bcol:bcol + 1], None, ALU.add, None, [psk(pg), 'lruc'], [K('g%d' % gi)])
                    act(dst, dst, AF.Exp, [K('g%d' % gi)], [K('g%d' % gi)], scale=-1.0)
                    ts('dve', dst, dst, 1.0, None, ALU.add, None, [K('g%d' % gi)], [K('g%d' % gi)])
                    recip(dst, dst, [K('g%d' % gi)], [K('g%d' % gi)])
                act(AAt[u], RG[u], AF.Exp, [K('g0'), 'lruc'], [K('AA')], scale=lruc[:, nb, 0:1])
                act(A2t[u], RG[u], AF.Exp, [K('g0'), 'lruc'], [K('A2')], scale=lruc[:, nb, 1:2])
                ts('dve', A2t[u], A2t[u], -1.0, 1.0, ALU.mult, ALU.add, [K('A2')], [K('A2')])
                ts('dve', A2t[u], A2t[u], 1e-18, None, ALU.max, None, [K('A2')], [K('A2')])
                act(A2t[u], A2t[u], AF.Ln, [K('A2')], [K('A2')])
                act(A2t[u], A2t[u], AF.Exp, [K('A2')], [K('A2')], scale=0.5)
                tt('pool', IGt[u], IGt[u], XC[u], ALU.mult, [K('g1'), K('XC')], [K('g1')])
                tt('pool', A2t[u], A2t[u], IGt[u], ALU.mult, [K('A2'), K('g1')], [K('A2')])
                for (slot, t0, t1) in segs:
                    if slot != 0:
                        cp('dve', hst[:, nb:nb + 1], SH[:, slot - 1, nb:nb + 1], ['SH'], [('hst', nb)])
                    P.op('dve', lambda e, u=u, nb=nb, t0=t0, t1=t1: e.tensor_tensor_scan(
                        HH[u][:, t0:t1], AAt[u][:, t0:t1], A2t[u][:, t0:t1], hst[:, nb:nb + 1], ALU.mult, ALU.add),
                        [K('AA'), K('A2'), ('hst', nb)], [K('HH')])
                    if slot != 0:
                        cp('dve', SHO[:, slot - 1, nb:nb + 1], HH[u][:, t1 - 1:t1], [K('HH')], ['SHO'])
                    else:
                        cp('dve', hst[:, nb:nb + 1], HH[u][:, t1 - 1:t1], [K('HH')], [('hst', nb)])
                if full:
                    tt('dve', yT[:, 16 + nb, :], GW[u], HH[u], ALU.mult, [K('GW'), K('HH')], [('yT', 16 + nb)])

            lru_proj(0)
            for nb in range(1, 16):
                lru_proj(nb)
                lru_rest(nb - 1)
            lru_rest(15)
            if has_sample:
                for (slot, t0, t1) in segs:
                    q_ = slot - 1
                    for j in range(3):
                        dma('sp', oscv_h.ap()[q_, j, :].rearrange('(b p) -> p b', p=128), SCVO[:, q_, :, j], ['SCVO'], [], final=True, nonc=True)
                    dma('sp', osh_h.ap()[q_, :].rearrange('(b p) -> p b', p=128), SHO[:, q_, :], ['SHO'], [], final=True, nonc=True)
            if not full:
                return

            if STOP == 'lru':
                raise StopBuild()
            P.mark(mode + str(T) + ':C')
            x1 = A.view(R_X, [NBLK, D], F32)
            P.alias(KEYS_X1, KEYS_L + KEYS_B + KEYS_A)
            for blk in range(NBLK):
                dma('sp', x1[:, blk, :], x_src[blk * 128:(blk + 1) * 128, :], (), [('x1', blk)])
            G1 = [A.view(TMP + 8192 + i * 1024, [256], F32) for i in range(2)]
            CT = [A.view(TMP + 8192 + 2048 + i * 1024, [256], F32) for i in range(2)]
            for cg in range(16):
                slab, skey = load_slab(wbout_v, cg * 256, 256, k_wout)
                g1 = G1[cg % 2]
                for (slot, t0, t1) in segs:
                    pa, pb_ = (0, 128) if T == 512 else (t0, t1)
                    dma('sp', g1[pa:pb_, :], bcast_ap(ada_h, slot * 6 * D + 2 * D + cg * 256, pb_ - pa, 256), ['ada_dram'], [('G1', cg % 2)])
                for blk in range(NBLK):
                    pb = 2 + (cg * NBLK + blk) % 4
                    mm(ps[pb][:, 0:256], [(yT[:, kc, blk * 128:(blk + 1) * 128], slab[:, kc, :]) for kc in range(KC)],
                       [skey] + ykeys, [psk(pb)])
                    ct = CT[blk % 2]
                    tt('dve', ct, ps[pb][:, 0:256], g1, ALU.mult, [psk(pb), ('G1', cg % 2)], [('CT', blk % 2)])
                    tt('pool', x1[:, blk, cg * 256:(cg + 1) * 256], x1[:, blk, cg * 256:(cg + 1) * 256], ct, ALU.add,
                       [('CT', blk % 2), ('x1', blk)], [('x1', blk)])
            h2T = A.view(R_H, [KC, T], BF16)

            def srcC(blk):
                return x1[:, blk, :], ('x1', blk)
            P.alias([('cxn', 0), ('cxn', 1), 'csqj'], ykeys)
            norm_to_T(srcC, h2T, lambda s, j: MODC[:, s, 2, j:j + 1], lambda s, j: MODC[:, s, 3, j:j + 1], R_Y, R_Y + 16384, 'c')

            if STOP == 'C':
                raise StopBuild()
            P.mark(mode + str(T) + ':D')
            aT = [A.view(R_Y + 16384 + i * 8192, [8, T], BF16) for i in range(2)]
            G2B = A.view(R_Y, [D], F32)
            P.alias(['G2B', ('aT', 0), ('aT', 1)], [('cxn', 0), ('cxn', 1), 'csqj'] + ykeys)
            for (slot, t0, t1) in segs:
                pa, pb_ = (0, 128) if T == 512 else (t0, t1)
                dma('sp', G2B[pa:pb_, :], bcast_ap(ada_h, slot * 6 * D + 5 * D, pb_ - pa, D), ['ada_dram'], ['G2B'])
            EX = [A.view(TMP + i * 2048, [512], F32) for i in range(2)]
            DT_ = [A.view(TMP + 4096 + i * 2048, [512], F32) for i in range(2)]
            NG = (DFF + 1023) // 1024
            for g in range(NG):
                nchk = min(8, (DFF - g * 1024) // 128)
                at = aT[g % 2]
                ak = ('aT', g % 2)
                for ci in range(nchk):
                    f0 = g * 1024 + ci * 128
                    si, so_, skey = ring_slot()
                    slab = A.view(so_, [2, KC, 128], BF16)
                    dma('sp', slab[:, 0, :, :], wbgu_v[:, :, f0:f0 + 128], k_wgu, [skey])
                    dma('sp', slab[:, 1, :, :], wbgu_v[:, :, DFF + f0:DFF + f0 + 128], k_wgu, [skey])
                    u = ci % 2
                    pg, pu = 0 + 2 * u, 1 + 2 * u
                    mm(ps[pg][:, 0:T], [(slab[:, 0, kc, :], h2T[:, kc, :]) for kc in range(KC)], [skey] + hkeys, [psk(pg)])
                    mm(ps[pu][:, 0:T], [(slab[:, 1, kc, :], h2T[:, kc, :]) for kc in range(KC)], [skey] + hkeys, [psk(pu)])
                    ex = EX[u][:, 0:T]
                    act(ex, ps[pg][:, 0:T], AF.Exp, [psk(pg)], [('EX', u)], scale=-1.0)
                    ts('dve', ex, ex, 1.0, None, ALU.add, None, [('EX', u)], [('EX', u)])
                    recip(ex, ex, [('EX', u)], [('EX', u)])
                    tt('dve', ex, ex, ps[pg][:, 0:T], ALU.mult, [('EX', u), psk(pg)], [('EX', u)])
                    tt('dve', at[:, ci, :], ex, ps[pu][:, 0:T], ALU.mult, [('EX', u), psk(pu)], [ak])
                for c8 in range(8):
                    si, so_, skey = ring_slot()
                    slab = A.view(so_, [nchk, 512], BF16)
                    dma('sp', slab, wbdn_v[:, g * 8:g * 8 + nchk, c8 * 512:(c8 + 1) * 512], k_wdn, [skey])
                    for blk in range(NBLK):
                        pd = 4 + (c8 * NBLK + blk) % 4
                        mm(ps[pd][:, 0:512], [(at[:, ci, blk * 128:(blk + 1) * 128], slab[:, ci, :]) for ci in range(nchk)],
                           [skey, ak], [psk(pd)])
                        dt_ = DT_[blk % 2]
                        tt('dve', dt_, ps[pd][:, 0:512], G2B[:, c8 * 512:(c8 + 1) * 512], ALU.mult, [psk(pd), 'G2B'], [('DT', blk % 2)])
                        tt('pool', x1[:, blk, c8 * 512:(c8 + 1) * 512], x1[:, blk, c8 * 512:(c8 + 1) * 512], dt_, ALU.add,
                           [('DT', blk % 2), ('x1', blk)], [('x1', blk)])

            if STOP == 'D':
                raise StopBuild()
            P.mark(mode + str(T) + ':E')
            NFG = A.view(R_H, [D], F32)
            P.alias(['NFG', 'fjunk'], hkeys)
            dma('sp', NFG, bcast_ap(nfg_h, 0, 128, D), (), ['NFG'])
            junk = A.view(R_H + 16384, [D], F32)
            for blk in range(NBLK):
                ssq = smallc[:, 8 + blk:9 + blk]
                sk = ('fssq', blk)
                act(junk, x1[:, blk, :], AF.Square, [('x1', blk)], ['fjunk'])
                rsum(ssq, junk, ['fjunk'], [sk])
                ts('dve', ssq, ssq, 1.0 / D, EPS, ALU.mult, ALU.add, [sk], [sk])
                act(ssq, ssq, AF.Ln, [sk], [sk])
                act(ssq, ssq, AF.Exp, [sk], [sk], scale=-0.5)
                stt('dve', x1[:, blk, :], x1[:, blk, :], ssq, NFG, ALU.mult, ALU.mult, [('x1', blk), sk, 'NFG'], [('x1', blk)])
                dma('sp', y_dst[blk * 128:(blk + 1) * 128, :], x1[:, blk, :], [('x1', blk)], [], final=True)

        for i in range(NPRE):
            run_tile(xpre_h.ap()[i * 512:(i + 1) * 512, :], 512, [(0, 0, 512)], 'prefix')
        for i in range(NMAIN):
            tile_first[0] = (i == 0)
            run_tile(xmain_h.ap()[i * 512:(i + 1) * 512, :], 512, [(0, 0, 512)], 'main', yp_h.ap()[i * 512:(i + 1) * 512, :])
        tile_first[0] = False
        for h in range(8):
            dma('sp', opC_h.ap()[h, :, :], Cst[:, h, 0:256], [('Cst', h)], [], final=True)
            dma('sp', opn_h.ap()[h:h + 1, :].rearrange('a d -> d a'), Cst[:, h, 256:257], [('Cst', h)], [], final=True, nonc=True)
        dma('sp', opm_h.ap().rearrange('a h -> h a'), mst[0:8, :], ['mst'], [], final=True, nonc=True)
        dma('sp', oph_h.ap().rearrange('(b p) -> p b', p=128), hst, [('hst', nb) for nb in range(16)], [], final=True, nonc=True)
        for nb in range(16):
            dma('sp', opcv_h.ap()[:, nb * 128:(nb + 1) * 128].rearrange('j p -> p j'), convbuf[:, nb, :], [('convbuf', nb)], [], final=True, nonc=True)
        if DO_SAMPLE:
            run_tile(xs_h.ap(), 128, [(1, 0, 64), (2, 64, 128)], 'main', ys_h.ap())


    try:
        body()
    except StopBuild:
        pass
    P.build()
    LAST_MARKS[:] = P.marks
    return nc


def make_consts():
    c = np.zeros((128, 1024), np.float32)
    c[:, 0:128] = np.eye(128, dtype=np.float32)
    s = np.arange(64)[:, None]; t = np.arange(64)[None, :]
    c[0:64, 128:192] = np.where(s <= t, 0.0, NEGBIG)
    m = np.ones((512,), np.float32); m[::64] = 0.0
    c[0:40, 192:704] = m[None, :]
    cc = np.zeros((128, 32), np.float32)
    cc[0:8, 0] = 1.0
    cc[32:40, 1] = 1.0
    cc[32:40, 2] = -1.0
    for h in range(8):
        cc[h, 3 + h] = 1.0; cc[32 + h, 3 + h] = 1.0
        cc[h, 11 + h] = 1.0
    c[:, 704:736] = cc
    return c


_NC_CACHE = {}


def _get_nc(cfg_key, cfg):
    if cfg_key not in _NC_CACHE:
        _NC_CACHE[cfg_key] = build_program(cfg)
    return _NC_CACHE[cfg_key]


def make_in_maps(inp, cores=range(8)):
    consts = make_consts()
    maps = []
    f = lambda a: np.ascontiguousarray(a, dtype=np.float32)
    for c in cores:
        b, half = c // 2, c % 2
        sq = [2 * c, 2 * c + 1]
        m = {
            'xpre': f(inp['x_prompt'][b, 0:2048]),
            'xmain': f(inp['x_prompt'][b, half * 2048:(half + 1) * 2048]),
            'xs': f(inp['x_sample'][sq].reshape(128, D)),
            'c3': f(np.concatenate([inp['c_prompt'][b:b + 1], inp['c_sample'][sq]], 0)),
            'flag': np.full((128, 1), float(half), np.float32),
            'isC': f(inp['state_mlstm_C'][0, sq]), 'isn': f(inp['state_mlstm_n'][0, sq]),
            'ism': f(inp['state_mlstm_m'][0, sq]), 'ish': f(inp['state_lru_h'][0, sq]),
            'iscv': f(inp['state_conv'][0, sq]),
            'w_ada': f(inp['w_ada'][0]), 'b_ada': f(inp['b_ada'][0]),
            'norm1_g': f(inp['norm1_g'][0]), 'norm2_g': f(inp['norm2_g'][0]),
            'w_in': f(inp['w_in'][0]), 'b_gates_a': f(inp['b_gates_a'][0]), 'head_norm_g': f(inp['head_norm_g'][0]),
            'conv_w': f(inp['conv_w'][0]), 'conv_b': f(inp['conv_b'][0]),
            'w_r': f(inp['w_r'][0]), 'b_r': f(inp['b_r'][0]), 'w_i': f(inp['w_i'][0]), 'b_i': f(inp['b_i'][0]),
            'lru_lambda': f(inp['lru_lambda'][0]),
            'w_out': f(inp['w_out'][0]), 'w_gu': f(inp['w_gu'][0]), 'w_down': f(inp['w_down'][0]),
            'normf_g': f(inp['normf_g']), 'consts': consts,
        }
        maps.append(m)
    return maps


def kernel(**inp):
    inp = {k: np.asarray(v) for k, v in inp.items()}
    nc = _get_nc('full', {})
    maps = make_in_maps(inp)
    res = run_bass_kernel_spmd(nc, maps, core_ids=list(range(8)))
    R = res.results
    y_prompt = np.zeros((4, 4096, D), np.float32)
    y_sample = np.zeros((16, 64, D), np.float32)
    p_C = np.zeros((1, 4, 8, 128, 256), np.float32); p_n = np.zeros((1, 4, 8, 128), np.float32)
    p_m = np.zeros((1, 4, 8), np.float32); p_h = np.zeros((1, 4, 2048), np.float32); p_conv = np.zeros((1, 4, 3, 2048), np.float32)
    s_C = np.zeros((1, 16, 8, 128, 256), np.float32); s_n = np.zeros((1, 16, 8, 128), np.float32)
    s_m = np.zeros((1, 16, 8), np.float32); s_h = np.zeros((1, 16, 2048), np.float32); s_conv = np.zeros((1, 16, 3, 2048), np.float32)
    for c in range(8):
        b, half = c // 2, c % 2
        r = R[c]
        y_prompt[b, half * 2048:(half + 1) * 2048] = r['yp']
        y_sample[2 * c:2 * c + 2] = r['ys'].reshape(2, 64, D)
        if half == 1:
            p_C[0, b] = r['opC']; p_n[0, b] = r['opn']; p_m[0, b] = r['opm'].reshape(8)
            p_h[0, b] = r['oph']; p_conv[0, b] = r['opcv']
        s_C[0, 2 * c:2 * c + 2] = r['osC']; s_n[0, 2 * c:2 * c + 2] = r['osn']; s_m[0, 2 * c:2 * c + 2] = r['osm']
        s_h[0, 2 * c:2 * c + 2] = r['osh']; s_conv[0, 2 * c:2 * c + 2] = r['oscv']
    return (y_prompt, y_sample, p_C, p_n, p_m, p_h, p_conv, s_C, s_n, s_m, s_h, s_conv)
```

```python
from contextlib import ExitStack
import numpy as np
import concourse.bass as bass
import concourse.mybir as mybir
from concourse.bass_utils import run_bass_kernel_spmd

F32 = mybir.dt.float32
BF16 = mybir.dt.bfloat16
U8 = mybir.dt.uint8
AF = mybir.ActivationFunctionType
ALU = mybir.AluOpType
AX = mybir.AxisListType

D = 4096
KC = 32
IN_COLS = 10256
DFF = 11008
EPS = 1e-6
NEGBIG = -1.0e9
Q0, K0, V0, O0, IG0, FG0, XB0, GB0 = 0, 1024, 2048, 4096, 6144, 6152, 6160, 8208


class Prog:
    E = ['pe', 'act', 'dve', 'pool', 'sp']

    def __init__(self, nc, n_dma_sems=24):
        self.nc = nc
        self.ops = {e: [] for e in self.E}
        self.cnt = {e: 0 for e in self.E}
        self.known = {e: {} for e in self.E}
        self.buf = {}
        self.nd = n_dma_sems
        self.dcnt = [0] * n_dma_sems
        self.drr = 0
        self.nsw = 6
        self.swrr = 0
        self.final = []
        self.marks = []

    def _deps(self, reads, writes):
        deps = {}

        def add(s, v):
            if deps.get(s, 0) < v:
                deps[s] = v
        for k in reads:
            b = self.buf.get(k)
            if b and b['w']:
                add(*b['w'])
        for k in writes:
            b = self.buf.get(k)
            if b:
                if b['w']:
                    add(*b['w'])
                for s, v in b['r'].items():
                    add(s, v)
        return deps

    def _wait(self, e, deps):
        kn = self.known[e]
        for s, v in deps.items():
            if s == 'pe' and e == 'pe':
                continue
            if kn.get(s, 0) < v:
                self.ops[e].append(('w', s, v))
                kn[s] = v

    def _mark(self, reads, writes, tag):
        s, v = tag
        for k in reads:
            b = self.buf.setdefault(k, {'w': None, 'r': {}})
            if b['r'].get(s, 0) < v:
                b['r'][s] = v
        for k in writes:
            self.buf[k] = {'w': tag, 'r': {}}

    @staticmethod
    def _excl(reads, writes):
        ps_r = [k for k in reads if isinstance(k, tuple) and k[0] == 'ps']
        if ps_r:
            reads = [k for k in reads if not (isinstance(k, tuple) and k[0] == 'ps')]
            writes = list(writes) + [k for k in ps_r if k not in writes]
        return reads, writes

    def op(self, e, fn, reads=(), writes=()):
        reads, writes = self._excl(reads, writes)
        self._wait(e, self._deps(reads, writes))
        self.cnt[e] += 1
        self.ops[e].append(('o', fn, e, 1))
        self._mark(reads, writes, (e, self.cnt[e]))

    def dma(self, q, fn, reads=(), writes=(), final=False):
        if q == 'pool':
            i = self.nd - self.nsw + self.swrr
            self.swrr = (self.swrr + 1) % self.nsw
        else:
            i = self.drr
            self.drr = (self.drr + 1) % (self.nd - self.nsw)
        s = 'd%d' % i
        deps = self._deps(reads, writes)
        if self.dcnt[i] > 0:
            deps[s] = max(deps.get(s, 0), self.dcnt[i])
        self._wait(q, deps)
        self.dcnt[i] += 16
        self.ops[q].append(('o', fn, s, 16))
        self._mark(reads, writes, (s, self.dcnt[i]))
        if final:
            self.final.append((s, self.dcnt[i]))

    def mark(self, label):
        for e in self.E:
            self.ops[e].append(('m', label))

    def alias(self, new_keys, old_keys):
        r = {}
        for ok in old_keys:
            b = self.buf.get(ok)
            if not b:
                continue
            if b['w']:
                r[b['w'][0]] = max(r.get(b['w'][0], 0), b['w'][1])
            for s_, v_ in b['r'].items():
                r[s_] = max(r.get(s_, 0), v_)
        for nk in new_keys:
            nb = self.buf.setdefault(nk, {'w': None, 'r': {}})
            for s_, v_ in r.items():
                nb['r'][s_] = max(nb['r'].get(s_, 0), v_)

    def build(self):
        nc = self.nc
        for s, v in self.final:
            self._wait('sp', {s: v})
        for i in range(self.nd):
            if self.dcnt[i] > 0:
                self._wait('sp', {'d%d' % i: self.dcnt[i]})
        with ExitStack() as st:
            sems = {}
            for name in self.E + ['d%d' % i for i in range(self.nd)]:
                sems[name] = st.enter_context(nc.semaphore('s_' + name))
            st.enter_context(nc.allow_low_precision(reason="bf16 matmul operands, fp32 accumulation"))
            block = st.enter_context(nc.Block())

            def run(eng, lst):
                for o in lst:
                    if o[0] == 'm':
                        self.marks.append((str(eng), o[1], nc.get_next_instruction_name()))
                        continue
                    if o[0] == 'w':
                        eng.wait_ge(sems[o[1]], o[2])
                    else:
                        inst = o[1](eng)
                        inst.then_inc(sems[o[2]], o[3])

            @block.tensor
            def _(t):
                run(t, self.ops['pe'])

            @block.scalar
            def _(t):
                run(t, self.ops['act'])

            @block.vector
            def _(t):
                run(t, self.ops['dve'])

            @block.gpsimd
            def _(t):
                run(t, self.ops['pool'])

            @block.sync
            def _(t):
                run(t, self.ops['sp'])


LAST_MARKS = []


class StopBuild(Exception):
    pass


class Arena:
    def __init__(self, nc, nbytes):
        self.t = nc.alloc_sbuf_tensor('arena', [128, nbytes], U8)
        self.ap = self.t.ap()
        self.nbytes = nbytes

    def view(self, off, shape, dt, parts=128, p0=0):
        sz = 2 if dt == BF16 else 4
        n = 1
        for s in shape:
            n *= s
        assert off % 4 == 0 and off + n * sz <= self.nbytes, (off, shape)
        v = self.ap[p0:p0 + parts, off:off + n * sz].bitcast(dt)
        if len(shape) == 2:
            v = v.rearrange('p (a b) -> p a b', a=shape[0])
        elif len(shape) == 3:
            v = v.rearrange('p (a b c) -> p a b c', a=shape[0], b=shape[1])
        return v


def bcast_ap(handle, offset, nparts, n):
    return bass.AP(handle, offset, [[0, nparts], [1, n]])


def build_program(cfg):
    NPRE = cfg.get('npre', 4)
    NMAIN = cfg.get('nmain', 4)
    DO_SAMPLE = cfg.get('sample', True)
    nc = bass.Bass("TRN2", target_bir_lowering=False)
    P = Prog(nc)
    STOP = cfg.get('stop')

    def din(name, shape):
        return nc.dram_tensor(name, list(shape), F32, kind="ExternalInput")

    def dout(name, shape):
        return nc.dram_tensor(name, list(shape), F32, kind="ExternalOutput")

    xpre_h = din('xpre', [2048, D]); xmain_h = din('xmain', [2048, D]); xs_h = din('xs', [128, D])
    c3_h = din('c3', [3, D]); flag_h = din('flag', [128, 1])
    isC_h = din('isC', [2, 8, 128, 256]); isn_h = din('isn', [2, 8, 128]); ism_h = din('ism', [2, 8])
    ish_h = din('ish', [2, 2048]); iscv_h = din('iscv', [2, 3, 2048])
    wada_h = din('w_ada', [D, 6 * D]); bada_h = din('b_ada', [6 * D])
    n1g_h = din('norm1_g', [D]); n2g_h = din('norm2_g', [D])
    win_h = din('w_in', [D, IN_COLS]); bg_h = din('b_gates_a', [2, 8]); hng_h = din('head_norm_g', [8, 256])
    cw_h = din('conv_w', [4, 2048]); cb_h = din('conv_b', [2048])
    wr_h = din('w_r', [16, 128, 128]); br_h = din('b_r', [2048]); wi_h = din('w_i', [16, 128, 128]); bi_h = din('b_i', [2048])
    lam_h = din('lru_lambda', [2048])
    wout_h = din('w_out', [D, D]); wgu_h = din('w_gu', [D, 2 * DFF]); wdn_h = din('w_down', [DFF, D])
    nfg_h = din('normf_g', [D]); consts_h = din('consts', [128, 1024])

    yp_h = dout('yp', [2048, D]); ys_h = dout('ys', [128, D])
    opC_h = dout('opC', [8, 128, 256]); opn_h = dout('opn', [8, 128]); opm_h = dout('opm', [1, 8])
    oph_h = dout('oph', [2048]); opcv_h = dout('opcv', [3, 2048])
    osC_h = dout('osC', [2, 8, 128, 256]); osn_h = dout('osn', [2, 8, 128]); osm_h = dout('osm', [2, 8])
    osh_h = dout('osh', [2, 2048]); oscv_h = dout('oscv', [2, 3, 2048])

    wbin_h = nc.dram_tensor('wbin', [D, IN_COLS], BF16, kind="Internal")
    wbout_h = nc.dram_tensor('wbout', [D, D], BF16, kind="Internal")
    wbgu_h = nc.dram_tensor('wbgu', [D, 2 * DFF], BF16, kind="Internal")
    wbdn_h = nc.dram_tensor('wbdn', [DFF, D], BF16, kind="Internal")
    ada_h = nc.dram_tensor('ada_s', [3, 6 * D], F32, kind="Internal")
    wbr_h = nc.dram_tensor('wbr', [16, 128, 128], BF16, kind="Internal")
    wgs_h = nc.dram_tensor('wgs', [128, 2 * KC * 40], BF16, kind="Internal")
    wbi_h = nc.dram_tensor('wbi', [16, 128, 128], BF16, kind="Internal")

    A = Arena(nc, 212480)
    RING, R_H, R_Y, R_X, PERS = 0, 49152, 81920, 114688, 180224
    NSLOT = 3
    ps = [nc.alloc_psum_tensor('ps%d' % i, [128, 512], F32).ap() for i in range(8)]

    def psk(i):
        return ('ps', i)

    po = [PERS]

    def pal(shape, dt, parts=128):
        sz = 2 if dt == BF16 else 4
        n = 1
        for s in shape:
            n *= s
        off = po[0]
        po[0] += (n * sz + 31) // 32 * 32
        return A.view(off, shape, dt, parts=parts)

    Cst = pal([8, 257], F32); Cb = pal([8, 257], BF16)
    identF = pal([128], F32); identB = pal([128], BF16)
    NEGM = pal([64], F32, parts=64)
    scanmask = pal([512], F32, parts=40)
    ones40 = pal([128], F32, parts=40)
    ccols = pal([32], F32)
    MODC = pal([3, 4, 32], F32)
    hst = pal([16], F32); convbuf = pal([16, 3], F32)
    lruc = pal([16, 10], F32)
    mst = pal([1], F32, parts=40)
    flagc = pal([1], F32)
    bgc = pal([2], F32, parts=40)
    smallc = pal([64], F32)
    TMP = po[0]
    TMPSZ = A.nbytes - TMP
    assert TMPSZ >= 12288 + 1024, TMPSZ
    SCV = A.view(TMP + 12288, [2, 16, 3], F32)
    SCVO = A.view(TMP + 12288 + 384, [2, 16, 3], F32)
    SH = A.view(TMP + 12288 + 768, [2, 16], F32)
    SHO = A.view(TMP + 12288 + 896, [2, 16], F32)

    consts = consts_h.ap()

    def mm(out_ap, pairs, reads, writes):
        def fn(pe, pairs=pairs, out_ap=out_ap):
            n = len(pairs)
            last = None
            for i, (l, r) in enumerate(pairs):
                last = pe.matmul(out_ap, l, r, start=(i == 0), stop=(i == n - 1))
            return last
        P.op('pe', fn, reads, writes)

    def tr(out_ap, in_ap, ident, reads, writes):
        P.op('pe', lambda pe: pe.transpose(out_ap, in_ap, ident), reads, writes)

    def act(out_ap, in_ap, func, reads, writes, bias=None, scale=None, accum=None):
        kw = {}
        if bias is not None:
            kw['bias'] = bias
        if scale is not None:
            kw['scale'] = scale
        if accum is not None:
            kw['accum_out'] = accum
        P.op('act', lambda e: e.activation(out_ap, in_ap, func, **kw), reads, writes)

    def ts(eng, out_ap, in_ap, s1, s2, op0, op1, reads, writes):
        if s2 is None and not isinstance(s1, (int, float)):
            P.op(eng, lambda e: e.tensor_scalar(out_ap, in_ap, s1, 0.0, op0, ALU.add), reads, writes)
        elif s2 is None:
            P.op(eng, lambda e: e.tensor_scalar(out_ap, in_ap, s1, None, op0), reads, writes)
        else:
            P.op(eng, lambda e: e.tensor_scalar(out_ap, in_ap, s1, s2, op0, op1), reads, writes)

    def tt(eng, out_ap, a, b, op, reads, writes):
        P.op(eng, lambda e: e.tensor_tensor(out_ap, a, b, op), reads, writes)

    def stt(eng, out_ap, in0, sc, in1, op0, op1, reads, writes):
        P.op(eng, lambda e: e.scalar_tensor_tensor(out_ap, in0, sc, in1, op0, op1), reads, writes)

    def cp(eng, out_ap, in_ap, reads, writes):
        if eng == 'act':
            P.op(eng, lambda e: e.activation(out_ap, in_ap, AF.Copy), reads, writes)
        else:
            P.op(eng, lambda e: e.tensor_copy(out_ap, in_ap), reads, writes)

    def rsum(out_ap, in_ap, reads, writes):
        P.op('dve', lambda e: e.reduce_sum(out_ap, in_ap, AX.X), reads, writes)

    def recip(out_ap, in_ap, reads, writes):
        P.op('dve', lambda e: e.reciprocal(out_ap, in_ap), reads, writes)

    def mset(eng, ap, val, writes):
        P.op(eng, lambda e: e.memset(ap, val), (), writes)

    def dma(q, out_ap, in_ap, reads, writes, final=False, nonc=False):
        if nonc:
            P.dma(q, lambda e: e.dma_start(out=out_ap, in_=in_ap, allow_slow_non_contiguous=True), reads, writes, final)
        else:
            P.dma(q, lambda e: e.dma_start(out=out_ap, in_=in_ap), reads, writes, final)

    ring_i = [0]

    def ring_slot():
        i = ring_i[0] % NSLOT
        ring_i[0] += 1
        return i, RING + i * 16384, ('ring', i)

    tile_first = [False]

    def body():
        dma('sp', identF, consts[:, 0:128], (), ['identF'])
        dma('sp', NEGM, consts[0:64, 128:192], (), ['NEGM'])
        dma('sp', scanmask, consts[0:40, 192:704], (), ['scanmask'])
        dma('sp', ccols, consts[:, 704:736], (), ['ccols'])
        dma('sp', flagc, flag_h.ap(), (), ['flagc'])
        cp('dve', identB, identF, ['identF'], ['identB'])
        mset('dve', ones40, 1.0, ['ones40'])
        mset('dve', bgc, 0.0, ['bgc'])
        bgap = bg_h.ap()
        for r0 in (0, 32):
            dma('sp', bgc[r0:r0 + 8, 0:1], bgap[0:1, :].rearrange('a h -> h a'), (), ['bgc'], nonc=True)
            dma('sp', bgc[r0:r0 + 8, 1:2], bgap[1:2, :].rearrange('a h -> h a'), (), ['bgc'], nonc=True)
        for col, h_ in ((6, cb_h), (7, br_h), (8, bi_h), (9, lam_h)):
            dma('sp', lruc[:, :, col], h_.ap().rearrange('(b p) -> p b', p=128), (), ['lruc'], nonc=True)
        for j in range(4):
            dma('sp', lruc[:, :, 2 + j], cw_h.ap()[j, :].rearrange('(b p) -> p b', p=128), (), ['lruc'], nonc=True)
        act(lruc[:, :, 0], lruc[:, :, 9], AF.Exp, ['lruc'], ['lruc'], scale=-1.0)
        act(lruc[:, :, 0], lruc[:, :, 0], AF.Ln, ['lruc'], ['lruc'], bias=1.0)
        ts('dve', lruc[:, :, 1], lruc[:, :, 0], -16.0, None, ALU.mult, None, ['lruc'], ['lruc'])
        ts('dve', lruc[:, :, 0], lruc[:, :, 0], -8.0, None, ALU.mult, None, ['lruc'], ['lruc'])

        if STOP == 'p0':
            raise StopBuild()
        P.mark('ada')
        XO = R_X
        c3t = A.view(XO, [D], F32, parts=3)
        c3e = A.view(XO + 16384, [D], F32, parts=3)
        scT = A.view(XO + 32768, [KC, 4], BF16)
        badat = A.view(XO + 36864, [D], F32, parts=3)
        adast = A.view(XO + 36864 + 16384, [D], F32, parts=3)
        dma('sp', c3t, c3_h.ap(), (), ['c3t'])
        act(c3e, c3t, AF.Exp, ['c3t'], ['c3e'], scale=-1.0)
        ts('dve', c3e, c3e, 1.0, None, ALU.add, None, ['c3e'], ['c3e'])
        recip(c3e, c3e, ['c3e'], ['c3e'])
        tt('dve', c3t, c3t, c3e, ALU.mult, ['c3t', 'c3e'], ['c3t'])
        for kc in range(KC):
            pst = ps[kc % 2]
            tr(pst[:, 0:3], c3t[:, kc * 128:(kc + 1) * 128], identF[0:3, 0:3], ['c3t', 'identF'], [psk(kc % 2)])
            cp('dve', scT[:, kc, 0:3], pst[:, 0:3], [psk(kc % 2)], ['scT'])
        wada_v = wada_h.ap().rearrange('(kc p) c -> p kc c', p=128)
        for part in range(6):
            dma('sp', badat, bcast_ap(bada_h, part * D, 3, D), (), ['badat'])
            for sg in range(16):
                si, so_, skey = ring_slot()
                slab = A.view(so_, [KC, 256], BF16)
                c0 = part * D + sg * 256
                dma('pool', slab, wada_v[:, :, c0:c0 + 256], (), [skey])
                pb = 2 + (sg % 2)
                mm(ps[pb][0:3, 0:256], [(scT[:, kc, 0:3], slab[:, kc, :]) for kc in range(KC)], ['scT', skey], [psk(pb)])
                tt('dve', adast[:, sg * 256:(sg + 1) * 256], ps[pb][0:3, 0:256], badat[:, sg * 256:(sg + 1) * 256], ALU.add,
                   [psk(pb), 'badat'], ['adast'])
            dma('sp', ada_h.ap()[:, part * D:(part + 1) * D], adast, ['adast'], ['ada_dram'])

        if STOP == 'ada':
            raise StopBuild()
        rowt = A.view(XO, [128], F32, parts=32)
        gcol = A.view(XO + 512, [2, 32], F32)
        for gi, gh in enumerate((n1g_h, n2g_h)):
            dma('sp', rowt, gh.ap().rearrange('(j p) -> j p', p=128), (), ['rowt'])
            tr(ps[0][:, 0:32], rowt, identF[0:32, 0:32], ['rowt', 'identF'], [psk(0)])
            cp('dve', gcol[:, gi, :], ps[0][:, 0:32], [psk(0)], ['gcol'])
        for slot in range(3):
            for (part, mi, gi) in ((1, 0, 0), (0, 1, None), (4, 2, 1), (3, 3, None)):
                dma('sp', rowt, ada_h.ap()[slot, part * D:(part + 1) * D].rearrange('(j p) -> j p', p=128), ['ada_dram'], ['rowt'])
                tr(ps[0][:, 0:32], rowt, identF[0:32, 0:32], ['rowt', 'identF'], [psk(0)])
                if gi is None:
                    cp('dve', MODC[:, slot, mi, :], ps[0][:, 0:32], [psk(0)], ['MODC'])
                else:
                    stt('dve', MODC[:, slot, mi, :], ps[0][:, 0:32], 1.0, gcol[:, gi, :], ALU.add, ALU.mult,
                        [psk(0), 'gcol'], ['MODC'])

        if STOP == 'modc':
            raise StopBuild()
        P.mark('cast')
        def cast_w(src_h, dst_h, rows, key, npieces):
            step = rows // npieces
            for i in range(npieces):
                dma('pool', dst_h.ap()[i * step:(i + 1) * step, :], src_h.ap()[i * step:(i + 1) * step, :], (), [(key, i)])
            return [(key, i) for i in range(npieces)]
        WG0 = A.view(R_X, [2, KC, 40], BF16)
        mset('dve', WG0, 0.0, ['WG0', 'c3t', 'rowt', 'gcol'])
        winv0 = win_h.ap().rearrange('(kc p) c -> p kc c', p=128)
        for gi, g0 in enumerate((IG0, FG0)):
            for r0 in (0, 32):
                dma('pool', WG0[:, gi, :, r0:r0 + 8], winv0[:, :, g0:g0 + 8], (), ['WG0'], nonc=True)
        dma('sp', wgs_h.ap(), WG0.rearrange('p a b c -> p (a b c)'), ['WG0'], ['wgs'])
        dma('pool', wbr_h.ap(), wr_h.ap(), (), ['wbr'])
        dma('pool', wbi_h.ap(), wi_h.ap(), (), ['wbr'])
        k_win = cast_w(win_h, wbin_h, D, 'wbin', 8)
        k_wout = cast_w(wout_h, wbout_h, D, 'wbout', 4)
        k_wgu = cast_w(wgu_h, wbgu_h, D, 'wbgu', 16)
        k_wdn = cast_w(wdn_h, wbdn_h, DFF, 'wbdn', 8)

        wbin_v = wbin_h.ap().rearrange('(kc p) c -> p kc c', p=128)
        wbout_v = wbout_h.ap().rearrange('(kc p) c -> p kc c', p=128)
        wbgu_v = wbgu_h.ap().rearrange('(kc p) c -> p kc c', p=128)
        wbdn_v = wbdn_h.ap().rearrange('(f p) c -> p f c', p=128)

        if STOP == 'cast':
            raise StopBuild()
        mset('dve', Cst, 0.0, ['Cst'])
        mset('dve', mst, 0.0, ['mst'])
        mset('dve', hst, 0.0, ['hst'])
        mset('dve', convbuf, 0.0, ['convbuf'])

        def run_tile(x_src, T, segs, mode, y_dst=None):
            NBLK = T // 128
            NCH = T // 64
            full = (mode == 'main')
            h1T = A.view(R_H, [KC, T], BF16)
            yT = A.view(R_Y, [KC, T], BF16)
            KEYS_B = [('st', i) for i in range(12)] + ['COLS', 'DECB', 'WG', 'gainb', 'sc4', ('gainb', 0), ('gainb', 1), ('OT', 0), ('OT', 1)] + \
                [(n_, i) for n_ in ('qT', 'kT', 'kw', 'vext', 'so', 'grh', 'WT', 'SWT', 'HT', 'ND', 'HN', 'T1', 'dcol', 'rcol') for i in range(2)] + \
                [('yatok', b_) for b_ in range(4)]
            KEYS_L = ['WRI'] + [(n_, i) for n_ in ('XBH', 'XC', 'XCb', 'g0', 'g1', 'AA', 'A2', 'HH', 'GX', 'GW') for i in range(2)]
            KEYS_A = [('xin', 0), ('xin', 1), ('axn', 0), ('axn', 1)]
            KEYS_X1 = [('x1', b_) for b_ in range(4)]
            ykeys = [('yT', j) for j in range(KC)]
            KEYS_YD = ['G2B', ('aT', 0), ('aT', 1), ('cxn', 0), ('cxn', 1), 'csqj']

            P.mark(mode + str(T) + ':A')
            def norm_to_T(src_fn, dstT, mod_scale, mod_shift, xn_off, junk_off, tag, otag='h'):
                for blk in range(NBLK):
                    xin, xkey = src_fn(blk)
                    xn = A.view(xn_off + (blk % 2) * 8192, [D], BF16)
                    xnk = (tag + 'xn', blk % 2)
                    ssq = smallc[:, (blk % 2):(blk % 2) + 1]
                    sk = (tag + 'ssq', blk % 2)
                    if cfg.get('astop') == 0:
                        raise StopBuild()
                    sqj = A.view(junk_off, [D], F32)
                    act(sqj, xin, AF.Square, [xkey], [(tag + 'sqj')])
                    rsum(ssq, sqj, [(tag + 'sqj')], [sk])
                    if cfg.get('astop') == 1:
                        raise StopBuild()
                    ts('dve', ssq, ssq, 1.0 / D, EPS, ALU.mult, ALU.add, [sk], [sk])
                    act(ssq, ssq, AF.Ln, [sk], [sk])
                    act(ssq, ssq, AF.Exp, [sk], [sk], scale=-0.5)
                    if cfg.get('astop') == 2:
                        raise StopBuild()
                    ts('dve', xn, xin, ssq, None, ALU.mult, None, [xkey, sk], [xnk])
                    if cfg.get('astop') == 3:
                        raise StopBuild()
                    for j in range(KC):
                        pb = (j // 4) % 2
                        pbt = ps[pb].bitcast(BF16).rearrange('p (a b) -> p a b', a=8)
                        if j % 4 == 0:
                            for jj in range(4):
                                tr(pbt[:, jj, :], xn[:, (j + jj) * 128:(j + jj + 1) * 128], identB, [xnk, 'identB'],
                                   [psk(pb)] if jj == 0 else [])
                            P.buf[psk(pb)]['w'] = ('pe', P.cnt['pe'])
                            if cfg.get('astop') == 4:
                                raise StopBuild()
                        for (slot, t0, t1) in segs:
                            lo = max(t0, blk * 128); hi = min(t1, (blk + 1) * 128)
                            if lo >= hi:
                                continue
                            eng = 'dve' if pb == 0 else 'pool'
                            if eng == 'pool':
                                act(dstT[:, j, lo:hi], pbt[:, j % 4, lo - blk * 128:hi - blk * 128], AF.Identity,
                                    [psk(pb), 'MODC'], [(otag + 'T', j)], bias=mod_shift(slot, j), scale=mod_scale(slot, j))
                            else:
                                ts('dve', dstT[:, j, lo:hi], pbt[:, j % 4, lo - blk * 128:hi - blk * 128],
                                   mod_scale(slot, j), mod_shift(slot, j), ALU.mult, ALU.add,
                                   [psk(pb), 'MODC'], [(otag + 'T', j)])
                    if cfg.get('astop') == 5:
                        raise StopBuild()

            def srcA(blk):
                xin = A.view(R_X + (blk % 2) * 16384, [D], F32)
                key = ('xin', blk % 2)
                dma('sp', xin, x_src[blk * 128:(blk + 1) * 128, :], (), [key])
                return xin, key
            hkeys = [('hT', j) for j in range(KC)]
            P.alias(KEYS_A, KEYS_X1 + KEYS_L + KEYS_B + ['c3t', 'c3e', 'scT', 'badat', 'adast', 'rowt', 'gcol', 'WG0'])
            P.alias(hkeys, ['NFG', 'fjunk'])
            norm_to_T(srcA, h1T, lambda s, j: MODC[:, s, 0, j:j + 1], lambda s, j: MODC[:, s, 1, j:j + 1],
                      R_X + 32768, R_X + 49152, 'a')
            hkeys = [('hT', j) for j in range(KC)]

            if STOP == 'A':
                raise StopBuild()
            P.mark(mode + str(T) + ':Bstats')
            SB = R_X
            MB = R_X + 28672

            def stat(i):
                return A.view(SB + i * 2048, [T], F32, parts=40)
            S_A, S_Z, S_W, S_BC, S_D0, S_M, S_GL, S_GRB, S_SI, S_EM, S_WL, S_MI = [stat(i) for i in range(12)]
            S_GRH = [stat(12), stat(13)]
            mo = [MB]

            def mal(shape, dt, parts=128):
                sz = 2 if dt == BF16 else 4
                n = 1
                for s in shape:
                    n *= s
                off = mo[0]
                mo[0] += (n * sz + 31) // 32 * 32
                assert mo[0] <= R_X + 65536, mo[0]
                return A.view(off, shape, dt, parts=parts)
            WG = mal([2, KC, 40], BF16)
            COLS = mal([NBLK, 120], F32)
            DECB = mal([8, 8], F32)
            qT = [mal([T], BF16) for _ in range(2)]
            kT = [mal([T], BF16) for _ in range(2)]
            kw = [mal([NBLK, 128], BF16) for _ in range(2)]
            vext = [mal([NBLK, 257], BF16) for _ in range(2)]
            so = [mal([NBLK, 256], BF16) for _ in range(2)]
            yatok = mal([NBLK, 256], BF16)
            gainb = mal([2, 256], F32)
            WT = [mal([64], F32) for _ in range(2)]
            SWT = [mal([64], BF16) for _ in range(2)]
            HT = [mal([257], F32) for _ in range(2)]
            ND = [mal([257], F32) for _ in range(2)]
            HN = [mal([256], F32) for _ in range(2)]
            T1 = [mal([256], F32) for _ in range(2)]
            OT = [A.view(MB + i_ * 1024, [256], F32) for i_ in range(2)]
            sc4 = mal([16], F32)

            P.alias(KEYS_B, KEYS_A)
            P.alias(ykeys, KEYS_YD)
            if mode == 'main' and tile_first[0]:
                ts('dve', mst, mst, flagc[0:40, 0:1], None, ALU.mult, None, ['mst', 'flagc'], ['mst'])
            dma('sp', WG.rearrange('p a b c -> p (a b c)'), wgs_h.ap(), ['wgs'] + k_win + k_wout + k_wgu + k_wdn, ['WG'])
            for gi, dst in enumerate((S_A, S_Z)):
                pb = 2 + gi
                mm(ps[pb][0:40, 0:T], [(WG[:, gi, kc, :], h1T[:, kc, :]) for kc in range(KC)], ['WG'] + hkeys, [psk(pb)])
                ts('dve', dst, ps[pb][0:40, 0:T], bgc[:, gi:gi + 1], None, ALU.add, None, [psk(pb), 'bgc'], [('st', gi)])
            P.alias([('OT', 0), ('OT', 1)], ['WG'])
            act(S_W, S_Z, AF.Abs, [('st', 1)], [('st', 2)])
            act(S_W, S_W, AF.Exp, [('st', 2)], [('st', 2)], scale=-1.0)
            act(S_W, S_W, AF.Ln, [('st', 2)], [('st', 2)], bias=1.0)
            stt('dve', S_Z, S_Z, 0.0, S_W, ALU.min, ALU.subtract, [('st', 1), ('st', 2)], [('st', 1)])
            for (slot, t0, t1) in segs:
                n = t1 - t0
                nch = n // 64
                P.op('dve', lambda e, t0=t0, t1=t1, n=n: e.tensor_tensor_scan(S_BC[:, t0:t1], scanmask[:, 0:n], S_Z[:, t0:t1], 0.0,
                                                                             ALU.mult, ALU.add),
                     [('st', 1), 'scanmask'], [('st', 3)])
            tt('dve', S_A, S_A, S_BC, ALU.subtract, [('st', 0), ('st', 3)], [('st', 0)])
            mset('dve', S_D0, 0.0, [('st', 4)])
            mkeys = []
            for (slot, t0, t1) in segs:
                n = t1 - t0
                nch = n // 64
                if nch > 1:
                    d0v = S_D0[:, t0:t1].rearrange('p (c l) -> p c l', l=64)
                    bcv = S_BC[:, t0:t1].rearrange('p (c l) -> p c l', l=64)
                    cp('dve', d0v[:, 1:nch, 0:1], bcv[:, 0:nch - 1, 63:64], [('st', 3)], [('st', 4)])
                mk = ('mst', slot)
                if slot != 0:
                    for r0 in (0, 32):
                        dma('sp', mst[r0:r0 + 8, :], ism_h.ap()[slot - 1:slot, :].rearrange('a h -> h a'), (), ['mst'], nonc=True)
                P.op('dve', lambda e, t0=t0, t1=t1: e.tensor_tensor_scan(S_M[:, t0:t1], S_D0[:, t0:t1], S_A[:, t0:t1], mst[:, 0:1],
                                                                         ALU.add, ALU.max),
                     [('st', 4), ('st', 0), 'mst'], [('st', 5)])
                miv = S_MI[:, t0:t1].rearrange('p (c l) -> p c l', l=64)
                mv = S_M[:, t0:t1].rearrange('p (c l) -> p c l', l=64)
                bcv = S_BC[:, t0:t1].rearrange('p (c l) -> p c l', l=64)
                mset('dve', S_MI[:, t0:t1], 0.0, [('st', 11)])
                ts('dve', S_MI[:, t0:t0 + 64], S_MI[:, t0:t0 + 64], mst[:, 0:1], None, ALU.add, None, [('st', 11), 'mst'], [('st', 11)])
                for c in range(1, nch):
                    tt('dve', sc4[0:40, 0:1], bcv[:, c - 1, 63:64], mv[:, c - 1, 63:64], ALU.add, [('st', 3), ('st', 5)], ['sc4'])
                    ts('dve', miv[:, c, :], miv[:, c, :], sc4[0:40, 0:1], None, ALU.add, None, [('st', 11), 'sc4'], [('st', 11)])
                for c in range(nch):
                    ts('dve', S_WL[:, t0 + c * 64:t0 + (c + 1) * 64], S_A[:, t0 + c * 64:t0 + (c + 1) * 64], mv[:, c, 63:64], None,
                       ALU.subtract, None, [('st', 0), ('st', 5)], [('st', 10)])
                tt('dve', mst[:, 0:1], bcv[:, nch - 1, 63:64], mv[:, nch - 1, 63:64], ALU.add, [('st', 3), ('st', 5), ('st', 11)], ['mst'])
                if slot != 0:
                    dma('sp', osm_h.ap()[slot - 1:slot, :].rearrange('a h -> h a'), mst[0:8, :], ['mst'], [], final=True, nonc=True)
            act(S_WL, S_WL, AF.Exp, [('st', 10)], [('st', 10)])
            tt('dve', S_SI, S_MI, S_M, ALU.subtract, [('st', 11), ('st', 5)], [('st', 8)])
            act(S_SI, S_SI, AF.Exp, [('st', 8)], [('st', 8)])
            tt('dve', S_EM, S_BC, S_M, ALU.add, [('st', 3), ('st', 5)], [('st', 9)])
            act(S_EM, S_EM, AF.Exp, [('st', 9)], [('st', 9)], scale=-1.0)
            ts('dve', S_GL, S_A, ccols[0:40, 0:1], ccols[0:40, 1:2], ALU.mult, ALU.add, [('st', 0), 'ccols'], [('st', 6)])
            ts('dve', S_GRB, S_M, ccols[0:40, 2:3], ccols[0:40, 0:1], ALU.mult, ALU.add, [('st', 5), 'ccols'], [('st', 7)])
            for blk in range(NBLK):
                pst = ps[4][:, 0:120].rearrange('p (a b) -> p a b', a=3)
                for i, (src, k) in enumerate(((S_SI, 8), (S_EM, 9), (S_WL, 10))):
                    tr(pst[:, i, :], src[:, blk * 128:(blk + 1) * 128], identF[0:40, 0:40], [('st', k), 'identF'],
                       [psk(4)] if i == 0 else [])
                P.buf[psk(4)]['w'] = ('pe', P.cnt['pe'])
                cp('dve', COLS[:, blk, :], ps[4][:, 0:120], [psk(4)], ['COLS'])
            siv = S_SI.rearrange('p (c l) -> p c l', l=64)
            for h in range(8):
                ts('dve', sc4[0:40, 0:NCH], siv[:, :, 63], ccols[0:40, 11 + h:12 + h], None, ALU.mult, None,
                   [('st', 8), 'ccols'], ['sc4'])
                mm(ps[5][:, 0:NCH], [(ones40, sc4[0:40, 0:NCH])], ['ones40', 'sc4'], [psk(5)])
                cp('dve', DECB[:, h, 0:NCH], ps[5][:, 0:NCH], [psk(5)], ['DECB'])

            if STOP == 'stats':
                raise StopBuild()
            P.mark(mode + str(T) + ':Bmlstm')
            def seg_of(tok):
                for (slot, t0, t1) in segs:
                    if t0 <= tok < t1:
                        return slot, t0, t1
                raise AssertionError

            def load_slab(view, col0, ncols, wkeys):
                si, so_, skey = ring_slot()
                slab = A.view(so_, [KC, ncols], BF16)
                dma('sp', slab, view[:, :, col0:col0 + ncols], wkeys, [skey])
                return slab, skey

            def proj_fm(dst, slab, skey, c0, scale, dkey):
                pb = 2 + (ring_i[0] + c0 // 128) % 2
                mm(ps[pb][:, 0:T], [(slab[:, kc, c0:c0 + 128], h1T[:, kc, :]) for kc in range(KC)], [skey] + hkeys, [psk(pb)])
                act(dst, ps[pb][:, 0:T], AF.Copy, [psk(pb)], [dkey], scale=scale)

            def proj_tm(slab, skey, blk, pb):
                mm(ps[pb][:, 0:256], [(h1T[:, kc, blk * 128:(blk + 1) * 128], slab[:, kc, :]) for kc in range(KC)],
                   [skey] + hkeys, [psk(pb)])

            def proj_steps(h):
                hh = h % 2
                steps = []
                if full:
                    def s_q():
                        slab, skey = load_slab(wbin_v, Q0 + 128 * h, 128, k_win)
                        proj_fm(qT[hh], slab, skey, 0, 1.0, ('qT', hh))
                    steps.append(s_q)

                def s_k():
                    slab, skey = load_slab(wbin_v, K0 + 128 * h, 128, k_win)
                    proj_fm(kT[hh], slab, skey, 0, 128.0 ** -0.5, ('kT', hh))
                steps.append(s_k)

                def s_kt():
                    for blk in range(NBLK):
                        pbt = ps[blk % 2].bitcast(BF16)
                        tr(pbt[:, 0:128], kT[hh][:, blk * 128:(blk + 1) * 128], identB, [('kT', hh), 'identB'], [psk(blk % 2)])
                        ts('dve', kw[hh][:, blk, :], pbt[:, 0:128], COLS[:, blk, 80 + h:81 + h], None, ALU.mult, None,
                           [psk(blk % 2), 'COLS'], [('kw', hh)])
                steps.append(s_kt)
                vs = {}
                for blk in range(NBLK):
                    def s_v(blk=blk):
                        if blk == 0:
                            vs['v'] = load_slab(wbin_v, V0 + 256 * h, 256, k_win)
                            mset('dve', vext[hh][:, :, 256:257], 1.0, [('vext', hh)])
                        slab, skey = vs['v']
                        pb = 2 + blk % 2
                        proj_tm(slab, skey, blk, pb)
                        act(vext[hh][:, blk, 0:256], ps[pb][:, 0:256], AF.Copy, [psk(pb)], [('vext', hh)])
                    steps.append(s_v)
                if full:
                    for blk in range(NBLK):
                        def s_o(blk=blk):
                            if blk == 0:
                                vs['o'] = load_slab(wbin_v, O0 + 256 * h, 256, k_win)
                                dma('sp', gainb[:, hh, :], bcast_ap(hng_h, h * 256, 128, 256), (), [('gainb', hh)])
                            slab, skey = vs['o']
                            pb = 2 + blk % 2
                            proj_tm(slab, skey, blk, pb)
                            ot = OT[blk % 2]
                            ok_ = ('OT', blk % 2)
                            act(ot, ps[pb][:, 0:256], AF.Exp, [psk(pb)], [ok_], scale=-1.0)
                            ts('dve', ot, ot, 1.0, None, ALU.add, None, [ok_], [ok_])
                            recip(so[hh][:, blk, :], ot, [ok_], [('so', hh)])
                        steps.append(s_o)
                return steps

            def chain_steps(h):
                hh = h % 2
                ck = ('Cst', h)
                cbk = ('Cb', h)
                grh = S_GRH[hh]
                steps = []
                for c in range(NCH):
                    def step(c=c):
                        slot, t0, t1 = seg_of(c * 64)
                        blk, half = c // 2, c % 2
                        p0 = half * 64
                        cs = slice(c * 64, (c + 1) * 64)
                        if c == 0:
                            ts('dve', grh, S_GRB, ccols[0:40, 3 + h:4 + h], None, ALU.mult, None, [('st', 7), 'ccols'], [('grh', hh)])
                        if c * 64 == t0:
                            if slot != 0:
                                q_ = slot - 1
                                dma('sp', Cst[:, h, 0:256], isC_h.ap()[q_, h, :, :], (), [ck])
                                dma('sp', Cst[:, h, 256:257], isn_h.ap()[q_, h:h + 1, :].rearrange('a d -> d a'), (), [ck], nonc=True)
                            elif mode == 'main' and c == 0 and tile_first[0]:
                                ts('dve', Cst[:, h, :], Cst[:, h, :], flagc[:, 0:1], None, ALU.mult, None, [ck, 'flagc'], [ck])
                            cp('act', Cb[:, h, :], Cst[:, h, :], [ck], [cbk])
                        if full:
                            pS = ps[6]
                            mm(pS[p0:p0 + 64, 0:64], [(kT[hh][:, cs], qT[hh][:, cs])], [('kT', hh), ('qT', hh)], [psk(6)])
                            mm(ps[7][p0:p0 + 64, 320:384], [(S_GL[:, cs], grh[:, cs]), (identF[0:64, 0:64], NEGM)],
                               [('st', 6), ('grh', hh), 'identF', 'NEGM'], [psk(7)])
                            wt = WT[c % 2]; swt = SWT[c % 2]
                            act(wt[p0:p0 + 64, :], ps[7][p0:p0 + 64, 320:384], AF.Exp, [psk(7)], [('WT', c % 2)])
                            tt('dve', swt[p0:p0 + 64, :], pS[p0:p0 + 64, 0:64], wt[p0:p0 + 64, :], ALU.mult,
                               [psk(6), ('WT', c % 2)], [('SWT', c % 2)])
                            mm(ps[7][p0:p0 + 64, 0:257], [(swt[p0:p0 + 64, :], vext[hh][p0:p0 + 64, blk, :])],
                               [('SWT', c % 2), ('vext', hh)], [psk(7)])
                            mm(ps[5][p0:p0 + 64, 0:257], [(qT[hh][:, cs], Cb[:, h, :])], [('qT', hh), cbk], [psk(5)])
                            ht = HT[c % 2]; nd = ND[c % 2]; hn = HN[c % 2]; t1_ = T1[c % 2]
                            act(ht[p0:p0 + 64, :], ps[5][p0:p0 + 64, 0:257], AF.Identity, [psk(5), 'COLS'], [('HT', c % 2)],
                                scale=COLS[p0:p0 + 64, blk, h:h + 1])
                            tt('dve', nd[p0:p0 + 64, :], ht[p0:p0 + 64, :], ps[7][p0:p0 + 64, 0:257], ALU.add,
                               [('HT', c % 2), psk(7)], [('ND', c % 2)])
                            dcol = sc4[p0:p0 + 64, 4 + (c % 2):5 + (c % 2)]
                            act(dcol, nd[p0:p0 + 64, 256:257], AF.Abs, [('ND', c % 2)], [('dcol', c % 2)])
                            ts('dve', dcol, dcol, COLS[p0:p0 + 64, blk, 40 + h:41 + h], None, ALU.max, None,
                               [('dcol', c % 2), 'COLS'], [('dcol', c % 2)])
                            recip(dcol, dcol, [('dcol', c % 2)], [('dcol', c % 2)])
                            ts('dve', hn[p0:p0 + 64, :], nd[p0:p0 + 64, 0:256], dcol, None, ALU.mult, None,
                               [('ND', c % 2), ('dcol', c % 2)], [('HN', c % 2)])
                            rcol = sc4[p0:p0 + 64, 6 + (c % 2):7 + (c % 2)]
                            act(t1_[p0:p0 + 64, :], hn[p0:p0 + 64, :], AF.Square, [('HN', c % 2)], [('T1', c % 2)])
                            rsum(rcol, t1_[p0:p0 + 64, :], [('T1', c % 2)], [('rcol', c % 2)])
                            ts('dve', rcol, rcol, 1.0 / 256, EPS, ALU.mult, ALU.add, [('rcol', c % 2)], [('rcol', c % 2)])
                            act(rcol, rcol, AF.Ln, [('rcol', c % 2)], [('rcol', c % 2)])
                            act(rcol, rcol, AF.Exp, [('rcol', c % 2)], [('rcol', c % 2)], scale=-0.5)
                            stt('dve', t1_[p0:p0 + 64, :], hn[p0:p0 + 64, :], rcol, gainb[p0:p0 + 64, hh, :], ALU.mult, ALU.mult,
                                [('HN', c % 2), ('rcol', c % 2), ('gainb', hh)], [('T1', c % 2)])
                            tt('pool', yatok[p0:p0 + 64, blk, :], t1_[p0:p0 + 64, :], so[hh][p0:p0 + 64, blk, :], ALU.mult,
                               [('T1', c % 2), ('so', hh)], [('yatok', blk)])
                        mm(ps[4][:, 0:257], [(kw[hh][p0:p0 + 64, blk, :], vext[hh][p0:p0 + 64, blk, :])],
                           [('kw', hh), ('vext', hh)], [psk(4)])
                        stt('dve', Cst[:, h, :], Cst[:, h, :], DECB[:, h, c:c + 1], ps[4][:, 0:257], ALU.mult, ALU.add,
                            [ck, 'DECB', psk(4)], [ck])
                        last_of_seg = ((c + 1) * 64 == t1)
                        if not last_of_seg:
                            cp('act', Cb[:, h, :], Cst[:, h, :], [ck], [cbk])
                        elif slot != 0:
                            q_ = slot - 1
                            dma('sp', osC_h.ap()[q_, h, :, :], Cst[:, h, 0:256], [ck], [], final=True)
                            dma('sp', osn_h.ap()[q_, h:h + 1, :].rearrange('a d -> d a'), Cst[:, h, 256:257], [ck], [], final=True, nonc=True)
                        if full and half == 1:
                            for e2 in range(2):
                                pbt = ps[e2].bitcast(BF16)
                                tr(pbt[:, 0:128], yatok[:, blk, e2 * 128:(e2 + 1) * 128], identB, [('yatok', blk), 'identB'], [psk(e2)])
                                cp('act', yT[:, 2 * h + e2, blk * 128:(blk + 1) * 128], pbt[:, 0:128], [psk(e2)], [('yT', 2 * h + e2)])
                    steps.append(step)
                return steps

            for st_ in proj_steps(0):
                st_()
            for h in range(8):
                cst = chain_steps(h)
                pst = proj_steps(h + 1) if h < 7 else []
                i_p = 0
                for i in range(len(cst)):
                    cst[i]()
                    tgt = (len(pst) * (i + 1)) // len(cst)
                    while i_p < tgt:
                        pst[i_p]()
                        i_p += 1
                while i_p < len(pst):
                    pst[i_p]()
                    i_p += 1

            if STOP == 'mlstm':
                raise StopBuild()
            P.mark(mode + str(T) + ':Blru')
            lo_ = [R_X]

            def lal(shape, dt, parts=128):
                sz = 2 if dt == BF16 else 4
                n = 1
                for s in shape:
                    n *= s
                off = lo_[0]
                lo_[0] += (n * sz + 31) // 32 * 32
                assert lo_[0] <= R_X + 65536
                return A.view(off, shape, dt, parts=parts)
            WRI = lal([2, 16, 128], BF16)
            XBH = [lal([3 + T], F32) for _ in range(2)]
            XC = [lal([T], F32) for _ in range(2)]
            XCb = [lal([T], BF16) for _ in range(2)]
            RG = [lal([T], F32) for _ in range(2)]
            IGt = [lal([T], F32) for _ in range(2)]
            AAt = [lal([T], F32) for _ in range(2)]
            A2t = [lal([T], F32) for _ in range(2)]
            HH = [lal([T], F32) for _ in range(2)]
            GX = [lal([T], F32) for _ in range(2)]
            GW = [lal([T], F32) for _ in range(2)]
            P.alias(KEYS_L, KEYS_B)
            dma('sp', WRI[:, 0, :, :], wbr_h.ap().rearrange('n i j -> i n j'), ['wbr'], ['WRI'])
            dma('sp', WRI[:, 1, :, :], wbi_h.ap().rearrange('n i j -> i n j'), ['wbr'], ['WRI'])
            has_sample = any(slot != 0 for (slot, _a, _b) in segs)
            if has_sample:
                for (slot, t0, t1) in segs:
                    q_ = slot - 1
                    for j in range(3):
                        dma('sp', SCV[:, q_, :, j], iscv_h.ap()[q_, j, :].rearrange('(b p) -> p b', p=128), (), ['SCV'], nonc=True)
                    dma('sp', SH[:, q_, :], ish_h.ap()[q_, :].rearrange('(b p) -> p b', p=128), (), ['SH'], nonc=True)
            lslab = {}

            def lru_proj(nb):
                sp_, b2 = nb // 2, nb % 2
                u = nb % 2
                K = lambda n_: (n_, u)
                if b2 == 0:
                    lslab['x'] = load_slab(wbin_v, XB0 + 256 * sp_, 256, k_win)
                    if full:
                        lslab['g'] = load_slab(wbin_v, GB0 + 256 * sp_, 256, k_win)
                slabx, skx = lslab['x']
                pb = 2 + u
                mm(ps[pb][:, 0:T], [(slabx[:, kc, b2 * 128:(b2 + 1) * 128], h1T[:, kc, :]) for kc in range(KC)],
                   [skx] + hkeys, [psk(pb)])
                act(XBH[u][:, 3:3 + T], ps[pb][:, 0:T], AF.Copy, [psk(pb)], [K('XBH')])
                for (slot, t0, t1) in segs:
                    n = t1 - t0
                    if slot != 0:
                        q_ = slot - 1
                        cp('dve', convbuf[:, nb, :], SCV[:, q_, nb, :], ['SCV'], [('convbuf', nb)])
                    elif mode == 'main' and tile_first[0]:
                        ts('dve', convbuf[:, nb, :], convbuf[:, nb, :], flagc[:, 0:1], None, ALU.mult, None, [('convbuf', nb), 'flagc'], [('convbuf', nb)])
                        ts('dve', hst[:, nb:nb + 1], hst[:, nb:nb + 1], flagc[:, 0:1], None, ALU.mult, None, [('hst', nb), 'flagc'], [('hst', nb)])
                    if t0 == 0:
                        cp('dve', XBH[u][:, 0:3], convbuf[:, nb, :], [('convbuf', nb)], [K('XBH')])
                        xp = XBH[u]
                        xk = K('XBH')
                    else:
                        xp = GW[u]
                        xk = K('GW')
                        cp('dve', xp[:, 0:3], convbuf[:, nb, :], [('convbuf', nb)], [K('GW')])
                        cp('dve', xp[:, 3:3 + n], XBH[u][:, 3 + t0:3 + t1], [K('XBH')], [K('GW')])
                    xcs = XC[u][:, t0:t1]
                    ts('dve', xcs, xp[:, 0:n], lruc[:, nb, 2:3], lruc[:, nb, 6:7], ALU.mult, ALU.add, [xk, 'lruc'], [K('XC')])
                    for j in range(1, 4):
                        stt('dve', xcs, xp[:, j:j + n], lruc[:, nb, 2 + j:3 + j], xcs, ALU.mult, ALU.add,
                            [xk, 'lruc', K('XC')], [K('XC')])
                    if slot != 0:
                        cp('dve', SCVO[:, slot - 1, nb, :], XBH[u][:, t1:t1 + 3], [K('XBH')], ['SCVO'])
                    else:
                        cp('dve', convbuf[:, nb, :], XBH[u][:, t1:t1 + 3], [K('XBH')], [('convbuf', nb)])
                cp('act', XCb[u], XC[u], [K('XC')], [K('XCb')])
                if full:
                    slabg, skg = lslab['g']
                    pq = 6 + u
                    mm(ps[pq][:, 0:T], [(slabg[:, kc, b2 * 128:(b2 + 1) * 128], h1T[:, kc, :]) for kc in range(KC)],
                       [skg] + hkeys, [psk(pq)])
                    act(GX[u], ps[pq][:, 0:T], AF.Copy, [psk(pq)], [K('GX')])
                    act(GW[u], ps[pq][:, 0:T], AF.Square, [psk(pq), K('GW')], [K('GW')])
                    ts('pool', GW[u], GW[u], 0.044715, 1.0, ALU.mult, ALU.add, [K('GW')], [K('GW')])
                    tt('pool', GW[u], GW[u], GX[u], ALU.mult, [K('GW'), K('GX')], [K('GW')])
                    act(GW[u], GW[u], AF.Exp, [K('GW')], [K('GW')], scale=-1.5957691216057308)
                    ts('pool', GW[u], GW[u], 1.0, None, ALU.add, None, [K('GW')], [K('GW')])
                    recip(GW[u], GW[u], [K('GW')], [K('GW')])
                    tt('pool', GW[u], GW[u], GX[u], ALU.mult, [K('GW'), K('GX')], [K('GW')])

            def lru_rest(nb):
                u = nb % 2
                K = lambda n_: (n_, u)
                for gi, (dst, bcol) in enumerate(((RG[u], 7), (IGt[u], 8))):
                    pg = 4 + gi
                    mm(ps[pg][:, 0:T], [(WRI[:, gi, nb, :], XCb[u])], ['WRI', K('XCb')], [psk(pg)])
                    ts('dve', dst, ps[pg][:, 0:T], lruc[:, nb, bcol:bcol + 1], None, ALU.add, None, [psk(pg), 'lruc'], [K('g%d' % gi)])
                    act(dst, dst, AF.Exp, [K('g%d' % gi)], [K('g%d' % gi)], scale=-1.0)
                    ts('dve', dst, dst, 1.0, None, ALU.add, None, [K('g%d' % gi)], [K('g%d' % gi)])
                    recip(dst, dst, [K('g%d' % gi)], [K('g%d' % gi)])
                act(AAt[u], RG[u], AF.Exp, [K('g0'), 'lruc'], [K('AA')], scale=lruc[:, nb, 0:1])
                act(A2t[u], RG[u], AF.Exp, [K('g0'), 'lruc'], [K('A2')], scale=lruc[:, nb, 1:2])
                ts('dve', A2t[u], A2t[u], -1.0, 1.0, ALU.mult, ALU.add, [K('A2')], [K('A2')])
                ts('dve', A2t[u], A2t[u], 1e-18, None, ALU.max, None, [K('A2')], [K('A2')])
                act(A2t[u], A2t[u], AF.Ln, [K('A2')], [K('A2')])
                act(A2t[u], A2t[u], AF.Exp, [K('A2')], [K('A2')], scale=0.5)
                tt('pool', IGt[u], IGt[u], XC[u], ALU.mult, [K('g1'), K('XC')], [K('g1')])
                tt('pool', A2t[u], A2t[u], IGt[u], ALU.mult, [K('A2'), K('g1')], [K('A2')])
                for (slot, t0, t1) in segs:
                    if slot != 0:
                        cp('dve', hst[:, nb:nb + 1], SH[:, slot - 1, nb:nb + 1], ['SH'], [('hst', nb)])
                    P.op('dve', lambda e, u=u, nb=nb, t0=t0, t1=t1: e.tensor_tensor_scan(
                        HH[u][:, t0:t1], AAt[u][:, t0:t1], A2t[u][:, t0:t1], hst[:, nb:nb + 1], ALU.mult, ALU.add),
                        [K('AA'), K('A2'), ('hst', nb)], [K('HH')])
                    if slot != 0:
                        cp('dve', SHO[:, slot - 1, nb:nb + 1], HH[u][:, t1 - 1:t1], [K('HH')], ['SHO'])
                    else:
                        cp('dve', hst[:, nb:nb + 1], HH[u][:, t1 - 1:t1], [K('HH')], [('hst', nb)])
                if full:
                    tt('dve', yT[:, 16 + nb, :], GW[u], HH[u], ALU.mult, [K('GW'), K('HH')], [('yT', 16 + nb)])

            lru_proj(0)
            for nb in range(1, 16):
                lru_proj(nb)
                lru_rest(nb - 1)
            lru_rest(15)
            if has_sample:
                for (slot, t0, t1) in segs:
                    q_ = slot - 1
                    for j in range(3):
                        dma('sp', oscv_h.ap()[q_, j, :].rearrange('(b p) -> p b', p=128), SCVO[:, q_, :, j], ['SCVO'], [], final=True, nonc=True)
                    dma('sp', osh_h.ap()[q_, :].rearrange('(b p) -> p b', p=128), SHO[:, q_, :], ['SHO'], [], final=True, nonc=True)
            if not full:
                return

            if STOP == 'lru':
                raise StopBuild()
            P.mark(mode + str(T) + ':C')
            x1 = A.view(R_X, [NBLK, D], F32)
            P.alias(KEYS_X1, KEYS_L + KEYS_B + KEYS_A)
            for blk in range(NBLK):
                dma('sp', x1[:, blk, :], x_src[blk * 128:(blk + 1) * 128, :], (), [('x1', blk)])
            G1 = [A.view(TMP + 8192 + i * 1024, [256], F32) for i in range(2)]
            CT = [A.view(TMP + 8192 + 2048 + i * 1024, [256], F32) for i in range(2)]
            for cg in range(16):
                slab, skey = load_slab(wbout_v, cg * 256, 256, k_wout)
                g1 = G1[cg % 2]
                for (slot, t0, t1) in segs:
                    pa, pb_ = (0, 128) if T == 512 else (t0, t1)
                    dma('sp', g1[pa:pb_, :], bcast_ap(ada_h, slot * 6 * D + 2 * D + cg * 256, pb_ - pa, 256), ['ada_dram'], [('G1', cg % 2)])
                for blk in range(NBLK):
                    pb = 2 + (cg * NBLK + blk) % 4
                    mm(ps[pb][:, 0:256], [(yT[:, kc, blk * 128:(blk + 1) * 128], slab[:, kc, :]) for kc in range(KC)],
                       [skey] + ykeys, [psk(pb)])
                    ct = CT[blk % 2]
                    tt('dve', ct, ps[pb][:, 0:256], g1, ALU.mult, [psk(pb), ('G1', cg % 2)], [('CT', blk % 2)])
                    tt('pool', x1[:, blk, cg * 256:(cg + 1) * 256], x1[:, blk, cg * 256:(cg + 1) * 256], ct, ALU.add,
                       [('CT', blk % 2), ('x1', blk)], [('x1', blk)])
            h2T = A.view(R_H, [KC, T], BF16)

            def srcC(blk):
                return x1[:, blk, :], ('x1', blk)
            P.alias([('cxn', 0), ('cxn', 1), 'csqj'], ykeys)
            norm_to_T(srcC, h2T, lambda s, j: MODC[:, s, 2, j:j + 1], lambda s, j: MODC[:, s, 3, j:j + 1], R_Y, R_Y + 16384, 'c')

            if STOP == 'C':
                raise StopBuild()
            P.mark(mode + str(T) + ':D')
            aT = [A.view(R_Y + 16384 + i * 8192, [8, T], BF16) for i in range(2)]
            G2B = A.view(R_Y, [D], F32)
            P.alias(['G2B', ('aT', 0), ('aT', 1)], [('cxn', 0), ('cxn', 1), 'csqj'] + ykeys)
            for (slot, t0, t1) in segs:
                pa, pb_ = (0, 128) if T == 512 else (t0, t1)
                dma('sp', G2B[pa:pb_, :], bcast_ap(ada_h, slot * 6 * D + 5 * D, pb_ - pa, D), ['ada_dram'], ['G2B'])
            EX = [A.view(TMP + i * 2048, [512], F32) for i in range(2)]
            DT_ = [A.view(TMP + 4096 + i * 2048, [512], F32) for i in range(2)]
            NG = (DFF + 1023) // 1024
            for g in range(NG):
                nchk = min(8, (DFF - g * 1024) // 128)
                at = aT[g % 2]
                ak = ('aT', g % 2)
                for ci in range(nchk):
                    f0 = g * 1024 + ci * 128
                    si, so_, skey = ring_slot()
                    slab = A.view(so_, [2, KC, 128], BF16)
                    dma('sp', slab[:, 0, :, :], wbgu_v[:, :, f0:f0 + 128], k_wgu, [skey])
                    dma('sp', slab[:, 1, :, :], wbgu_v[:, :, DFF + f0:DFF + f0 + 128], k_wgu, [skey])
                    u = ci % 2
                    pg, pu = 0 + 2 * u, 1 + 2 * u
                    mm(ps[pg][:, 0:T], [(slab[:, 0, kc, :], h2T[:, kc, :]) for kc in range(KC)], [skey] + hkeys, [psk(pg)])
                    mm(ps[pu][:, 0:T], [(slab[:, 1, kc, :], h2T[:, kc, :]) for kc in range(KC)], [skey] + hkeys, [psk(pu)])
                    ex = EX[u][:, 0:T]
                    act(ex, ps[pg][:, 0:T], AF.Exp, [psk(pg)], [('EX', u)], scale=-1.0)
                    ts('dve', ex, ex, 1.0, None, ALU.add, None, [('EX', u)], [('EX', u)])
                    recip(ex, ex, [('EX', u)], [('EX', u)])
                    tt('dve', ex, ex, ps[pg][:, 0:T], ALU.mult, [('EX', u), psk(pg)], [('EX', u)])
                    tt('dve', at[:, ci, :], ex, ps[pu][:, 0:T], ALU.mult, [('EX', u), psk(pu)], [ak])
                for c8 in range(8):
                    si, so_, skey = ring_slot()
                    slab = A.view(so_, [nchk, 512], BF16)
                    dma('sp', slab, wbdn_v[:, g * 8:g * 8 + nchk, c8 * 512:(c8 + 1) * 512], k_wdn, [skey])
                    for blk in range(NBLK):
                        pd = 4 + (c8 * NBLK + blk) % 4
                        mm(ps[pd][:, 0:512], [(at[:, ci, blk * 128:(blk + 1) * 128], slab[:, ci, :]) for ci in range(nchk)],
                           [skey, ak], [psk(pd)])
                        dt_ = DT_[blk % 2]
                        tt('dve', dt_, ps[pd][:, 0:512], G2B[:, c8 * 512:(c8 + 1) * 512], ALU.mult, [psk(pd), 'G2B'], [('DT', blk % 2)])
                        tt('pool', x1[:, blk, c8 * 512:(c8 + 1) * 512], x1[:, blk, c8 * 512:(c8 + 1) * 512], dt_, ALU.add,
                           [('DT', blk % 2), ('x1', blk)], [('x1', blk)])

            if STOP == 'D':
                raise StopBuild()
            P.mark(mode + str(T) + ':E')
            NFG = A.view(R_H, [D], F32)
            P.alias(['NFG', 'fjunk'], hkeys)
            dma('sp', NFG, bcast_ap(nfg_h, 0, 128, D), (), ['NFG'])
            junk = A.view(R_H + 16384, [D], F32)
            for blk in range(NBLK):
                ssq = smallc[:, 8 + blk:9 + blk]
                sk = ('fssq', blk)
                act(junk, x1[:, blk, :], AF.Square, [('x1', blk)], ['fjunk'])
                rsum(ssq, junk, ['fjunk'], [sk])
                ts('dve', ssq, ssq, 1.0 / D, EPS, ALU.mult, ALU.add, [sk], [sk])
                act(ssq, ssq, AF.Ln, [sk], [sk])
                act(ssq, ssq, AF.Exp, [sk], [sk], scale=-0.5)
                stt('dve', x1[:, blk, :], x1[:, blk, :], ssq, NFG, ALU.mult, ALU.mult, [('x1', blk), sk, 'NFG'], [('x1', blk)])
                dma('sp', y_dst[blk * 128:(blk + 1) * 128, :], x1[:, blk, :], [('x1', blk)], [], final=True)

        for i in range(NPRE):
            run_tile(xpre_h.ap()[i * 512:(i + 1) * 512, :], 512, [(0, 0, 512)], 'prefix')
        for i in range(NMAIN):
            tile_first[0] = (i == 0)
            run_tile(xmain_h.ap()[i * 512:(i + 1) * 512, :], 512, [(0, 0, 512)], 'main', yp_h.ap()[i * 512:(i + 1) * 512, :])
        tile_first[0] = False
        for h in range(8):
            dma('sp', opC_h.ap()[h, :, :], Cst[:, h, 0:256], [('Cst', h)], [], final=True)
            dma('sp', opn_h.ap()[h:h + 1, :].rearrange('a d -> d a'), Cst[:, h, 256:257], [('Cst', h)], [], final=True, nonc=True)
        dma('sp', opm_h.ap().rearrange('a h -> h a'), mst[0:8, :], ['mst'], [], final=True, nonc=True)
        dma('sp', oph_h.ap().rearrange('(b p) -> p b', p=128), hst, [('hst', nb) for nb in range(16)], [], final=True, nonc=True)
        for nb in range(16):
            dma('sp', opcv_h.ap()[:, nb * 128:(nb + 1) * 128].rearrange('j p -> p j'), convbuf[:, nb, :], [('convbuf', nb)], [], final=True, nonc=True)
        if DO_SAMPLE:
            run_tile(xs_h.ap(), 128, [(1, 0, 64), (2, 64, 128)], 'main', ys_h.ap())


    try:
        body()
    except StopBuild:
        pass
    P.build()
    LAST_MARKS[:] = P.marks
    return nc


def make_consts():
    c = np.zeros((128, 1024), np.float32)
    c[:, 0:128] = np.eye(128, dtype=np.float32)
    s = np.arange(64)[:, None]; t = np.arange(64)[None, :]
    c[0:64, 128:192] = np.where(s <= t, 0.0, NEGBIG)
    m = np.ones((512,), np.float32); m[::64] = 0.0
    c[0:40, 192:704] = m[None, :]
    cc = np.zeros((128, 32), np.float32)
    cc[0:8, 0] = 1.0
    cc[32:40, 1] = 1.0
    cc[32:40, 2] = -1.0
    for h in range(8):
        cc[h, 3 + h] = 1.0; cc[32 + h, 3 + h] = 1.0
        cc[h, 11 + h] = 1.0
    c[:, 704:736] = cc
    return c


_NC_CACHE = {}


def _get_nc(cfg_key, cfg):
    if cfg_key not in _NC_CACHE:
        _NC_CACHE[cfg_key] = build_program(cfg)
    return _NC_CACHE[cfg_key]


def make_in_maps(inp, cores=range(8)):
    consts = make_consts()
    maps = []
    f = lambda a: np.ascontiguousarray(a, dtype=np.float32)
    for c in cores:
        b, half = c // 2, c % 2
        sq = [2 * c, 2 * c + 1]
        m = {
            'xpre': f(inp['x_prompt'][b, 0:2048]),
            'xmain': f(inp['x_prompt'][b, half * 2048:(half + 1) * 2048]),
            'xs': f(inp['x_sample'][sq].reshape(128, D)),
            'c3': f(np.concatenate([inp['c_prompt'][b:b + 1], inp['c_sample'][sq]], 0)),
            'flag': np.full((128, 1), float(half), np.float32),
            'isC': f(inp['state_mlstm_C'][0, sq]), 'isn': f(inp['state_mlstm_n'][0, sq]),
            'ism': f(inp['state_mlstm_m'][0, sq]), 'ish': f(inp['state_lru_h'][0, sq]),
            'iscv': f(inp['state_conv'][0, sq]),
            'w_ada': f(inp['w_ada'][0]), 'b_ada': f(inp['b_ada'][0]),
            'norm1_g': f(inp['norm1_g'][0]), 'norm2_g': f(inp['norm2_g'][0]),
            'w_in': f(inp['w_in'][0]), 'b_gates_a': f(inp['b_gates_a'][0]), 'head_norm_g': f(inp['head_norm_g'][0]),
            'conv_w': f(inp['conv_w'][0]), 'conv_b': f(inp['conv_b'][0]),
            'w_r': f(inp['w_r'][0]), 'b_r': f(inp['b_r'][0]), 'w_i': f(inp['w_i'][0]), 'b_i': f(inp['b_i'][0]),
            'lru_lambda': f(inp['lru_lambda'][0]),
            'w_out': f(inp['w_out'][0]), 'w_gu': f(inp['w_gu'][0]), 'w_down': f(inp['w_down'][0]),
            'normf_g': f(inp['normf_g']), 'consts': consts,
        }
        maps.append(m)
    return maps


def kernel(**inp):
    inp = {k: np.asarray(v) for k, v in inp.items()}
    nc = _get_nc('full', {})
    maps = make_in_maps(inp)
    res = run_bass_kernel_spmd(nc, maps, core_ids=list(range(8)))
    R = res.results
    y_prompt = np.zeros((4, 4096, D), np.float32)
    y_sample = np.zeros((16, 64, D), np.float32)
    p_C = np.zeros((1, 4, 8, 128, 256), np.float32); p_n = np.zeros((1, 4, 8, 128), np.float32)
    p_m = np.zeros((1, 4, 8), np.float32); p_h = np.zeros((1, 4, 2048), np.float32); p_conv = np.zeros((1, 4, 3, 2048), np.float32)
    s_C = np.zeros((1, 16, 8, 128, 256), np.float32); s_n = np.zeros((1, 16, 8, 128), np.float32)
    s_m = np.zeros((1, 16, 8), np.float32); s_h = np.zeros((1, 16, 2048), np.float32); s_conv = np.zeros((1, 16, 3, 2048), np.float32)
    for c in range(8):
        b, half = c // 2, c % 2
        r = R[c]
        y_prompt[b, half * 2048:(half + 1) * 2048] = r['yp']
        y_sample[2 * c:2 * c + 2] = r['ys'].reshape(2, 64, D)
        if half == 1:
            p_C[0, b] = r['opC']; p_n[0, b] = r['opn']; p_m[0, b] = r['opm'].reshape(8)
            p_h[0, b] = r['oph']; p_conv[0, b] = r['opcv']
        s_C[0, 2 * c:2 * c + 2] = r['osC']; s_n[0, 2 * c:2 * c + 2] = r['osn']; s_m[0, 2 * c:2 * c + 2] = r['osm']
        s_h[0, 2 * c:2 * c + 2] = r['osh']; s_conv[0, 2 * c:2 * c + 2] = r['oscv']
    return (y_prompt, y_sample, p_C, p_n, p_m, p_h, p_conv, s_C, s_n, s_m, s_h, s_conv)
```

```python
from contextlib import ExitStack
import numpy as np
import concourse.bass as bass
import concourse.mybir as mybir
from concourse.bass_utils import run_bass_kernel_spmd

F32 = mybir.dt.float32
BF16 = mybir.dt.bfloat16
U8 = mybir.dt.uint8
AF = mybir.ActivationFunctionType
ALU = mybir.AluOpType
AX = mybir.AxisListType

D = 4096
KC = 32
IN_COLS = 10256
DFF = 11008
EPS = 1e-6
NEGBIG = -1.0e9
Q0, K0, V0, O0, IG0, FG0, XB0, GB0 = 0, 1024, 2048, 4096, 6144, 6152, 6160, 8208


class Prog:
    E = ['pe', 'act', 'dve', 'pool', 'sp']

    def __init__(self, nc, n_dma_sems=24):
        self.nc = nc
        self.ops = {e: [] for e in self.E}
        self.cnt = {e: 0 for e in self.E}
        self.known = {e: {} for e in self.E}
        self.buf = {}
        self.nd = n_dma_sems
        self.dcnt = [0] * n_dma_sems
        self.drr = 0
        self.nsw = 6
        self.swrr = 0
        self.final = []
        self.marks = []

    def _deps(self, reads, writes):
        deps = {}

        def add(s, v):
            if deps.get(s, 0) < v:
                deps[s] = v
        for k in reads:
            b = self.buf.get(k)
            if b and b['w']:
                add(*b['w'])
        for k in writes:
            b = self.buf.get(k)
            if b:
                if b['w']:
                    add(*b['w'])
                for s, v in b['r'].items():
                    add(s, v)
        return deps

    def _wait(self, e, deps):
        kn = self.known[e]
        for s, v in deps.items():
            if s == 'pe' and e == 'pe':
                continue
            if kn.get(s, 0) < v:
                self.ops[e].append(('w', s, v))
                kn[s] = v

    def _mark(self, reads, writes, tag):
        s, v = tag
        for k in reads:
            b = self.buf.setdefault(k, {'w': None, 'r': {}})
            if b['r'].get(s, 0) < v:
                b['r'][s] = v
        for k in writes:
            self.buf[k] = {'w': tag, 'r': {}}

    @staticmethod
    def _excl(reads, writes):
        ps_r = [k for k in reads if isinstance(k, tuple) and k[0] == 'ps']
        if ps_r:
            reads = [k for k in reads if not (isinstance(k, tuple) and k[0] == 'ps')]
            writes = list(writes) + [k for k in ps_r if k not in writes]
        return reads, writes

    def op(self, e, fn, reads=(), writes=()):
        reads, writes = self._excl(reads, writes)
        self._wait(e, self._deps(reads, writes))
        self.cnt[e] += 1
        self.ops[e].append(('o', fn, e, 1))
        self._mark(reads, writes, (e, self.cnt[e]))

    def dma(self, q, fn, reads=(), writes=(), final=False):
        if q == 'pool':
            i = self.nd - self.nsw + self.swrr
            self.swrr = (self.swrr + 1) % self.nsw
        else:
            i = self.drr
            self.drr = (self.drr + 1) % (self.nd - self.nsw)
        s = 'd%d' % i
        deps = self._deps(reads, writes)
        if self.dcnt[i] > 0:
            deps[s] = max(deps.get(s, 0), self.dcnt[i])
        self._wait(q, deps)
        self.dcnt[i] += 16
        self.ops[q].append(('o', fn, s, 16))
        self._mark(reads, writes, (s, self.dcnt[i]))
        if final:
            self.final.append((s, self.dcnt[i]))

    def mark(self, label):
        for e in self.E:
            self.ops[e].append(('m', label))

    def alias(self, new_keys, old_keys):
        r = {}
        for ok in old_keys:
            b = self.buf.get(ok)
            if not b:
                continue
            if b['w']:
                r[b['w'][0]] = max(r.get(b['w'][0], 0), b['w'][1])
            for s_, v_ in b['r'].items():
                r[s_] = max(r.get(s_, 0), v_)
        for nk in new_keys:
            nb = self.buf.setdefault(nk, {'w': None, 'r': {}})
            for s_, v_ in r.items():
                nb['r'][s_] = max(nb['r'].get(s_, 0), v_)

    def build(self):
        nc = self.nc
        for s, v in self.final:
            self._wait('sp', {s: v})
        for i in range(self.nd):
            if self.dcnt[i] > 0:
                self._wait('sp', {'d%d' % i: self.dcnt[i]})
        with ExitStack() as st:
            sems = {}
            for name in self.E + ['d%d' % i for i in range(self.nd)]:
                sems[name] = st.enter_context(nc.semaphore('s_' + name))
            st.enter_context(nc.allow_low_precision(reason="bf16 matmul operands, fp32 accumulation"))
            block = st.enter_context(nc.Block())

            def run(eng, lst):
                for o in lst:
                    if o[0] == 'm':
                        self.marks.append((str(eng), o[1], nc.get_next_instruction_name()))
                        continue
                    if o[0] == 'w':
                        eng.wait_ge(sems[o[1]], o[2])
                    else:
                        inst = o[1](eng)
                        inst.then_inc(sems[o[2]], o[3])

            @block.tensor
            def _(t):
                run(t, self.ops['pe'])

            @block.scalar
            def _(t):
                run(t, self.ops['act'])

            @block.vector
            def _(t):
                run(t, self.ops['dve'])

            @block.gpsimd
            def _(t):
                run(t, self.ops['pool'])

            @block.sync
            def _(t):
                run(t, self.ops['sp'])


LAST_MARKS = []


class StopBuild(Exception):
    pass


class Arena:
    def __init__(self, nc, nbytes):
        self.t = nc.alloc_sbuf_tensor('arena', [128, nbytes], U8)
        self.ap = self.t.ap()
        self.nbytes = nbytes

    def view(self, off, shape, dt, parts=128, p0=0):
        sz = 2 if dt == BF16 else 4
        n = 1
        for s in shape:
            n *= s
        assert off % 4 == 0 and off + n * sz <= self.nbytes, (off, shape)
        v = self.ap[p0:p0 + parts, off:off + n * sz].bitcast(dt)
        if len(shape) == 2:
            v = v.rearrange('p (a b) -> p a b', a=shape[0])
        elif len(shape) == 3:
            v = v.rearrange('p (a b c) -> p a b c', a=shape[0], b=shape[1])
        return v


def bcast_ap(handle, offset, nparts, n):
    return bass.AP(handle, offset, [[0, nparts], [1, n]])


def build_program(cfg):
    NPRE = cfg.get('npre', 4)
    NMAIN = cfg.get('nmain', 4)
    DO_SAMPLE = cfg.get('sample', True)
    nc = bass.Bass("TRN2", target_bir_lowering=False)
    P = Prog(nc)
    STOP = cfg.get('stop')

    def din(name, shape):
        return nc.dram_tensor(name, list(shape), F32, kind="ExternalInput")

    def dout(name, shape):
        return nc.dram_tensor(name, list(shape), F32, kind="ExternalOutput")

    xpre_h = din('xpre', [2048, D]); xmain_h = din('xmain', [2048, D]); xs_h = din('xs', [128, D])
    c3_h = din('c3', [3, D]); flag_h = din('flag', [128, 1])
    isC_h = din('isC', [2, 8, 128, 256]); isn_h = din('isn', [2, 8, 128]); ism_h = din('ism', [2, 8])
    ish_h = din('ish', [2, 2048]); iscv_h = din('iscv', [2, 3, 2048])
    wada_h = din('w_ada', [D, 6 * D]); bada_h = din('b_ada', [6 * D])
    n1g_h = din('norm1_g', [D]); n2g_h = din('norm2_g', [D])
    win_h = din('w_in', [D, IN_COLS]); bg_h = din('b_gates_a', [2, 8]); hng_h = din('head_norm_g', [8, 256])
    cw_h = din('conv_w', [4, 2048]); cb_h = din('conv_b', [2048])
    wr_h = din('w_r', [16, 128, 128]); br_h = din('b_r', [2048]); wi_h = din('w_i', [16, 128, 128]); bi_h = din('b_i', [2048])
    lam_h = din('lru_lambda', [2048])
    wout_h = din('w_out', [D, D]); wgu_h = din('w_gu', [D, 2 * DFF]); wdn_h = din('w_down', [DFF, D])
    nfg_h = din('normf_g', [D]); consts_h = din('consts', [128, 1024])

    yp_h = dout('yp', [2048, D]); ys_h = dout('ys', [128, D])
    opC_h = dout('opC', [8, 128, 256]); opn_h = dout('opn', [8, 128]); opm_h = dout('opm', [1, 8])
    oph_h = dout('oph', [2048]); opcv_h = dout('opcv', [3, 2048])
    osC_h = dout('osC', [2, 8, 128, 256]); osn_h = dout('osn', [2, 8, 128]); osm_h = dout('osm', [2, 8])
    osh_h = dout('osh', [2, 2048]); oscv_h = dout('oscv', [2, 3, 2048])

    wbin_h = nc.dram_tensor('wbin', [D, IN_COLS], BF16, kind="Internal")
    wbout_h = nc.dram_tensor('wbout', [D, D], BF16, kind="Internal")
    wbgu_h = nc.dram_tensor('wbgu', [D, 2 * DFF], BF16, kind="Internal")
    wbdn_h = nc.dram_tensor('wbdn', [DFF, D], BF16, kind="Internal")
    ada_h = nc.dram_tensor('ada_s', [3, 6 * D], F32, kind="Internal")
    wbr_h = nc.dram_tensor('wbr', [16, 128, 128], BF16, kind="Internal")
    wgs_h = nc.dram_tensor('wgs', [128, 2 * KC * 40], BF16, kind="Internal")
    wbi_h = nc.dram_tensor('wbi', [16, 128, 128], BF16, kind="Internal")

    A = Arena(nc, 212480)
    RING, R_H, R_Y, R_X, PERS = 0, 49152, 81920, 114688, 180224
    NSLOT = 3
    ps = [nc.alloc_psum_tensor('ps%d' % i, [128, 512], F32).ap() for i in range(8)]

    def psk(i):
        return ('ps', i)

    po = [PERS]

    def pal(shape, dt, parts=128):
        sz = 2 if dt == BF16 else 4
        n = 1
        for s in shape:
            n *= s
        off = po[0]
        po[0] += (n * sz + 31) // 32 * 32
        return A.view(off, shape, dt, parts=parts)

    Cst = pal([8, 257], F32); Cb = pal([8, 257], BF16)
    identF = pal([128], F32); identB = pal([128], BF16)
    NEGM = pal([64], F32, parts=64)
    scanmask = pal([512], F32, parts=40)
    ones40 = pal([128], F32, parts=40)
    ccols = pal([32], F32)
    MODC = pal([3, 4, 32], F32)
    hst = pal([16], F32); convbuf = pal([16, 3], F32)
    lruc = pal([16, 10], F32)
    mst = pal([1], F32, parts=40)
    flagc = pal([1], F32)
    bgc = pal([2], F32, parts=40)
    smallc = pal([64], F32)
    TMP = po[0]
    TMPSZ = A.nbytes - TMP
    assert TMPSZ >= 12288 + 1024, TMPSZ
    SCV = A.view(TMP + 12288, [2, 16, 3], F32)
    SCVO = A.view(TMP + 12288 + 384, [2, 16, 3], F32)
    SH = A.view(TMP + 12288 + 768, [2, 16], F32)
    SHO = A.view(TMP + 12288 + 896, [2, 16], F32)

    consts = consts_h.ap()

    def mm(out_ap, pairs, reads, writes):
        def fn(pe, pairs=pairs, out_ap=out_ap):
            n = len(pairs)
            last = None
            for i, (l, r) in enumerate(pairs):
                last = pe.matmul(out_ap, l, r, start=(i == 0), stop=(i == n - 1))
            return last
        P.op('pe', fn, reads, writes)

    def tr(out_ap, in_ap, ident, reads, writes):
        P.op('pe', lambda pe: pe.transpose(out_ap, in_ap, ident), reads, writes)

    def act(out_ap, in_ap, func, reads, writes, bias=None, scale=None, accum=None):
        kw = {}
        if bias is not None:
            kw['bias'] = bias
        if scale is not None:
            kw['scale'] = scale
        if accum is not None:
            kw['accum_out'] = accum
        P.op('act', lambda e: e.activation(out_ap, in_ap, func, **kw), reads, writes)

    def ts(eng, out_ap, in_ap, s1, s2, op0, op1, reads, writes):
        if s2 is None and not isinstance(s1, (int, float)):
            P.op(eng, lambda e: e.tensor_scalar(out_ap, in_ap, s1, 0.0, op0, ALU.add), reads, writes)
        elif s2 is None:
            P.op(eng, lambda e: e.tensor_scalar(out_ap, in_ap, s1, None, op0), reads, writes)
        else:
            P.op(eng, lambda e: e.tensor_scalar(out_ap, in_ap, s1, s2, op0, op1), reads, writes)

    def tt(eng, out_ap, a, b, op, reads, writes):
        P.op(eng, lambda e: e.tensor_tensor(out_ap, a, b, op), reads, writes)

    def stt(eng, out_ap, in0, sc, in1, op0, op1, reads, writes):
        P.op(eng, lambda e: e.scalar_tensor_tensor(out_ap, in0, sc, in1, op0, op1), reads, writes)

    def cp(eng, out_ap, in_ap, reads, writes):
        if eng == 'act':
            P.op(eng, lambda e: e.activation(out_ap, in_ap, AF.Copy), reads, writes)
        else:
            P.op(eng, lambda e: e.tensor_copy(out_ap, in_ap), reads, writes)

    def rsum(out_ap, in_ap, reads, writes):
        P.op('dve', lambda e: e.reduce_sum(out_ap, in_ap, AX.X), reads, writes)

    def recip(out_ap, in_ap, reads, writes):
        P.op('dve', lambda e: e.reciprocal(out_ap, in_ap), reads, writes)

    def mset(eng, ap, val, writes):
        P.op(eng, lambda e: e.memset(ap, val), (), writes)

    def dma(q, out_ap, in_ap, reads, writes, final=False, nonc=False):
        if nonc:
            P.dma(q, lambda e: e.dma_start(out=out_ap, in_=in_ap, allow_slow_non_contiguous=True), reads, writes, final)
        else:
            P.dma(q, lambda e: e.dma_start(out=out_ap, in_=in_ap), reads, writes, final)

    ring_i = [0]

    def ring_slot():
        i = ring_i[0] % NSLOT
        ring_i[0] += 1
        return i, RING + i * 16384, ('ring', i)

    tile_first = [False]

    def body():
        dma('sp', identF, consts[:, 0:128], (), ['identF'])
        dma('sp', NEGM, consts[0:64, 128:192], (), ['NEGM'])
        dma('sp', scanmask, consts[0:40, 192:704], (), ['scanmask'])
        dma('sp', ccols, consts[:, 704:736], (), ['ccols'])
        dma('sp', flagc, flag_h.ap(), (), ['flagc'])
        cp('dve', identB, identF, ['identF'], ['identB'])
        mset('dve', ones40, 1.0, ['ones40'])
        mset('dve', bgc, 0.0, ['bgc'])
        bgap = bg_h.ap()
        for r0 in (0, 32):
            dma('sp', bgc[r0:r0 + 8, 0:1], bgap[0:1, :].rearrange('a h -> h a'), (), ['bgc'], nonc=True)
            dma('sp', bgc[r0:r0 + 8, 1:2], bgap[1:2, :].rearrange('a h -> h a'), (), ['bgc'], nonc=True)
        for col, h_ in ((6, cb_h), (7, br_h), (8, bi_h), (9, lam_h)):
            dma('sp', lruc[:, :, col], h_.ap().rearrange('(b p) -> p b', p=128), (), ['lruc'], nonc=True)
        for j in range(4):
            dma('sp', lruc[:, :, 2 + j], cw_h.ap()[j, :].rearrange('(b p) -> p b', p=128), (), ['lruc'], nonc=True)
        act(lruc[:, :, 0], lruc[:, :, 9], AF.Exp, ['lruc'], ['lruc'], scale=-1.0)
        act(lruc[:, :, 0], lruc[:, :, 0], AF.Ln, ['lruc'], ['lruc'], bias=1.0)
        ts('dve', lruc[:, :, 1], lruc[:, :, 0], -16.0, None, ALU.mult, None, ['lruc'], ['lruc'])
        ts('dve', lruc[:, :, 0], lruc[:, :, 0], -8.0, None, ALU.mult, None, ['lruc'], ['lruc'])

        if STOP == 'p0':
            raise StopBuild()
        P.mark('ada')
        XO = R_X
        c3t = A.view(XO, [D], F32, parts=3)
        c3e = A.view(XO + 16384, [D], F32, parts=3)
        scT = A.view(XO + 32768, [KC, 4], BF16)
        badat = A.view(XO + 36864, [D], F32, parts=3)
        adast = A.view(XO + 36864 + 16384, [D], F32, parts=3)
        dma('sp', c3t, c3_h.ap(), (), ['c3t'])
        act(c3e, c3t, AF.Exp, ['c3t'], ['c3e'], scale=-1.0)
        ts('dve', c3e, c3e, 1.0, None, ALU.add, None, ['c3e'], ['c3e'])
        recip(c3e, c3e, ['c3e'], ['c3e'])
        tt('dve', c3t, c3t, c3e, ALU.mult, ['c3t', 'c3e'], ['c3t'])
        for kc in range(KC):
            pst = ps[kc % 2]
            tr(pst[:, 0:3], c3t[:, kc * 128:(kc + 1) * 128], identF[0:3, 0:3], ['c3t', 'identF'], [psk(kc % 2)])
            cp('dve', scT[:, kc, 0:3], pst[:, 0:3], [psk(kc % 2)], ['scT'])
        wada_v = wada_h.ap().rearrange('(kc p) c -> p kc c', p=128)
        for part in range(6):
            dma('sp', badat, bcast_ap(bada_h, part * D, 3, D), (), ['badat'])
            for sg in range(16):
                si, so_, skey = ring_slot()
                slab = A.view(so_, [KC, 256], BF16)
                c0 = part * D + sg * 256
                dma('pool', slab, wada_v[:, :, c0:c0 + 256], (), [skey])
                pb = 2 + (sg % 2)
                mm(ps[pb][0:3, 0:256], [(scT[:, kc, 0:3], slab[:, kc, :]) for kc in range(KC)], ['scT', skey], [psk(pb)])
                tt('dve', adast[:, sg * 256:(sg + 1) * 256], ps[pb][0:3, 0:256], badat[:, sg * 256:(sg + 1) * 256], ALU.add,
                   [psk(pb), 'badat'], ['adast'])
            dma('sp', ada_h.ap()[:, part * D:(part + 1) * D], adast, ['adast'], ['ada_dram'])

        if STOP == 'ada':
            raise StopBuild()
        rowt = A.view(XO, [128], F32, parts=32)
        gcol = A.view(XO + 512, [2, 32], F32)
        for gi, gh in enumerate((n1g_h, n2g_h)):
            dma('sp', rowt, gh.ap().rearrange('(j p) -> j p', p=128), (), ['rowt'])
            tr(ps[0][:, 0:32], rowt, identF[0:32, 0:32], ['rowt', 'identF'], [psk(0)])
            cp('dve', gcol[:, gi, :], ps[0][:, 0:32], [psk(0)], ['gcol'])
        for slot in range(3):
            for (part, mi, gi) in ((1, 0, 0), (0, 1, None), (4, 2, 1), (3, 3, None)):
                dma('sp', rowt, ada_h.ap()[slot, part * D:(part + 1) * D].rearrange('(j p) -> j p', p=128), ['ada_dram'], ['rowt'])
                tr(ps[0][:, 0:32], rowt, identF[0:32, 0:32], ['rowt', 'identF'], [psk(0)])
                if gi is None:
                    cp('dve', MODC[:, slot, mi, :], ps[0][:, 0:32], [psk(0)], ['MODC'])
                else:
                    stt('dve', MODC[:, slot, mi, :], ps[0][:, 0:32], 1.0, gcol[:, gi, :], ALU.add, ALU.mult,
                        [psk(0), 'gcol'], ['MODC'])

        if STOP == 'modc':
            raise StopBuild()
        P.mark('cast')
        def cast_w(src_h, dst_h, rows, key, npieces):
            step = rows // npieces
            for i in range(npieces):
                dma('pool', dst_h.ap()[i * step:(i + 1) * step, :], src_h.ap()[i * step:(i + 1) * step, :], (), [(key, i)])
            return [(key, i) for i in range(npieces)]
        WG0 = A.view(R_X, [2, KC, 40], BF16)
        mset('dve', WG0, 0.0, ['WG0', 'c3t', 'rowt', 'gcol'])
        winv0 = win_h.ap().rearrange('(kc p) c -> p kc c', p=128)
        for gi, g0 in enumerate((IG0, FG0)):
            for r0 in (0, 32):
                dma('pool', WG0[:, gi, :, r0:r0 + 8], winv0[:, :, g0:g0 + 8], (), ['WG0'], nonc=True)
        dma('sp', wgs_h.ap(), WG0.rearrange('p a b c -> p (a b c)'), ['WG0'], ['wgs'])
        dma('pool', wbr_h.ap(), wr_h.ap(), (), ['wbr'])
        dma('pool', wbi_h.ap(), wi_h.ap(), (), ['wbr'])
        k_win = cast_w(win_h, wbin_h, D, 'wbin', 8)
        k_wout = cast_w(wout_h, wbout_h, D, 'wbout', 4)
        k_wgu = cast_w(wgu_h, wbgu_h, D, 'wbgu', 16)
        k_wdn = cast_w(wdn_h, wbdn_h, DFF, 'wbdn', 8)

        wbin_v = wbin_h.ap().rearrange('(kc p) c -> p kc c', p=128)
        wbout_v = wbout_h.ap().rearrange('(kc p) c -> p kc c', p=128)
        wbgu_v = wbgu_h.ap().rearrange('(kc p) c -> p kc c', p=128)
        wbdn_v = wbdn_h.ap().rearrange('(f p) c -> p f c', p=128)

        if STOP == 'cast':
            raise StopBuild()
        mset('dve', Cst, 0.0, ['Cst'])
        mset('dve', mst, 0.0, ['mst'])
        mset('dve', hst, 0.0, ['hst'])
        mset('dve', convbuf, 0.0, ['convbuf'])

        def run_tile(x_src, T, segs, mode, y_dst=None):
            NBLK = T // 128
            NCH = T // 64
            full = (mode == 'main')
            h1T = A.view(R_H, [KC, T], BF16)
            yT = A.view(R_Y, [KC, T], BF16)
            KEYS_B = [('st', i) for i in range(12)] + ['COLS', 'DECB', 'WG', 'gainb', 'sc4', ('gainb', 0), ('gainb', 1), ('OT', 0), ('OT', 1)] + \
                [(n_, i) for n_ in ('qT', 'kT', 'kw', 'vext', 'so', 'grh', 'WT', 'SWT', 'HT', 'ND', 'HN', 'T1', 'dcol', 'rcol') for i in range(2)] + \
                [('yatok', b_) for b_ in range(4)]
            KEYS_L = ['WRI'] + [(n_, i) for n_ in ('XBH', 'XC', 'XCb', 'g0', 'g1', 'AA', 'A2', 'HH', 'GX', 'GW') for i in range(2)]
            KEYS_A = [('xin', 0), ('xin', 1), ('axn', 0), ('axn', 1)]
            KEYS_X1 = [('x1', b_) for b_ in range(4)]
            ykeys = [('yT', j) for j in range(KC)]
            KEYS_YD = ['G2B', ('aT', 0), ('aT', 1), ('cxn', 0), ('cxn', 1), 'csqj']

            P.mark(mode + str(T) + ':A')
            def norm_to_T(src_fn, dstT, mod_scale, mod_shift, xn_off, junk_off, tag, otag='h'):
                for blk in range(NBLK):
                    xin, xkey = src_fn(blk)
                    xn = A.view(xn_off + (blk % 2) * 8192, [D], BF16)
                    xnk = (tag + 'xn', blk % 2)
                    ssq = smallc[:, (blk % 2):(blk % 2) + 1]
                    sk = (tag + 'ssq', blk % 2)
                    if cfg.get('astop') == 0:
                        raise StopBuild()
                    sqj = A.view(junk_off, [D], F32)
                    act(sqj, xin, AF.Square, [xkey], [(tag + 'sqj')])
                    rsum(ssq, sqj, [(tag + 'sqj')], [sk])
                    if cfg.get('astop') == 1:
                        raise StopBuild()
                    ts('dve', ssq, ssq, 1.0 / D, EPS, ALU.mult, ALU.add, [sk], [sk])
                    act(ssq, ssq, AF.Ln, [sk], [sk])
                    act(ssq, ssq, AF.Exp, [sk], [sk], scale=-0.5)
                    if cfg.get('astop') == 2:
                        raise StopBuild()
                    ts('dve', xn, xin, ssq, None, ALU.mult, None, [xkey, sk], [xnk])
                    if cfg.get('astop') == 3:
                        raise StopBuild()
                    for j in range(KC):
                        pb = (j // 4) % 2
                        pbt = ps[pb].bitcast(BF16).rearrange('p (a b) -> p a b', a=8)
                        if j % 4 == 0:
                            for jj in range(4):
                                tr(pbt[:, jj, :], xn[:, (j + jj) * 128:(j + jj + 1) * 128], identB, [xnk, 'identB'],
                                   [psk(pb)] if jj == 0 else [])
                            P.buf[psk(pb)]['w'] = ('pe', P.cnt['pe'])
                            if cfg.get('astop') == 4:
                                raise StopBuild()
                        for (slot, t0, t1) in segs:
                            lo = max(t0, blk * 128); hi = min(t1, (blk + 1) * 128)
                            if lo >= hi:
                                continue
                            eng = 'dve' if pb == 0 else 'pool'
                            if eng == 'pool':
                                act(dstT[:, j, lo:hi], pbt[:, j % 4, lo - blk * 128:hi - blk * 128], AF.Identity,
                                    [psk(pb), 'MODC'], [(otag + 'T', j)], bias=mod_shift(slot, j), scale=mod_scale(slot, j))
                            else:
                                ts('dve', dstT[:, j, lo:hi], pbt[:, j % 4, lo - blk * 128:hi - blk * 128],
                                   mod_scale(slot, j), mod_shift(slot, j), ALU.mult, ALU.add,
                                   [psk(pb), 'MODC'], [(otag + 'T', j)])
                    if cfg.get('astop') == 5:
                        raise StopBuild()

            def srcA(blk):
                xin = A.view(R_X + (blk % 2) * 16384, [D], F32)
                key = ('xin', blk % 2)
                dma('sp', xin, x_src[blk * 128:(blk + 1) * 128, :], (), [key])
                return xin, key
            hkeys = [('hT', j) for j in range(KC)]
            P.alias(KEYS_A, KEYS_X1 + KEYS_L + KEYS_B + ['c3t', 'c3e', 'scT', 'badat', 'adast', 'rowt', 'gcol', 'WG0'])
            P.alias(hkeys, ['NFG', 'fjunk'])
            norm_to_T(srcA, h1T, lambda s, j: MODC[:, s, 0, j:j + 1], lambda s, j: MODC[:, s, 1, j:j + 1],
                      R_X + 32768, R_X + 49152, 'a')
            hkeys = [('hT', j) for j in range(KC)]

            if STOP == 'A':
                raise StopBuild()
            P.mark(mode + str(T) + ':Bstats')
            SB = R_X
            MB = R_X + 28672

            def stat(i):
                return A.view(SB + i * 2048, [T], F32, parts=40)
            S_A, S_Z, S_W, S_BC, S_D0, S_M, S_GL, S_GRB, S_SI, S_EM, S_WL, S_MI = [stat(i) for i in range(12)]
            S_GRH = [stat(12), stat(13)]
            mo = [MB]

            def mal(shape, dt, parts=128):
                sz = 2 if dt == BF16 else 4
                n = 1
                for s in shape:
                    n *= s
                off = mo[0]
                mo[0] += (n * sz + 31) // 32 * 32
                assert mo[0] <= R_X + 65536, mo[0]
                return A.view(off, shape, dt, parts=parts)
            WG = mal([2, KC, 40], BF16)
            COLS = mal([NBLK, 120], F32)
            DECB = mal([8, 8], F32)
            qT = [mal([T], BF16) for _ in range(2)]
            kT = [mal([T], BF16) for _ in range(2)]
            kw = [mal([NBLK, 128], BF16) for _ in range(2)]
            vext = [mal([NBLK, 257], BF16) for _ in range(2)]
            so = [mal([NBLK, 256], BF16) for _ in range(2)]
            yatok = mal([NBLK, 256], BF16)
            gainb = mal([2, 256], F32)
            WT = [mal([64], F32) for _ in range(2)]
            SWT = [mal([64], BF16) for _ in range(2)]
            HT = [mal([257], F32) for _ in range(2)]
            ND = [mal([257], F32) for _ in range(2)]
            HN = [mal([256], F32) for _ in range(2)]
            T1 = [mal([256], F32) for _ in range(2)]
            OT = [A.view(MB + i_ * 1024, [256], F32) for i_ in range(2)]
            sc4 = mal([16], F32)

            P.alias(KEYS_B, KEYS_A)
            P.alias(ykeys, KEYS_YD)
            if mode == 'main' and tile_first[0]:
                ts('dve', mst, mst, flagc[0:40, 0:1], None, ALU.mult, None, ['mst', 'flagc'], ['mst'])
            dma('sp', WG.rearrange('p a b c -> p (a b c)'), wgs_h.ap(), ['wgs'] + k_win + k_wout + k_wgu + k_wdn, ['WG'])
            for gi, dst in enumerate((S_A, S_Z)):
                pb = 2 + gi
                mm(ps[pb][0:40, 0:T], [(WG[:, gi, kc, :], h1T[:, kc, :]) for kc in range(KC)], ['WG'] + hkeys, [psk(pb)])
                ts('dve', dst, ps[pb][0:40, 0:T], bgc[:, gi:gi + 1], None, ALU.add, None, [psk(pb), 'bgc'], [('st', gi)])
            P.alias([('OT', 0), ('OT', 1)], ['WG'])
            act(S_W, S_Z, AF.Abs, [('st', 1)], [('st', 2)])
            act(S_W, S_W, AF.Exp, [('st', 2)], [('st', 2)], scale=-1.0)
            act(S_W, S_W, AF.Ln, [('st', 2)], [('st', 2)], bias=1.0)
            stt('dve', S_Z, S_Z, 0.0, S_W, ALU.min, ALU.subtract, [('st', 1), ('st', 2)], [('st', 1)])
            for (slot, t0, t1) in segs:
                n = t1 - t0
                nch = n // 64
                P.op('dve', lambda e, t0=t0, t1=t1, n=n: e.tensor_tensor_scan(S_BC[:, t0:t1], scanmask[:, 0:n], S_Z[:, t0:t1], 0.0,
                                                                             ALU.mult, ALU.add),
                     [('st', 1), 'scanmask'], [('st', 3)])
            tt('dve', S_A, S_A, S_BC, ALU.subtract, [('st', 0), ('st', 3)], [('st', 0)])
            mset('dve', S_D0, 0.0, [('st', 4)])
            mkeys = []
            for (slot, t0, t1) in segs:
                n = t1 - t0
                nch = n // 64
                if nch > 1:
                    d0v = S_D0[:, t0:t1].rearrange('p (c l) -> p c l', l=64)
                    bcv = S_BC[:, t0:t1].rearrange('p (c l) -> p c l', l=64)
                    cp('dve', d0v[:, 1:nch, 0:1], bcv[:, 0:nch - 1, 63:64], [('st', 3)], [('st', 4)])
                mk = ('mst', slot)
                if slot != 0:
                    for r0 in (0, 32):
                        dma('sp', mst[r0:r0 + 8, :], ism_h.ap()[slot - 1:slot, :].rearrange('a h -> h a'), (), ['mst'], nonc=True)
                P.op('dve', lambda e, t0=t0, t1=t1: e.tensor_tensor_scan(S_M[:, t0:t1], S_D0[:, t0:t1], S_A[:, t0:t1], mst[:, 0:1],
                                                                         ALU.add, ALU.max),
                     [('st', 4), ('st', 0), 'mst'], [('st', 5)])
                miv = S_MI[:, t0:t1].rearrange('p (c l) -> p c l', l=64)
                mv = S_M[:, t0:t1].rearrange('p (c l) -> p c l', l=64)
                bcv = S_BC[:, t0:t1].rearrange('p (c l) -> p c l', l=64)
                mset('dve', S_MI[:, t0:t1], 0.0, [('st', 11)])
                ts('dve', S_MI[:, t0:t0 + 64], S_MI[:, t0:t0 + 64], mst[:, 0:1], None, ALU.add, None, [('st', 11), 'mst'], [('st', 11)])
                for c in range(1, nch):
                    tt('dve', sc4[0:40, 0:1], bcv[:, c - 1, 63:64], mv[:, c - 1, 63:64], ALU.add, [('st', 3), ('st', 5)], ['sc4'])
                    ts('dve', miv[:, c, :], miv[:, c, :], sc4[0:40, 0:1], None, ALU.add, None, [('st', 11), 'sc4'], [('st', 11)])
                for c in range(nch):
                    ts('dve', S_WL[:, t0 + c * 64:t0 + (c + 1) * 64], S_A[:, t0 + c * 64:t0 + (c + 1) * 64], mv[:, c, 63:64], None,
                       ALU.subtract, None, [('st', 0), ('st', 5)], [('st', 10)])
                tt('dve', mst[:, 0:1], bcv[:, nch - 1, 63:64], mv[:, nch - 1, 63:64], ALU.add, [('st', 3), ('st', 5), ('st', 11)], ['mst'])
                if slot != 0:
                    dma('sp', osm_h.ap()[slot - 1:slot, :].rearrange('a h -> h a'), mst[0:8, :], ['mst'], [], final=True, nonc=True)
            act(S_WL, S_WL, AF.Exp, [('st', 10)], [('st', 10)])
            tt('dve', S_SI, S_MI, S_M, ALU.subtract, [('st', 11), ('st', 5)], [('st', 8)])
            act(S_SI, S_SI, AF.Exp, [('st', 8)], [('st', 8)])
            tt('dve', S_EM, S_BC, S_M, ALU.add, [('st', 3), ('st', 5)], [('st', 9)])
            act(S_EM, S_EM, AF.Exp, [('st', 9)], [('st', 9)], scale=-1.0)
            ts('dve', S_GL, S_A, ccols[0:40, 0:1], ccols[0:40, 1:2], ALU.mult, ALU.add, [('st', 0), 'ccols'], [('st', 6)])
            ts('dve', S_GRB, S_M, ccols[0:40, 2:3], ccols[0:40, 0:1], ALU.mult, ALU.add, [('st', 5), 'ccols'], [('st', 7)])
            for blk in range(NBLK):
                pst = ps[4][:, 0:120].rearrange('p (a b) -> p a b', a=3)
                for i, (src, k) in enumerate(((S_SI, 8), (S_EM, 9), (S_WL, 10))):
                    tr(pst[:, i, :], src[:, blk * 128:(blk + 1) * 128], identF[0:40, 0:40], [('st', k), 'identF'],
                       [psk(4)] if i == 0 else [])
                P.buf[psk(4)]['w'] = ('pe', P.cnt['pe'])
                cp('dve', COLS[:, blk, :], ps[4][:, 0:120], [psk(4)], ['COLS'])
            siv = S_SI.rearrange('p (c l) -> p c l', l=64)
            for h in range(8):
                ts('dve', sc4[0:40, 0:NCH], siv[:, :, 63], ccols[0:40, 11 + h:12 + h], None, ALU.mult, None,
                   [('st', 8), 'ccols'], ['sc4'])
                mm(ps[5][:, 0:NCH], [(ones40, sc4[0:40, 0:NCH])], ['ones40', 'sc4'], [psk(5)])
                cp('dve', DECB[:, h, 0:NCH], ps[5][:, 0:NCH], [psk(5)], ['DECB'])

            if STOP == 'stats':
                raise StopBuild()
            P.mark(mode + str(T) + ':Bmlstm')
            def seg_of(tok):
                for (slot, t0, t1) in segs:
                    if t0 <= tok < t1:
                        return slot, t0, t1
                raise AssertionError

            def load_slab(view, col0, ncols, wkeys):
                si, so_, skey = ring_slot()
                slab = A.view(so_, [KC, ncols], BF16)
                dma('sp', slab, view[:, :, col0:col0 + ncols], wkeys, [skey])
                return slab, skey

            def proj_fm(dst, slab, skey, c0, scale, dkey):
                pb = 2 + (ring_i[0] + c0 // 128) % 2
                mm(ps[pb][:, 0:T], [(slab[:, kc, c0:c0 + 128], h1T[:, kc, :]) for kc in range(KC)], [skey] + hkeys, [psk(pb)])
                act(dst, ps[pb][:, 0:T], AF.Copy, [psk(pb)], [dkey], scale=scale)

            def proj_tm(slab, skey, blk, pb):
                mm(ps[pb][:, 0:256], [(h1T[:, kc, blk * 128:(blk + 1) * 128], slab[:, kc, :]) for kc in range(KC)],
                   [skey] + hkeys, [psk(pb)])

            def proj_steps(h):
                hh = h % 2
                steps = []
                if full:
                    def s_q():
                        slab, skey = load_slab(wbin_v, Q0 + 128 * h, 128, k_win)
                        proj_fm(qT[hh], slab, skey, 0, 1.0, ('qT', hh))
                    steps.append(s_q)

                def s_k():
                    slab, skey = load_slab(wbin_v, K0 + 128 * h, 128, k_win)
                    proj_fm(kT[hh], slab, skey, 0, 128.0 ** -0.5, ('kT', hh))
                steps.append(s_k)

                def s_kt():
                    for blk in range(NBLK):
                        pbt = ps[blk % 2].bitcast(BF16)
                        tr(pbt[:, 0:128], kT[hh][:, blk * 128:(blk + 1) * 128], identB, [('kT', hh), 'identB'], [psk(blk % 2)])
                        ts('dve', kw[hh][:, blk, :], pbt[:, 0:128], COLS[:, blk, 80 + h:81 + h], None, ALU.mult, None,
                           [psk(blk % 2), 'COLS'], [('kw', hh)])
                steps.append(s_kt)
                vs = {}
                for blk in range(NBLK):
                    def s_v(blk=blk):
                        if blk == 0:
                            vs['v'] = load_slab(wbin_v, V0 + 256 * h, 256, k_win)
                            mset('dve', vext[hh][:, :, 256:257], 1.0, [('vext', hh)])
                        slab, skey = vs['v']
                        pb = 2 + blk % 2
                        proj_tm(slab, skey, blk, pb)
                        act(vext[hh][:, blk, 0:256], ps[pb][:, 0:256], AF.Copy, [psk(pb)], [('vext', hh)])
                    steps.append(s_v)
                if full:
                    for blk in range(NBLK):
                        def s_o(blk=blk):
                            if blk == 0:
                                vs['o'] = load_slab(wbin_v, O0 + 256 * h, 256, k_win)
                                dma('sp', gainb[:, hh, :], bcast_ap(hng_h, h * 256, 128, 256), (), [('gainb', hh)])
                            slab, skey = vs['o']
                            pb = 2 + blk % 2
                            proj_tm(slab, skey, blk, pb)
                            ot = OT[blk % 2]
                            ok_ = ('OT', blk % 2)
                            act(ot, ps[pb][:, 0:256], AF.Exp, [psk(pb)], [ok_], scale=-1.0)
                            ts('dve', ot, ot, 1.0, None, ALU.add, None, [ok_], [ok_])
                            recip(so[hh][:, blk, :], ot, [ok_], [('so', hh)])
                        steps.append(s_o)
                return steps

            def chain_steps(h):
                hh = h % 2
                ck = ('Cst', h)
                cbk = ('Cb', h)
                grh = S_GRH[hh]
                steps = []
                for c in range(NCH):
                    def step(c=c):
                        slot, t0, t1 = seg_of(c * 64)
                        blk, half = c // 2, c % 2
                        p0 = half * 64
                        cs = slice(c * 64, (c + 1) * 64)
                        if c == 0:
                            ts('dve', grh, S_GRB, ccols[0:40, 3 + h:4 + h], None, ALU.mult, None, [('st', 7), 'ccols'], [('grh', hh)])
                        if c * 64 == t0:
                            if slot != 0:
                                q_ = slot - 1
                                dma('sp', Cst[:, h, 0:256], isC_h.ap()[q_, h, :, :], (), [ck])
                                dma('sp', Cst[:, h, 256:257], isn_h.ap()[q_, h:h + 1, :].rearrange('a d -> d a'), (), [ck], nonc=True)
                            elif mode == 'main' and c == 0 and tile_first[0]:
                                ts('dve', Cst[:, h, :], Cst[:, h, :], flagc[:, 0:1], None, ALU.mult, None, [ck, 'flagc'], [ck])
                            cp('act', Cb[:, h, :], Cst[:, h, :], [ck], [cbk])
                        if full:
                            pS = ps[6]
                            mm(pS[p0:p0 + 64, 0:64], [(kT[hh][:, cs], qT[hh][:, cs])], [('kT', hh), ('qT', hh)], [psk(6)])
                            mm(ps[7][p0:p0 + 64, 320:384], [(S_GL[:, cs], grh[:, cs]), (identF[0:64, 0:64], NEGM)],
                               [('st', 6), ('grh', hh), 'identF', 'NEGM'], [psk(7)])
                            wt = WT[c % 2]; swt = SWT[c % 2]
                            act(wt[p0:p0 + 64, :], ps[7][p0:p0 + 64, 320:384], AF.Exp, [psk(7)], [('WT', c % 2)])
                            tt('dve', swt[p0:p0 + 64, :], pS[p0:p0 + 64, 0:64], wt[p0:p0 + 64, :], ALU.mult,
                               [psk(6), ('WT', c % 2)], [('SWT', c % 2)])
                            mm(ps[7][p0:p0 + 64, 0:257], [(swt[p0:p0 + 64, :], vext[hh][p0:p0 + 64, blk, :])],
                               [('SWT', c % 2), ('vext', hh)], [psk(7)])
                            mm(ps[5][p0:p0 + 64, 0:257], [(qT[hh][:, cs], Cb[:, h, :])], [('qT', hh), cbk], [psk(5)])
                            ht = HT[c % 2]; nd = ND[c % 2]; hn = HN[c % 2]; t1_ = T1[c % 2]
                            act(ht[p0:p0 + 64, :], ps[5][p0:p0 + 64, 0:257], AF.Identity, [psk(5), 'COLS'], [('HT', c % 2)],
                                scale=COLS[p0:p0 + 64, blk, h:h + 1])
                            tt('dve', nd[p0:p0 + 64, :], ht[p0:p0 + 64, :], ps[7][p0:p0 + 64, 0:257], ALU.add,
                               [('HT', c % 2), psk(7)], [('ND', c % 2)])
                            dcol = sc4[p0:p0 + 64, 4 + (c % 2):5 + (c % 2)]
                            act(dcol, nd[p0:p0 + 64, 256:257], AF.Abs, [('ND', c % 2)], [('dcol', c % 2)])
                            ts('dve', dcol, dcol, COLS[p0:p0 + 64, blk, 40 + h:41 + h], None, ALU.max, None,
                               [('dcol', c % 2), 'COLS'], [('dcol', c % 2)])
                            recip(dcol, dcol, [('dcol', c % 2)], [('dcol', c % 2)])
                            ts('dve', hn[p0:p0 + 64, :], nd[p0:p0 + 64, 0:256], dcol, None, ALU.mult, None,
                               [('ND', c % 2), ('dcol', c % 2)], [('HN', c % 2)])
                            rcol = sc4[p0:p0 + 64, 6 + (c % 2):7 + (c % 2)]
                            act(t1_[p0:p0 + 64, :], hn[p0:p0 + 64, :], AF.Square, [('HN', c % 2)], [('T1', c % 2)])
                            rsum(rcol, t1_[p0:p0 + 64, :], [('T1', c % 2)], [('rcol', c % 2)])
                            ts('dve', rcol, rcol, 1.0 / 256, EPS, ALU.mult, ALU.add, [('rcol', c % 2)], [('rcol', c % 2)])
                            act(rcol, rcol, AF.Ln, [('rcol', c % 2)], [('rcol', c % 2)])
                            act(rcol, rcol, AF.Exp, [('rcol', c % 2)], [('rcol', c % 2)], scale=-0.5)
                            stt('dve', t1_[p0:p0 + 64, :], hn[p0:p0 + 64, :], rcol, gainb[p0:p0 + 64, hh, :], ALU.mult, ALU.mult,
                                [('HN', c % 2), ('rcol', c % 2), ('gainb', hh)], [('T1', c % 2)])
                            tt('pool', yatok[p0:p0 + 64, blk, :], t1_[p0:p0 + 64, :], so[hh][p0:p0 + 64, blk, :], ALU.mult,
                               [('T1', c % 2), ('so', hh)], [('yatok', blk)])
                        mm(ps[4][:, 0:257], [(kw[hh][p0:p0 + 64, blk, :], vext[hh][p0:p0 + 64, blk, :])],
                           [('kw', hh), ('vext', hh)], [psk(4)])
                        stt('dve', Cst[:, h, :], Cst[:, h, :], DECB[:, h, c:c + 1], ps[4][:, 0:257], ALU.mult, ALU.add,
                            [ck, 'DECB', psk(4)], [ck])
                        last_of_seg = ((c + 1) * 64 == t1)
                        if not last_of_seg:
                            cp('act', Cb[:, h, :], Cst[:, h, :], [ck], [cbk])
                        elif slot != 0:
                            q_ = slot - 1
                            dma('sp', osC_h.ap()[q_, h, :, :], Cst[:, h, 0:256], [ck], [], final=True)
                            dma('sp', osn_h.ap()[q_, h:h + 1, :].rearrange('a d -> d a'), Cst[:, h, 256:257], [ck], [], final=True, nonc=True)
                        if full and half == 1:
                            for e2 in range(2):
                                pbt = ps[e2].bitcast(BF16)
                                tr(pbt[:, 0:128], yatok[:, blk, e2 * 128:(e2 + 1) * 128], identB, [('yatok', blk), 'identB'], [psk(e2)])
                                cp('act', yT[:, 2 * h + e2, blk * 128:(blk + 1) * 128], pbt[:, 0:128], [psk(e2)], [('yT', 2 * h + e2)])
                    steps.append(step)
                return steps

            for st_ in proj_steps(0):
                st_()
            for h in range(8):
                cst = chain_steps(h)
                pst = proj_steps(h + 1) if h < 7 else []
                i_p = 0
                for i in range(len(cst)):
                    cst[i]()
                    tgt = (len(pst) * (i + 1)) // len(cst)
                    while i_p < tgt:
                        pst[i_p]()
                        i_p += 1
                while i_p < len(pst):
                    pst[i_p]()
                    i_p += 1

            if STOP == 'mlstm':
                raise StopBuild()
            P.mark(mode + str(T) + ':Blru')
            lo_ = [R_X]

            def lal(shape, dt, parts=128):
                sz = 2 if dt == BF16 else 4
                n = 1
                for s in shape:
                    n *= s
                off = lo_[0]
                lo_[0] += (n * sz + 31) // 32 * 32
                assert lo_[0] <= R_X + 65536
                return A.view(off, shape, dt, parts=parts)
            WRI = lal([2, 16, 128], BF16)
            XBH = [lal([3 + T], F32) for _ in range(2)]
            XC = [lal([T], F32) for _ in range(2)]
            XCb = [lal([T], BF16) for _ in range(2)]
            RG = [lal([T], F32) for _ in range(2)]
            IGt = [lal([T], F32) for _ in range(2)]
            AAt = [lal([T], F32) for _ in range(2)]
            A2t = [lal([T], F32) for _ in range(2)]
            HH = [lal([T], F32) for _ in range(2)]
            GX = [lal([T], F32) for _ in range(2)]
            GW = [lal([T], F32) for _ in range(2)]
            P.alias(KEYS_L, KEYS_B)
            dma('sp', WRI[:, 0, :, :], wbr_h.ap().rearrange('n i j -> i n j'), ['wbr'], ['WRI'])
            dma('sp', WRI[:, 1, :, :], wbi_h.ap().rearrange('n i j -> i n j'), ['wbr'], ['WRI'])
            has_sample = any(slot != 0 for (slot, _a, _b) in segs)
            if has_sample:
                for (slot, t0, t1) in segs:
                    q_ = slot - 1
                    for j in range(3):
                        dma('sp', SCV[:, q_, :, j], iscv_h.ap()[q_, j, :].rearrange('(b p) -> p b', p=128), (), ['SCV'], nonc=True)
                    dma('sp', SH[:, q_, :], ish_h.ap()[q_, :].rearrange('(b p) -> p b', p=128), (), ['SH'], nonc=True)
            lslab = {}

            def lru_proj(nb):
                sp_, b2 = nb // 2, nb % 2
                u = nb % 2
                K = lambda n_: (n_, u)
                if b2 == 0:
                    lslab['x'] = load_slab(wbin_v, XB0 + 256 * sp_, 256, k_win)
                    if full:
                        lslab['g'] = load_slab(wbin_v, GB0 + 256 * sp_, 256, k_win)
                slabx, skx = lslab['x']
                pb = 2 + u
                mm(ps[pb][:, 0:T], [(slabx[:, kc, b2 * 128:(b2 + 1) * 128], h1T[:, kc, :]) for kc in range(KC)],
                   [skx] + hkeys, [psk(pb)])
                act(XBH[u][:, 3:3 + T], ps[pb][:, 0:T], AF.Copy, [psk(pb)], [K('XBH')])
                for (slot, t0, t1) in segs:
                    n = t1 - t0
                    if slot != 0:
                        q_ = slot - 1
                        cp('dve', convbuf[:, nb, :], SCV[:, q_, nb, :], ['SCV'], [('convbuf', nb)])
                    elif mode == 'main' and tile_first[0]:
                        ts('dve', convbuf[:, nb, :], convbuf[:, nb, :], flagc[:, 0:1], None, ALU.mult, None, [('convbuf', nb), 'flagc'], [('convbuf', nb)])
                        ts('dve', hst[:, nb:nb + 1], hst[:, nb:nb + 1], flagc[:, 0:1], None, ALU.mult, None, [('hst', nb), 'flagc'], [('hst', nb)])
                    if t0 == 0:
                        cp('dve', XBH[u][:, 0:3], convbuf[:, nb, :], [('convbuf', nb)], [K('XBH')])
                        xp = XBH[u]
                        xk = K('XBH')
                    else:
                        xp = GW[u]
                        xk = K('GW')
                        cp('dve', xp[:, 0:3], convbuf[:, nb, :], [('convbuf', nb)], [K('GW')])
                        cp('dve', xp[:, 3:3 + n], XBH[u][:, 3 + t0:3 + t1], [K('XBH')], [K('GW')])
                    xcs = XC[u][:, t0:t1]
                    ts('dve', xcs, xp[:, 0:n], lruc[:, nb, 2:3], lruc[:, nb, 6:7], ALU.mult, ALU.add, [xk, 'lruc'], [K('XC')])
                    for j in range(1, 4):
                        stt('dve', xcs, xp[:, j:j + n], lruc[:, nb, 2 + j:3 + j], xcs, ALU.mult, ALU.add,
                            [xk, 'lruc', K('XC')], [K('XC')])
                    if slot != 0:
                        cp('dve', SCVO[:, slot - 1, nb, :], XBH[u][:, t1:t1 + 3], [K('XBH')], ['SCVO'])
                    else:
                        cp('dve', convbuf[:, nb, :], XBH[u][:, t1:t1 + 3], [K('XBH')], [('convbuf', nb)])
                cp('act', XCb[u], XC[u], [K('XC')], [K('XCb')])
                if full:
                    slabg, skg = lslab['g']
                    pq = 6 + u
                    mm(ps[pq][:, 0:T], [(slabg[:, kc, b2 * 128:(b2 + 1) * 128], h1T[:, kc, :]) for kc in range(KC)],
                       [skg] + hkeys, [psk(pq)])
                    act(GX[u], ps[pq][:, 0:T], AF.Copy, [psk(pq)], [K('GX')])
                    act(GW[u], ps[pq][:, 0:T], AF.Square, [psk(pq), K('GW')], [K('GW')])
                    ts('dve', GW[u], GW[u], 0.044715, 1.0, ALU.mult, ALU.add, [K('GW')], [K('GW')])
                    tt('dve', GW[u], GW[u], GX[u], ALU.mult, [K('GW'), K('GX')], [K('GW')])
                    act(GW[u], GW[u], AF.Exp, [K('GW')], [K('GW')], scale=-1.5957691216057308)
                    ts('dve', GW[u], GW[u], 1.0, None, ALU.add, None, [K('GW')], [K('GW')])
                    recip(GW[u], GW[u], [K('GW')], [K('GW')])
                    tt('dve', GW[u], GW[u], GX[u], ALU.mult, [K('GW'), K('GX')], [K('GW')])

            def lru_rest(nb):
                u = nb % 2
                K = lambda n_: (n_, u)
                for gi, (dst, bcol) in enumerate(((RG[u], 7), (IGt[u], 8))):
                    pg = 4 + gi
                    mm(ps[pg][:, 0:T], [(WRI[:, gi, nb, :], XCb[u])], ['WRI', K('XCb')], [psk(pg)])
                    ts('dve', dst, ps[pg][:, 0:T], lruc[:, nb, bcol:bcol + 1], None, ALU.add, None, [psk(pg), 'lruc'], [K('g%d' % gi)])
                    act(dst, dst, AF.Exp, [K('g%d' % gi)], [K('g%d' % gi)], scale=-1.0)
                    ts('dve', dst, dst, 1.0, None, ALU.add, None, [K('g%d' % gi)], [K('g%d' % gi)])
                    recip(dst, dst, [K('g%d' % gi)], [K('g%d' % gi)])
                act(AAt[u], RG[u], AF.Exp, [K('g0'), 'lruc'], [K('AA')], scale=lruc[:, nb, 0:1])
                act(A2t[u], RG[u], AF.Exp, [K('g0'), 'lruc'], [K('A2')], scale=lruc[:, nb, 1:2])
                ts('dve', A2t[u], A2t[u], -1.0, 1.0, ALU.mult, ALU.add, [K('A2')], [K('A2')])
                ts('dve', A2t[u], A2t[u], 1e-18, None, ALU.max, None, [K('A2')], [K('A2')])
                act(A2t[u], A2t[u], AF.Ln, [K('A2')], [K('A2')])
                act(A2t[u], A2t[u], AF.Exp, [K('A2')], [K('A2')], scale=0.5)
                tt('dve', IGt[u], IGt[u], XC[u], ALU.mult, [K('g1'), K('XC')], [K('g1')])
                tt('dve', A2t[u], A2t[u], IGt[u], ALU.mult, [K('A2'), K('g1')], [K('A2')])
                for (slot, t0, t1) in segs:
                    if slot != 0:
                        cp('dve', hst[:, nb:nb + 1], SH[:, slot - 1, nb:nb + 1], ['SH'], [('hst', nb)])
                    P.op('dve', lambda e, u=u, nb=nb, t0=t0, t1=t1: e.tensor_tensor_scan(
                        HH[u][:, t0:t1], AAt[u][:, t0:t1], A2t[u][:, t0:t1], hst[:, nb:nb + 1], ALU.mult, ALU.add),
                        [K('AA'), K('A2'), ('hst', nb)], [K('HH')])
                    if slot != 0:
                        cp('dve', SHO[:, slot - 1, nb:nb + 1], HH[u][:, t1 - 1:t1], [K('HH')], ['SHO'])
                    else:
                        cp('dve', hst[:, nb:nb + 1], HH[u][:, t1 - 1:t1], [K('HH')], [('hst', nb)])
                if full:
                    tt('dve', yT[:, 16 + nb, :], GW[u], HH[u], ALU.mult, [K('GW'), K('HH')], [('yT', 16 + nb)])

            lru_proj(0)
            for nb in range(1, 16):
                lru_proj(nb)
                lru_rest(nb - 1)
            lru_rest(15)
            if has_sample:
                for (slot, t0, t1) in segs:
                    q_ = slot - 1
                    for j in range(3):
                        dma('sp', oscv_h.ap()[q_, j, :].rearrange('(b p) -> p b', p=128), SCVO[:, q_, :, j], ['SCVO'], [], final=True, nonc=True)
                    dma('sp', osh_h.ap()[q_, :].rearrange('(b p) -> p b', p=128), SHO[:, q_, :], ['SHO'], [], final=True, nonc=True)
            if not full:
                return

            if STOP == 'lru':
                raise StopBuild()
            P.mark(mode + str(T) + ':C')
            x1 = A.view(R_X, [NBLK, D], F32)
            P.alias(KEYS_X1, KEYS_L + KEYS_B + KEYS_A)
            for blk in range(NBLK):
                dma('sp', x1[:, blk, :], x_src[blk * 128:(blk + 1) * 128, :], (), [('x1', blk)])
            G1 = [A.view(TMP + 8192 + i * 1024, [256], F32) for i in range(2)]
            CT = [A.view(TMP + 8192 + 2048 + i * 1024, [256], F32) for i in range(2)]
            for cg in range(16):
                slab, skey = load_slab(wbout_v, cg * 256, 256, k_wout)
                g1 = G1[cg % 2]
                for (slot, t0, t1) in segs:
                    pa, pb_ = (0, 128) if T == 512 else (t0, t1)
                    dma('sp', g1[pa:pb_, :], bcast_ap(ada_h, slot * 6 * D + 2 * D + cg * 256, pb_ - pa, 256), ['ada_dram'], [('G1', cg % 2)])
                for blk in range(NBLK):
                    pb = 2 + (cg * NBLK + blk) % 4
                    mm(ps[pb][:, 0:256], [(yT[:, kc, blk * 128:(blk + 1) * 128], slab[:, kc, :]) for kc in range(KC)],
                       [skey] + ykeys, [psk(pb)])
                    ct = CT[blk % 2]
                    tt('dve', ct, ps[pb][:, 0:256], g1, ALU.mult, [psk(pb), ('G1', cg % 2)], [('CT', blk % 2)])
                    tt('pool', x1[:, blk, cg * 256:(cg + 1) * 256], x1[:, blk, cg * 256:(cg + 1) * 256], ct, ALU.add,
                       [('CT', blk % 2), ('x1', blk)], [('x1', blk)])
            h2T = A.view(R_H, [KC, T], BF16)

            def srcC(blk):
                return x1[:, blk, :], ('x1', blk)
            P.alias([('cxn', 0), ('cxn', 1), 'csqj'], ykeys)
            norm_to_T(srcC, h2T, lambda s, j: MODC[:, s, 2, j:j + 1], lambda s, j: MODC[:, s, 3, j:j + 1], R_Y, R_Y + 16384, 'c')

            if STOP == 'C':
                raise StopBuild()
            P.mark(mode + str(T) + ':D')
            aT = [A.view(R_Y + 16384 + i * 8192, [8, T], BF16) for i in range(2)]
            G2B = A.view(R_Y, [D], F32)
            P.alias(['G2B', ('aT', 0), ('aT', 1)], [('cxn', 0), ('cxn', 1), 'csqj'] + ykeys)
            for (slot, t0, t1) in segs:
                pa, pb_ = (0, 128) if T == 512 else (t0, t1)
                dma('sp', G2B[pa:pb_, :], bcast_ap(ada_h, slot * 6 * D + 5 * D, pb_ - pa, D), ['ada_dram'], ['G2B'])
            EX = [A.view(TMP + i * 2048, [512], F32) for i in range(2)]
            DT_ = [A.view(TMP + 4096 + i * 2048, [512], F32) for i in range(2)]
            NG = (DFF + 1023) // 1024
            for g in range(NG):
                nchk = min(8, (DFF - g * 1024) // 128)
                at = aT[g % 2]
                ak = ('aT', g % 2)
                for ci in range(nchk):
                    f0 = g * 1024 + ci * 128
                    si, so_, skey = ring_slot()
                    slab = A.view(so_, [2, KC, 128], BF16)
                    dma('sp', slab[:, 0, :, :], wbgu_v[:, :, f0:f0 + 128], k_wgu, [skey])
                    dma('sp', slab[:, 1, :, :], wbgu_v[:, :, DFF + f0:DFF + f0 + 128], k_wgu, [skey])
                    u = ci % 2
                    pg, pu = 0 + 2 * u, 1 + 2 * u
                    mm(ps[pg][:, 0:T], [(slab[:, 0, kc, :], h2T[:, kc, :]) for kc in range(KC)], [skey] + hkeys, [psk(pg)])
                    mm(ps[pu][:, 0:T], [(slab[:, 1, kc, :], h2T[:, kc, :]) for kc in range(KC)], [skey] + hkeys, [psk(pu)])
                    ex = EX[u][:, 0:T]
                    act(ex, ps[pg][:, 0:T], AF.Exp, [psk(pg)], [('EX', u)], scale=-1.0)
                    ts('dve', ex, ex, 1.0, None, ALU.add, None, [('EX', u)], [('EX', u)])
                    recip(ex, ex, [('EX', u)], [('EX', u)])
                    tt('dve', ex, ex, ps[pg][:, 0:T], ALU.mult, [('EX', u), psk(pg)], [('EX', u)])
                    tt('dve', at[:, ci, :], ex, ps[pu][:, 0:T], ALU.mult, [('EX', u), psk(pu)], [ak])
                for c8 in range(8):
                    si, so_, skey = ring_slot()
                    slab = A.view(so_, [nchk, 512], BF16)
                    dma('sp', slab, wbdn_v[:, g * 8:g * 8 + nchk, c8 * 512:(c8 + 1) * 512], k_wdn, [skey])
                    for blk in range(NBLK):
                        pd = 4 + (c8 * NBLK + blk) % 4
                        mm(ps[pd][:, 0:512], [(at[:, ci, blk * 128:(blk + 1) * 128], slab[:, ci, :]) for ci in range(nchk)],
                           [skey, ak], [psk(pd)])
                        dt_ = DT_[blk % 2]
                        tt('dve', dt_, ps[pd][:, 0:512], G2B[:, c8 * 512:(c8 + 1) * 512], ALU.mult, [psk(pd), 'G2B'], [('DT', blk % 2)])
                        tt('pool', x1[:, blk, c8 * 512:(c8 + 1) * 512], x1[:, blk, c8 * 512:(c8 + 1) * 512], dt_, ALU.add,
                           [('DT', blk % 2), ('x1', blk)], [('x1', blk)])

            if STOP == 'D':
                raise StopBuild()
            P.mark(mode + str(T) + ':E')
            NFG = A.view(R_H, [D], F32)
            P.alias(['NFG', 'fjunk'], hkeys)
            dma('sp', NFG, bcast_ap(nfg_h, 0, 128, D), (), ['NFG'])
            junk = A.view(R_H + 16384, [D], F32)
            for blk in range(NBLK):
                ssq = smallc[:, 8 + blk:9 + blk]
                sk = ('fssq', blk)
                act(junk, x1[:, blk, :], AF.Square, [('x1', blk)], ['fjunk'])
                rsum(ssq, junk, ['fjunk'], [sk])
                ts('dve', ssq, ssq, 1.0 / D, EPS, ALU.mult, ALU.add, [sk], [sk])
                act(ssq, ssq, AF.Ln, [sk], [sk])
                act(ssq, ssq, AF.Exp, [sk], [sk], scale=-0.5)
                stt('dve', x1[:, blk, :], x1[:, blk, :], ssq, NFG, ALU.mult, ALU.mult, [('x1', blk), sk, 'NFG'], [('x1', blk)])
                dma('sp', y_dst[blk * 128:(blk + 1) * 128, :], x1[:, blk, :], [('x1', blk)], [], final=True)

        for i in range(NPRE):
            run_tile(xpre_h.ap()[i * 512:(i + 1) * 512, :], 512, [(0, 0, 512)], 'prefix')
        for i in range(NMAIN):
            tile_first[0] = (i == 0)
            run_tile(xmain_h.ap()[i * 512:(i + 1) * 512, :], 512, [(0, 0, 512)], 'main', yp_h.ap()[i * 512:(i + 1) * 512, :])
        tile_first[0] = False
        for h in range(8):
            dma('sp', opC_h.ap()[h, :, :], Cst[:, h, 0:256], [('Cst', h)], [], final=True)
            dma('sp', opn_h.ap()[h:h + 1, :].rearrange('a d -> d a'), Cst[:, h, 256:257], [('Cst', h)], [], final=True, nonc=True)
        dma('sp', opm_h.ap().rearrange('a h -> h a'), mst[0:8, :], ['mst'], [], final=True, nonc=True)
        dma('sp', oph_h.ap().rearrange('(b p) -> p b', p=128), hst, [('hst', nb) for nb in range(16)], [], final=True, nonc=True)
        for nb in range(16):
            dma('sp', opcv_h.ap()[:, nb * 128:(nb + 1) * 128].rearrange('j p -> p j'), convbuf[:, nb, :], [('convbuf', nb)], [], final=True, nonc=True)
        if DO_SAMPLE:
            run_tile(xs_h.ap(), 128, [(1, 0, 64), (2, 64, 128)], 'main', ys_h.ap())


    try:
        body()
    except StopBuild:
        pass
    P.build()
    LAST_MARKS[:] = P.marks
    return nc


def make_consts():
    c = np.zeros((128, 1024), np.float32)
    c[:, 0:128] = np.eye(128, dtype=np.float32)
    s = np.arange(64)[:, None]; t = np.arange(64)[None, :]
    c[0:64, 128:192] = np.where(s <= t, 0.0, NEGBIG)
    m = np.ones((512,), np.float32); m[::64] = 0.0
    c[0:40, 192:704] = m[None, :]
    cc = np.zeros((128, 32), np.float32)
    cc[0:8, 0] = 1.0
    cc[32:40, 1] = 1.0
    cc[32:40, 2] = -1.0
    for h in range(8):
        cc[h, 3 + h] = 1.0; cc[32 + h, 3 + h] = 1.0
        cc[h, 11 + h] = 1.0
    c[:, 704:736] = cc
    return c


_NC_CACHE = {}


def _get_nc(cfg_key, cfg):
    if cfg_key not in _NC_CACHE:
        _NC_CACHE[cfg_key] = build_program(cfg)
    return _NC_CACHE[cfg_key]


def make_in_maps(inp, cores=range(8)):
    consts = make_consts()
    maps = []
    f = lambda a: np.ascontiguousarray(a, dtype=np.float32)
    for c in cores:
        b, half = c // 2, c % 2
        sq = [2 * c, 2 * c + 1]
        m = {
            'xpre': f(inp['x_prompt'][b, 0:2048]),
            'xmain': f(inp['x_prompt'][b, half * 2048:(half + 1) * 2048]),
            'xs': f(inp['x_sample'][sq].reshape(128, D)),
            'c3': f(np.concatenate([inp['c_prompt'][b:b + 1], inp['c_sample'][sq]], 0)),
            'flag': np.full((128, 1), float(half), np.float32),
            'isC': f(inp['state_mlstm_C'][0, sq]), 'isn': f(inp['state_mlstm_n'][0, sq]),
            'ism': f(inp['state_mlstm_m'][0, sq]), 'ish': f(inp['state_lru_h'][0, sq]),
            'iscv': f(inp['state_conv'][0, sq]),
            'w_ada': f(inp['w_ada'][0]), 'b_ada': f(inp['b_ada'][0]),
            'norm1_g': f(inp['norm1_g'][0]), 'norm2_g': f(inp['norm2_g'][0]),
            'w_in': f(inp['w_in'][0]), 'b_gates_a': f(inp['b_gates_a'][0]), 'head_norm_g': f(inp['head_norm_g'][0]),
            'conv_w': f(inp['conv_w'][0]), 'conv_b': f(inp['conv_b'][0]),
            'w_r': f(inp['w_r'][0]), 'b_r': f(inp['b_r'][0]), 'w_i': f(inp['w_i'][0]), 'b_i': f(inp['b_i'][0]),
            'lru_lambda': f(inp['lru_lambda'][0]),
            'w_out': f(inp['w_out'][0]), 'w_gu': f(inp['w_gu'][0]), 'w_down': f(inp['w_down'][0]),
            'normf_g': f(inp['normf_g']), 'consts': consts,
        }
        maps.append(m)
    return maps


def kernel(**inp):
    inp = {k: np.asarray(v) for k, v in inp.items()}
    nc = _get_nc('full', {})
    maps = make_in_maps(inp)
    res = run_bass_kernel_spmd(nc, maps, core_ids=list(range(8)))
    R = res.results
    y_prompt = np.zeros((4, 4096, D), np.float32)
    y_sample = np.zeros((16, 64, D), np.float32)
    p_C = np.zeros((1, 4, 8, 128, 256), np.float32); p_n = np.zeros((1, 4, 8, 128), np.float32)
    p_m = np.zeros((1, 4, 8), np.float32); p_h = np.zeros((1, 4, 2048), np.float32); p_conv = np.zeros((1, 4, 3, 2048), np.float32)
    s_C = np.zeros((1, 16, 8, 128, 256), np.float32); s_n = np.zeros((1, 16, 8, 128), np.float32)
    s_m = np.zeros((1, 16, 8), np.float32); s_h = np.zeros((1, 16, 2048), np.float32); s_conv = np.zeros((1, 16, 3, 2048), np.float32)
    for c in range(8):
        b, half = c // 2, c % 2
        r = R[c]
        y_prompt[b, half * 2048:(half + 1) * 2048] = r['yp']
        y_sample[2 * c:2 * c + 2] = r['ys'].reshape(2, 64, D)
        if half == 1:
            p_C[0, b] = r['opC']; p_n[0, b] = r['opn']; p_m[0, b] = r['opm'].reshape(8)
            p_h[0, b] = r['oph']; p_conv[0, b] = r['opcv']
        s_C[0, 2 * c:2 * c + 2] = r['osC']; s_n[0, 2 * c:2 * c + 2] = r['osn']; s_m[0, 2 * c:2 * c + 2] = r['osm']
        s_h[0, 2 * c:2 * c + 2] = r['osh']; s_conv[0, 2 * c:2 * c + 2] = r['oscv']
    return (y_prompt, y_sample, p_C, p_n, p_m, p_h, p_conv, s_C, s_n, s_m, s_h, s_conv)
```
